# Optimizing a Trainium2 kernel written in Bass

```python
import math
import jax, jax.numpy as jnp
from jax import lax
import numpy as np

D_MODEL = 1024
BATCH = 8
SEQ = 4096
DEPTH = 2

CTX_LEN = 256
GRID_W = 64
Q_BLOCK = 128
ROPE_THETA = 10000.0
EPS = 1e-6

A_HEAD_DIM = 128
A_HEADS = D_MODEL // (2 * A_HEAD_DIM)
A_KV_HEADS = A_HEADS // 2
A_WIDTH = A_HEADS * A_HEAD_DIM
A_KV_WIDTH = A_KV_HEADS * A_HEAD_DIM
B_HEAD_DIM = 64
B_HEADS = D_MODEL // (4 * B_HEAD_DIM)
B_QK_WIDTH = B_HEADS * 2 * B_HEAD_DIM
B_WIDTH = B_HEADS * 2 * B_HEAD_DIM
ATTN_WIDTH = A_WIDTH + B_WIDTH
KV_COLS = 2 * A_KV_WIDTH + B_QK_WIDTH + B_WIDTH
ATTN_IN_COLS = KV_COLS + A_WIDTH + B_QK_WIDTH + ATTN_WIDTH
F_GROUPS = 4
F_WIDTH = D_MODEL
F_GROUP_DIM = F_WIDTH // F_GROUPS

N_ATTN_LAYERS = (DEPTH + 1) // 2
N_FOURIER_LAYERS = DEPTH // 2

kernel_name = "hybrid_gqa_diffattn_fourier_dit"


def rms_norm(x, g):
    xf = x.astype(jnp.float32)
    y = xf * lax.rsqrt(jnp.mean(xf * xf, axis=-1, keepdims=True) + EPS)
    return (y * g.astype(jnp.float32)).astype(x.dtype)


def modulate(x, g, shift, scale):
    return rms_norm(x, g) * (1 + scale) + shift


def ada_params(cond, w, b):
    m = jax.nn.silu(cond) @ w + b
    return jnp.split(m, 3, axis=-1)


def axial_rope_tables(n_tokens, dim, dtype):
    rows_count = n_tokens // GRID_W
    rows = jnp.repeat(jnp.arange(rows_count, dtype=jnp.float32), GRID_W)
    cols = jnp.tile(jnp.arange(GRID_W, dtype=jnp.float32), rows_count)
    quarter = dim // 4
    inv = ROPE_THETA ** (-jnp.arange(quarter, dtype=jnp.float32) / quarter)
    ang = jnp.stack([rows[:, None] * inv, cols[:, None] * inv], axis=1)
    return jnp.cos(ang).astype(dtype), jnp.sin(ang).astype(dtype)


def apply_axial_rope(x, cos, sin):
    b, n, h, dim = x.shape
    q4 = dim // 4
    xs = x.reshape(b, n, h, 2, 2, q4)
    x1, x2 = xs[..., 0, :], xs[..., 1, :]
    c = cos[None, :, None]
    s = sin[None, :, None]
    out = jnp.stack([x1 * c - x2 * s, x1 * s + x2 * c], axis=-2)
    return out.reshape(x.shape)


def sweep_query_blocks(fn, *qs):
    b, n = qs[0].shape[:2]
    nb = n // Q_BLOCK
    blocks = tuple(jnp.moveaxis(q.reshape(b, nb, Q_BLOCK, *q.shape[2:]), 1, 0) for q in qs)
    out = lax.map(lambda blk: fn(*blk), blocks)
    return jnp.moveaxis(out, 0, 1).reshape(b, n, *out.shape[3:])


def gqa_block(q, k, v):
    b, nq = q.shape[:2]
    qg = q.reshape(b, nq, A_KV_HEADS, A_HEADS // A_KV_HEADS, A_HEAD_DIM)
    s = jnp.einsum('bqkgd,blkd->bkgql', qg, k).astype(jnp.float32) * (A_HEAD_DIM ** -0.5)
    p = jax.nn.softmax(s, axis=-1).astype(v.dtype)
    o = jnp.einsum('bkgql,blkd->bqkgd', p, v)
    return o.reshape(b, nq, A_WIDTH)


def diff_block(q1, q2, k1, k2, v, lam, subln_g, lam_init):
    b, nq = q1.shape[:2]
    scale = B_HEAD_DIM ** -0.5
    p1 = jax.nn.softmax(jnp.einsum('bqhd,blhd->bhql', q1, k1).astype(jnp.float32) * scale, axis=-1)
    p2 = jax.nn.softmax(jnp.einsum('bqhd,blhd->bhql', q2, k2).astype(jnp.float32) * scale, axis=-1)
    p = (p1 - lam * p2).astype(v.dtype)
    o = jnp.einsum('bhql,blhe->bqhe', p, v)
    o = rms_norm(o, subln_g) * (1.0 - lam_init)
    return o.reshape(b, nq, B_WIDTH)


def split_kv(p, kn_g):
    b, n = p.shape[:2]
    kA, vA, kB, vB = jnp.split(p, [A_KV_WIDTH, 2 * A_KV_WIDTH, 2 * A_KV_WIDTH + B_QK_WIDTH], axis=-1)
    kA = rms_norm(kA.reshape(b, n, A_KV_HEADS, A_HEAD_DIM), kn_g)
    vA = vA.reshape(b, n, A_KV_HEADS, A_HEAD_DIM)
    kB = kB.reshape(b, n, B_HEADS, 2, B_HEAD_DIM)
    vB = vB.reshape(b, n, B_HEADS, 2 * B_HEAD_DIM)
    return kA, vA, kB[..., 0, :], kB[..., 1, :], vB


def split_q(p, qn_g):
    b, n = p.shape[:2]
    qA, qB, gate = jnp.split(p, [A_WIDTH, A_WIDTH + B_QK_WIDTH], axis=-1)
    qA = rms_norm(qA.reshape(b, n, A_HEADS, A_HEAD_DIM), qn_g)
    qB = qB.reshape(b, n, B_HEADS, 2, B_HEAD_DIM)
    return qA, qB[..., 0, :], qB[..., 1, :], gate


def attention_layer(x, ctx, mods_x, mods_c, norm_g, w_in, qn_g, kn_g,
                    lam_q1, lam_k1, lam_q2, lam_k2, subln_g, w_out, lam_init, update_ctx):
    shift_x, scale_x, gate_x = mods_x
    shift_c, scale_c, gate_c = mods_c
    hx = modulate(x, norm_g, shift_x, scale_x)
    hc = modulate(ctx, norm_g, shift_c, scale_c)
    lam = (jnp.exp(jnp.sum(lam_q1.astype(jnp.float32) * lam_k1.astype(jnp.float32)))
           - jnp.exp(jnp.sum(lam_q2.astype(jnp.float32) * lam_k2.astype(jnp.float32)))
           + lam_init)

    pc = hc @ (w_in if update_ctx else w_in[:, :KV_COLS])
    kA_c, vA_c, k1_c, k2_c, vB_c = split_kv(pc[..., :KV_COLS], kn_g)

    px = hx @ w_in
    kA_x, vA_x, k1_x, k2_x, vB_x = split_kv(px[..., :KV_COLS], kn_g)
    qA_x, q1_x, q2_x, g_x = split_q(px[..., KV_COLS:], qn_g)
    n = x.shape[1]
    cos_a, sin_a = axial_rope_tables(n, A_HEAD_DIM, x.dtype)
    cos_b, sin_b = axial_rope_tables(n, B_HEAD_DIM, x.dtype)
    qA_x, kA_x = apply_axial_rope(qA_x, cos_a, sin_a), apply_axial_rope(kA_x, cos_a, sin_a)
    q1_x, k1_x = apply_axial_rope(q1_x, cos_b, sin_b), apply_axial_rope(k1_x, cos_b, sin_b)
    q2_x, k2_x = apply_axial_rope(q2_x, cos_b, sin_b), apply_axial_rope(k2_x, cos_b, sin_b)

    def mix(qA, q1, q2, kA, vA, k1, k2, vB):
        def blk(a, b1, b2):
            return jnp.concatenate([gqa_block(a, kA, vA),
                                    diff_block(b1, b2, k1, k2, vB, lam, subln_g, lam_init)], axis=-1)
        return sweep_query_blocks(blk, qA, q1, q2)

    cat = lambda a, b: jnp.concatenate([a, b], axis=1)
    ox = mix(qA_x, q1_x, q2_x, cat(kA_c, kA_x), cat(vA_c, vA_x),
             cat(k1_c, k1_x), cat(k2_c, k2_x), cat(vB_c, vB_x))
    x = x + gate_x * ((ox * jax.nn.silu(g_x)) @ w_out)

    if update_ctx:
        qA_c, q1_c, q2_c, g_c = split_q(pc[..., KV_COLS:], qn_g)
        oc = mix(qA_c, q1_c, q2_c, kA_c, vA_c, k1_c, k2_c, vB_c)
        ctx = ctx + gate_c * ((oc * jax.nn.silu(g_c)) @ w_out)
    return x, ctx


def fourier_layer(x, shift, scale, gate, norm_g, w_in, w_out):
    h = modulate(x, norm_g, shift, scale)
    u, g = jnp.split(h @ w_in, 2, axis=-1)
    b, n = u.shape[:2]
    uf = u.reshape(b, n, F_GROUPS, F_GROUP_DIM).astype(jnp.float32)
    f = jnp.fft.fft2(uf, axes=(1, 3), norm="ortho").real.astype(x.dtype).reshape(b, n, F_WIDTH)
    return x + gate * ((f * jax.nn.silu(g)) @ w_out)


def setup_inputs(seed: int = 0) -> dict:
    key = jax.random.key(seed)
    ks = jax.random.split(key, 24)
    nrm = lambda k, shape: jax.random.normal(k, shape, dtype=jnp.float32)
    D = D_MODEL
    return {
        "x": nrm(ks[0], (BATCH, SEQ, D)),
        "c": nrm(ks[1], (BATCH, D)),
        "ctx": nrm(ks[2], (BATCH, CTX_LEN, D)),
        "c_ctx": nrm(ks[3], (D,)),
        "ada_w": nrm(ks[4], (DEPTH, D, 3 * D)) * (0.5 * D ** -0.5),
        "ada_b": nrm(ks[5], (DEPTH, 3 * D)) * 0.01,
        "norm_g": 1.0 + 0.02 * nrm(ks[6], (DEPTH, D)),
        "attn_in_w": nrm(ks[7], (N_ATTN_LAYERS, D, ATTN_IN_COLS)) * D ** -0.5,
        "attn_qn_g": 1.0 + 0.02 * nrm(ks[8], (N_ATTN_LAYERS, A_HEAD_DIM)),
        "attn_kn_g": 1.0 + 0.02 * nrm(ks[9], (N_ATTN_LAYERS, A_HEAD_DIM)),
        "lam_q1": 0.1 * nrm(ks[10], (N_ATTN_LAYERS, B_HEAD_DIM)),
        "lam_k1": 0.1 * nrm(ks[11], (N_ATTN_LAYERS, B_HEAD_DIM)),
        "lam_q2": 0.1 * nrm(ks[12], (N_ATTN_LAYERS, B_HEAD_DIM)),
        "lam_k2": 0.1 * nrm(ks[13], (N_ATTN_LAYERS, B_HEAD_DIM)),
        "attn_subln_g": 1.0 + 0.02 * nrm(ks[14], (N_ATTN_LAYERS, 2 * B_HEAD_DIM)),
        "attn_out_w": nrm(ks[15], (N_ATTN_LAYERS, ATTN_WIDTH, D)) * ATTN_WIDTH ** -0.5,
        "fourier_in_w": nrm(ks[16], (N_FOURIER_LAYERS, D, 2 * F_WIDTH)) * D ** -0.5,
        "fourier_out_w": nrm(ks[17], (N_FOURIER_LAYERS, F_WIDTH, D)) * F_WIDTH ** -0.5,
        "final_g": 1.0 + 0.02 * nrm(ks[18], (D,)),
    }


def reference(x, c, ctx, c_ctx, ada_w, ada_b, norm_g, attn_in_w, attn_qn_g, attn_kn_g,
              lam_q1, lam_k1, lam_q2, lam_k2, attn_subln_g, attn_out_w,
              fourier_in_w, fourier_out_w, final_g):
    for l in range(DEPTH):
        i = l // 2
        update_ctx = any(j % 2 == 0 for j in range(l + 1, DEPTH))
        mods_x = [m[:, None, :] for m in ada_params(c, ada_w[l], ada_b[l])]
        if l % 2 == 0:
            mods_c = ada_params(c_ctx, ada_w[l], ada_b[l])
            lam_init = 0.8 - 0.6 * math.exp(-0.3 * l)
            x, ctx = attention_layer(x, ctx, mods_x, mods_c, norm_g[l], attn_in_w[i],
                                     attn_qn_g[i], attn_kn_g[i], lam_q1[i], lam_k1[i],
                                     lam_q2[i], lam_k2[i], attn_subln_g[i], attn_out_w[i],
                                     lam_init, update_ctx)
        else:
            x = fourier_layer(x, *mods_x, norm_g[l], fourier_in_w[i], fourier_out_w[i])
            if update_ctx:
                mods_c = ada_params(c_ctx, ada_w[l], ada_b[l])
                ctx = fourier_layer(ctx, *mods_c, norm_g[l], fourier_in_w[i], fourier_out_w[i])
    return rms_norm(x, final_g)
```

```python
import math
import contextlib
import numpy as np
import ml_dtypes
import concourse.bass as bass
import concourse.mybir as mybir
from concourse.bass_utils import run_bass_kernel_spmd

F32 = mybir.dt.float32
BF16 = mybir.dt.bfloat16
AF = mybir.ActivationFunctionType
ALU = mybir.AluOpType
AX = mybir.AxisListType
NPBF = ml_dtypes.bfloat16

S = 4096
D = 1024
CTX = 256
NKT = 34
EPS = 1e-6
LAM_INIT0 = 0.8 - 0.6 * math.exp(-0.3 * 0)
N_CORES = 8
ARENA_BYTES = 212736
import os
DBGL = int(os.environ.get('KDBG', '9'))
DBG2 = int(os.environ.get('KDBG2', '2'))
KTBANK = int(os.environ.get('KTBANK', '0'))


class Prog:
    ENGS = ("pe", "act", "dve", "pool", "sp")

    def __init__(self):
        self.ops = []
        self.lw = {}
        self.rd = {}
        self.bar_deps = set()
        self.bar_done = set(self.ENGS)
        self.last_on = {}
        self.dma_since = []

    def add(self, eng, fn, r=(), w=(), dma=False):
        i = len(self.ops)
        deps = {}
        for k in r:
            j = self.lw.get(k)
            if j is not None:
                deps[j] = True
        for k in w:
            j = self.lw.get(k)
            if j is not None:
                deps.setdefault(j, False)
            for j in self.rd.get(k, ()):
                deps.setdefault(j, False)
        if eng not in self.bar_done:
            for j in self.bar_deps:
                deps.setdefault(j, True)
            self.bar_done.add(eng)
        self.ops.append([eng, fn, deps, dma])
        for k in r:
            lst = self.rd.setdefault(k, [])
            if not dma:
                lst[:] = [j for j in lst if self.ops[j][3] or self.ops[j][0] != eng]
            lst.append(i)
        for k in w:
            self.lw[k] = i
            self.rd[k] = []
        self.last_on[eng] = i
        if dma:
            self.dma_since.append(i)
        return i

    def barrier(self):
        deps = set(self.last_on.values()) | set(self.dma_since)
        if len(self.bar_done) < len(self.ENGS):
            deps |= self.bar_deps
        self.bar_deps = deps
        self.bar_done = set()
        self.dma_since = []

    def emit(self, nc, sems, dsems, final_wait_all=True):
        ops = self.ops
        ms = set()
        for i, op in enumerate(ops):
            eng, fn, deps, dma = op
            nd = []
            for j, raw in deps.items():
                ej, _, _, dj = ops[j]
                if (not dj) and ej == eng:
                    if eng == "pe":
                        continue
                nd.append(j)
            op[2] = nd
            for j in nd:
                ms.add(j)
        val = {}
        prev = {}
        cnt = {e: 0 for e in self.ENGS}
        dcnt = {e: 0 for e in self.ENGS}
        for i, (eng, fn, deps, dma) in enumerate(ops):
            if dma:
                n = dcnt[eng]
                K = len(dsems[eng])
                sem = dsems[eng][n % K]
                val[i] = (sem, 16 * (n // K + 1))
                if n >= K:
                    prev[i] = (sem, 16 * (n // K))
                dcnt[eng] = n + 1
            elif i in ms:
                cnt[eng] += 1
                val[i] = (sems[eng], cnt[eng])
        self.stats = dict(n_ops=len(ops), milestones=dict(cnt), dmas=dict(dcnt))

        def run(eng, e):
            waited = {}

            def wait(sem, v):
                if waited.get(id(sem), 0) < v:
                    e.wait_ge(sem, v)
                    waited[id(sem)] = v

            for i, (en, fn, deps, dma) in enumerate(ops):
                if en != eng:
                    continue
                for j in deps:
                    wait(*val[j])
                if i in prev:
                    wait(*prev[i])
                ins = fn(e)
                if i in val:
                    sem, v = val[i]
                    ins.then_inc(sem, 16 if dma else 1)
            if eng == "sp" and final_wait_all:
                for q in self.ENGS:
                    n = dcnt[q]
                    K = len(dsems[q])
                    for s_i in range(min(n, K)):
                        uses = (n - 1 - s_i) // K + 1
                        wait(dsems[q][s_i], 16 * uses)
                for q in ("pe", "act", "dve", "pool"):
                    if cnt[q] > 0:
                        wait(sems[q], cnt[q])

        with nc.Block() as block:
            @block.tensor
            def _(e):
                run("pe", e)

            @block.scalar
            def _(e):
                run("act", e)

            @block.vector
            def _(e):
                run("dve", e)

            @block.gpsimd
            def _(e):
                run("pool", e)

            @block.sync
            def _(e):
                run("sp", e)


class Arena:
    def __init__(self, ap, nbytes):
        self.ap = ap
        self.cap = nbytes
        self.off = 0
        self.peak = 0

    def alloc(self, nbytes, dtype=BF16):
        off = (self.off + 63) // 64 * 64
        assert off + nbytes <= self.cap, f"arena overflow: {off}+{nbytes} > {self.cap}"
        self.off = off + nbytes
        self.peak = max(self.peak, self.off)
        v = self.ap[:, off // 2:(off + nbytes) // 2]
        if dtype == F32:
            v = v.bitcast(F32)
        return v


def _rope_tables():
    tab = np.zeros((NKT, 128, 384), np.float32)
    tab[:2, :, 0:128] = 1.0
    tab[:2, :, 256:320] = 1.0
    n = np.arange(S)
    rows = (n // 64).astype(np.float32)
    cols = (n % 64).astype(np.float32)

    def cs(dim):
        q = dim // 4
        inv = (np.float32(10000.0) ** (-(np.arange(q, dtype=np.float32) / np.float32(q)))).astype(np.float32)
        ang = np.stack([rows[:, None] * inv, cols[:, None] * inv], axis=1).astype(np.float32)
        c = np.cos(ang).astype(np.float32)
        s = np.sin(ang).astype(np.float32)
        ce = np.broadcast_to(c[:, :, None, :], (S, 2, 2, q)).reshape(S, dim)
        se = np.broadcast_to(s[:, :, None, :], (S, 2, 2, q)).reshape(S, dim)
        return ce, se

    ca, sa = cs(128)
    cb, sb = cs(64)
    full = np.concatenate([ca, sa, cb, sb], axis=1).reshape(32, 128, 384)
    tab[2:] = full
    return tab


def _fourier_tables():
    nh = np.arange(128)[:, None].astype(np.float64)
    kl = np.arange(128)[None, :].astype(np.float64)
    ang = 2 * np.pi * nh * kl / 128.0
    norm = 1.0 / math.sqrt(4096.0 * 256.0)
    f1 = np.zeros((128, 128, 2))
    f1[:, :, 0] = np.cos(ang) * norm
    f1[:, :, 1] = -np.sin(ang) * norm
    f1 = f1.reshape(128, 256)
    j = (np.arange(2)[None, :, None] * 128 + np.arange(128)[:, None, None]).astype(np.float64)
    m = np.arange(256)[None, None, :].astype(np.float64)
    a2 = 2 * np.pi * j * m / 256.0
    cs = np.concatenate([np.cos(a2), np.sin(a2)], axis=2).reshape(128, 1024)
    par = np.arange(2)[:, None, None, None, None, None]
    nlo = np.arange(32)[None, :, None, None, None, None].astype(np.float64)
    ri = np.arange(2)[None, None, :, None, None, None]
    kp = np.arange(64)[None, None, None, :, None, None]
    wh = np.arange(2)[None, None, None, None, :, None]
    khi = np.arange(32)[None, None, None, None, None, :]
    k = (2 * kp + par) + 128 * khi
    ang3 = 2 * np.pi * nlo * k / 4096.0
    tr = np.cos(ang3)
    ti = -np.sin(ang3)
    shape = (2, 32, 2, 64, 2, 32)
    tr = np.broadcast_to(tr, shape)
    ti = np.broadcast_to(ti, shape)
    rib = np.broadcast_to(ri, shape)
    whb = np.broadcast_to(wh, shape)
    t = np.where(whb == 0, np.where(rib == 0, tr, -ti), np.where(rib == 0, ti, tr))
    tt = t.reshape(128, 64 * 2 * 32)
    ttz = np.zeros((128, 2, 64 * 2 * 32))
    ttz[0:64, 0, :] = tt[0:64]
    ttz[64:128, 1, :] = tt[64:128]
    return f1.astype(NPBF), cs.astype(NPBF), ttz.reshape(128, 8192).astype(NPBF)


_CONSTS = {}


def _consts():
    if _CONSTS:
        return _CONSTS
    cf32 = np.zeros((128, 388), np.float32)
    cf32[:, 260:388] = 1.0
    cf32[0, 0:128] = 1.0
    cf32[32, 128:256] = 1.0
    cf32[0, 256] = 1.0
    cf32[32, 258] = 1.0
    cbf = np.zeros((128, 256), np.float32)
    cbf[:, 0:128] = np.eye(128, dtype=np.float32)
    cbf[:, 128:256] = 1.0
    f1, cs, tt = _fourier_tables()
    _CONSTS.update(cf32=cf32, cbf=cbf.astype(NPBF), rope=_rope_tables(), f1=f1, cs=cs, tt=tt)
    return _CONSTS


def build_program(stage="full"):
    nc = bass.Bass("TRN2", target_bir_lowering=False)
    P = Prog()

    def din(name, shape, dt=F32):
        return nc.dram_tensor(name, list(shape), dt, kind="ExternalInput").ap()

    def dint(name, shape, dt, ext=False):
        return nc.dram_tensor(name, list(shape), dt,
                              kind=("ExternalOutput" if ext else "Internal")).ap()

    x_d = din("x", [S, D])
    ctx_d = din("ctx", [CTX, D])
    cT_d = din("cT", [128, 16])
    adaw_d = din("ada_w", [2, D, 3 * D])
    adab_d = din("ada_b", [2, 3 * D])
    ngT_d = din("ngT", [128, 16])
    win_d = din("win", [D, 3584])
    wout_d = din("wout", [D, D])
    fin_d = din("fin", [D, 2 * D])
    fout_d = din("fout", [D, D])
    gbc_d = din("gbc", [128, 1280])
    lam_d = din("lam", [1, 256])
    sgT_d = din("sgT", [128, 1])
    cf32_d = din("cf32", [128, 388])
    cbf_d = din("cbf", [128, 256], BF16)
    rope_d = din("rope", [NKT, 128, 384])
    f1_d = din("f1", [128, 256], BF16)
    cs_d = din("cs", [128, 1024], BF16)
    tt_d = din("tt", [128, 8192], BF16)
    out_d = nc.dram_tensor("out", [S, D], F32, kind="ExternalOutput").ap()
    og_d = dint("ogd", [D, S], BF16, ext=(stage in ("L0", "L0s")))
    x1_d = dint("x1d", [S, D], F32)
    u_d = dint("ud", [S, D], BF16)
    g_d = dint("gd", [D, S], BF16)
    dbg_d = dint("dbg", [128, 2048], F32, ext=True) if stage[0] in "PS" else None

    es = contextlib.ExitStack()
    with es:
        arena_t = es.enter_context(nc.sbuf_tensor("arena", [128, ARENA_BYTES // 2], BF16))
        ps = es.enter_context(nc.psum_tensor("ps", [128, 8, 512], F32))
        sems = {e: es.enter_context(nc.semaphore("s_" + e)) for e in ("pe", "act", "dve", "pool")}
        dsems = {e: [] for e in Prog.ENGS}
        dsems["sp"] = [es.enter_context(nc.semaphore(f"d_sp{i}")) for i in range(12)]
        dsems["pool"] = [es.enter_context(nc.semaphore(f"d_pl{i}")) for i in range(6)]
        AR = Arena(arena_t[:, :], ARENA_BYTES)

        def bank(i):
            return ps[:, i, :]

        def bankbf(i):
            return ps[:, i, :].bitcast(BF16)

        def pk(i):
            return f"ps{i}"

        CBF = AR.alloc(512)
        ident = CBF[:, 0:128]
        onesb = CBF[:, 128:256]
        CF = AR.alloc(388 * 4, F32)
        sel0 = CF[:, 0:128]
        sel32 = CF[:, 128:256]
        e0 = CF[:, 256:258]
        e32 = CF[:, 258:260]
        onesf = CF[:, 260:388]
        GB = AR.alloc(256 * 4, F32)
        qn_bc = GB[:, 0:128]
        kn_bc = GB[:, 128:256]
        SM = AR.alloc(128 * 4, F32)
        CT = SM[:, 0:16]
        NG = SM[:, 16:32]
        MODS = SM[:, 32:64]
        neghalf = SM[:, 64:65]
        sgcol = SM[:, 65:66]
        gcol = SM[:, 66:67]
        neglam = SM[:, 67:68]
        SSX = SM[:, 68:72]
        MSX = SM[:, 72:76]
        RSX = SM[:, 76:80]
        SSH = SM[:, 80:84]
        MSH = SM[:, 84:88]
        RSH = SM[:, 88:92]
        LR = SM[:, 92:94]
        LT = SM[0:1, 96:128]
        junk = AR.alloc(256)
        mark_persist = AR.off

        XT = [AR.alloc(4096, F32) for _ in range(2)]
        XN = [AR.alloc(2048) for _ in range(2)]
        ROPE = [AR.alloc(384 * 4, F32) for _ in range(2)]
        HT = AR.alloc(8192)
        HT3 = HT.rearrange("p (k t) -> p k t", k=8)
        tmp_off = (AR.off + 63) // 64 * 64
        TMP = [AR.alloc(2048, F32) for _ in range(4)]
        W1 = AR.alloc(32768)
        W1v = W1.rearrange("p (k c) -> p k c", k=8)
        mark_generic = AR.off

        def dma(out, in_, r, w, q="sp"):
            return P.add(q, lambda e: e.dma_start(out=out, in_=in_), r, w, dma=True)

        def act(out, in_, func, r, w, bias=0.0, scale=1.0, accum_out=None):
            if accum_out is None:
                return P.add("act", lambda e: e.activation(out=out, in_=in_, func=func,
                                                           bias=bias, scale=scale), r, w)
            return P.add("act", lambda e: e.activation(out=out, in_=in_, func=func, bias=bias,
                                                       scale=scale, accum_out=accum_out), r, w)

        def tt(eng, out, in0, in1, op, r, w):
            return P.add(eng, lambda e: e.tensor_tensor(out=out, in0=in0, in1=in1, op=op), r, w)

        def ts(eng, out, in0, s1, s2, op0, op1, r, w):
            if s2 is None:
                return P.add(eng, lambda e: e.tensor_scalar(out=out, in0=in0, scalar1=s1,
                                                            scalar2=None, op0=op0), r, w)
            return P.add(eng, lambda e: e.tensor_scalar(out=out, in0=in0, scalar1=s1, scalar2=s2,
                                                        op0=op0, op1=op1), r, w)

        def stt(out, in0, scalar, in1, op0, op1, r, w):
            return P.add("dve", lambda e: e.scalar_tensor_tensor(out=out, in0=in0, scalar=scalar,
                                                                 in1=in1, op0=op0, op1=op1), r, w)

        def cp(eng, out, in_, r, w):
            if eng == "act":
                return P.add("act", lambda e: e.copy(out=out, in_=in_), r, w)
            return P.add(eng, lambda e: e.tensor_copy(out=out, in_=in_), r, w)

        def mm(out, lhsT, rhs, start, stop, r, w):
            return P.add("pe", lambda e: e.matmul(out, lhsT, rhs, start=start, stop=stop), r, w)

        def tr(out, in_, r, w):
            return P.add("pe", lambda e: e.transpose(out, in_, ident), r, w)

        def memset(eng, ap, v, w):
            return P.add(eng, lambda e: e.memset(ap, v), (), w)

        fe_cnt = [0]

        def fe1(src_ap, sb=None):
            n = fe_cnt[0]
            fe_cnt[0] += 1
            b = n % 2
            sl = n % 4
            xt, xn = XT[b], XN[b]
            kx, kn = f"xt{b}", f"xn{b}"
            if sb is None:
                dma(xt, src_ap, (), (kx,))
            else:
                xt, kx = sb
            act(xn, xt, AF.Square, (kx,), (kn, f"ssx{sl}"), accum_out=SSX[:, sl:sl + 1])
            ts("dve", MSX[:, sl:sl + 1], SSX[:, sl:sl + 1], 1.0 / D, EPS, ALU.mult, ALU.add,
               (f"ssx{sl}",), (f"msx{sl}",))
            tt("pool", RSX[:, sl:sl + 1], MSX[:, sl:sl + 1], neghalf, ALU.pow,
               (f"msx{sl}", "consts"), (f"rsx{sl}",))
            act(xn, xt, AF.Identity, (kx, f"rsx{sl}"), (kn,), scale=RSX[:, sl:sl + 1])
            return b

        def fe2(b, hslot, moff, tbank):
            xn, kn = XN[b], f"xn{b}"
            tb = bankbf(tbank)
            for c in range(8):
                tr(tb[:, c * 128:(c + 1) * 128], xn[:, c * 128:(c + 1) * 128], (kn, "consts"),
                   (pk(tbank),))
            for c in range(8):
                ts("dve", HT3[:, c, hslot * 128:(hslot + 1) * 128], tb[:, c * 128:(c + 1) * 128],
                   MODS[:, moff + 8 + c:moff + 9 + c], MODS[:, moff + c:moff + c + 1],
                   ALU.mult, ALU.add, (pk(tbank), "mods"), (f"hT{hslot}_{c}",))

        def hkeys(slots):
            return tuple(f"hT{j}_{c}" for j in slots for c in range(8))

        def ada_third(l, t, SCv, MR, ADB):
            dma(ADB[0:1, :], adab_d[l:l + 1, t * 1024:(t + 1) * 1024], (), ("adb0",))
            dma(ADB[32:33, :], adab_d[l:l + 1, t * 1024:(t + 1) * 1024], (), ("adb32",))
            for k in range(8):
                b = k % 2
                dma(XT[b], adaw_d[l, k * 128:(k + 1) * 128, t * 1024:(t + 1) * 1024], (), (f"xt{b}",))
                for hf in range(2):
                    mm(bank(hf), SCv[:, k, :], XT[b][:, hf * 512:(hf + 1) * 512], k == 0, k == 7,
                       (f"xt{b}", "sc"), (pk(hf),))
            for hf in range(2):
                tt("dve", MR[:, hf * 512:(hf + 1) * 512], bank(hf), ADB[:, hf * 512:(hf + 1) * 512],
                   ALU.add, (pk(hf), "adb0", "adb32"), ("mr",))

        def cols_from_rows(MR, which, with_ctx):
            pc = bank(2)
            for c in range(8):
                i0 = ((0 * 2 + which) * 8 + c) * 2
                mm(pc[:, i0:i0 + 2], MR[:, c * 128:(c + 1) * 128], e0, True, True,
                   ("mr", "c_cf"), (pk(2),))
                if with_ctx:
                    i1 = ((1 * 2 + which) * 8 + c) * 2
                    mm(pc[:, i1:i1 + 2], MR[:, c * 128:(c + 1) * 128], e32, True, True,
                       ("mr", "c_cf"), (pk(2),))

        def mods_finish(ngoff, with_ctx):
            pc = bank(2).rearrange("p (n two) -> p n two", two=2)
            for src in range(2 if with_ctx else 1):
                o = src * 16
                cp("dve", MODS[:, o:o + 8], pc[:, src * 16:src * 16 + 8, 0], (pk(2),), ("mods",))
                ts("dve", MODS[:, o + 8:o + 16], pc[:, src * 16 + 8:src * 16 + 16, 0], 1.0, None,
                   ALU.add, None, (pk(2),), ("mods",))
                tt("dve", MODS[:, o + 8:o + 16], MODS[:, o + 8:o + 16], NG[:, ngoff:ngoff + 8],
                   ALU.mult, ("mods", "c_ng"), ("mods",))

        def load_cast_weights(src_d, c0, ncols, dst3, dcol0, keyw):
            i = 0
            for k in range(8):
                for cc in range(0, ncols, 1024):
                    w = min(1024, ncols - cc)
                    b = i % 2
                    dma(XT[b][:, 0:w], src_d[k * 128:(k + 1) * 128, c0 + cc:c0 + cc + w], (), (f"xt{b}",))
                    eng = ("pool", "act", "dve")[i % 3]
                    cp(eng, dst3[:, k, dcol0 + cc:dcol0 + cc + w], XT[b][:, 0:w], (f"xt{b}",), (keyw,))
                    i += 1

        def finish_dbg():
            P.barrier()
            dma(dbg_d[:, 0:32], MODS, ("mods",), ())
            cp("pool", TMP[0][:, 0:512], KTA[:, 0, 0:512], (), ("tmp0",))
            dma(dbg_d[:, 512:1024], TMP[0][:, 0:512], ("tmp0",), ())
            cp("pool", TMP[1][:, 0:512], KTB[:, 1, 256:768], (), ("tmp1",))
            dma(dbg_d[:, 1024:1536], TMP[1][:, 0:512], ("tmp1",), ())
            cp("pool", TMP[2][:, 0:256], VA[:, 3, :], (), ("tmp2",))
            cp("pool", TMP[2][:, 256:512], VB[:, 3, 0:256], (), ("tmp2",))
            dma(dbg_d[:, 1536:2048], TMP[2][:, 0:512], ("tmp2",), ())
            memset("pool", TMP[3][:, 0:32], 0.0, ("tmp3",))
            cp("pool", TMP[3][:, 0:1], neglam, (), ("tmp3",))
            cp("pool", TMP[3][:, 1:2], gcol, (), ("tmp3",))
            dma(dbg_d[:, 32:64], TMP[3][:, 0:32], ("tmp3",), ())
            P.emit(nc, sems, dsems)
            return nc, P

        dma(CBF, cbf_d, (), ("c_cbf",))
        dma(CF, cf32_d, (), ("c_cf",))
        dma(GB, gbc_d[:, 0:256], (), ("c_gb",))
        dma(CT, cT_d, (), ("c_ct",))
        dma(NG, ngT_d, (), ("c_ng",))
        dma(sgcol, sgT_d, (), ("c_sg",))
        LAMT = TMP[3][:, 0:256]
        dma(LAMT[0:1, :], lam_d, (), ("lamt",))
        memset("pool", neghalf, -0.5, ("c_nh",))
        memset("pool", LR, 0.0, ("lr",))
        if stage == "S0":
            cp("pool", MODS, GB[:, 0:32], ("c_gb",), ("mods",))
            P.emit(nc, sems, dsems)
            return nc, P

        KTA = AR.alloc(2 * 4352 * 2).rearrange("p (h n) -> p h n", h=2)
        KTB = AR.alloc(4 * 4352 * 2).rearrange("p (h n) -> p h n", h=4)
        VA = AR.alloc(NKT * 256 * 2).rearrange("p (t c) -> p t c", t=NKT)
        VB = AR.alloc(NKT * 512 * 2).rearrange("p (t c) -> p t c", t=NKT)
        ROT = AR.alloc(4096)
        QTA = AR.alloc(4096).rearrange("p (h t) -> p h t", h=4)
        QTB = AR.alloc(8192).rearrange("p (h i t) -> p h i t", h=4, i=2)
        SG = AR.alloc(8192).rearrange("p (c t) -> p c t", c=8)
        NPT = 4
        PT = [AR.alloc(2048) for _ in range(NPT)]
        GF = AR.alloc(1024, F32)
        SQ = AR.alloc(1024)
        ACC = AR.alloc(2048, F32)
        QS = [AR.alloc(1024) for _ in range(2)]
        l0_peak = AR.off
        SCf = QTB.rearrange("p h i t -> p (h i t)")[:, 0:2048].bitcast(F32)
        SCv = SCf.rearrange("p (k m) -> p k m", k=8)
        MR = SG.rearrange("p c t -> p (c t)")[:, 0:2048].bitcast(F32)
        ADB = SG.rearrange("p c t -> p (c t)")[:, 2048:4096].bitcast(F32)

        memset("pool", SCf, 0.0, ("sc",))
        memset("pool", ADB, 0.0, ("adb0", "adb32"))
        act(SCv[:, :, 0], CT[:, 0:8], AF.Silu, ("c_ct", "sc"), ("sc",))
        act(SCv[:, :, 32], CT[:, 8:16], AF.Silu, ("c_ct", "sc"), ("sc",))
        for t in range(2):
            ada_third(0, t, SCv, MR, ADB)
            cols_from_rows(MR, t, True)
        mods_finish(0, True)
        if stage == "S1":
            return finish_dbg()

        LV = LAMT[0:1, :].rearrange("p (a n) -> p a n", a=4)
        LP = LT[:, 0:2]
        tt("dve", LAMT[0:1, 0:64], LV[:, 0, :], LV[:, 1, :], ALU.mult, ("lamt",), ("lamt",))
        tt("dve", LAMT[0:1, 128:192], LV[:, 2, :], LV[:, 3, :], ALU.mult, ("lamt",), ("lamt",))
        P.add("dve", lambda e: e.reduce_sum(out=LP[:, 0:1], in_=LAMT[0:1, 0:64], axis=AX.X),
              ("lamt",), ("lp",))
        P.add("dve", lambda e: e.reduce_sum(out=LP[:, 1:2], in_=LAMT[0:1, 128:192], axis=AX.X),
              ("lamt",), ("lp",))
        act(LP, LP, AF.Exp, ("lp",), ("lp",))
        tt("dve", LR[0:1, 0:1], LP[:, 1:2], LP[:, 0:1], ALU.subtract, ("lp", "lr"), ("lr",))
        ts("dve", LR[0:1, 0:1], LR[0:1, 0:1], -LAM_INIT0, None, ALU.add, None, ("lr",), ("lr",))
        mm(bank(3)[:, 0:2], sel0, LR, True, True, ("lr", "c_cf"), (pk(3),))
        cp("dve", neglam, bank(3)[:, 0:1], (pk(3),), ("consts2",))
        ts("dve", gcol, sgcol, 1.0 - LAM_INIT0, None, ALU.mult, None, ("c_sg",), ("consts2",))

        if stage == "S2":
            return finish_dbg()
        load_cast_weights(win_d, 0, 1536, W1v, 0, "w1")
        if stage == "S3":
            return finish_dbg()

        P.barrier()
        memset("pool", QTB.rearrange("p h i t -> p (h i t)"), 0.0, ("qtb0", "qtb1"))

        def kv_post(kt, s, rb):
            b1, b2, b3 = 4 * s + 1, 4 * s + 2, 4 * s + 3
            rope = ROPE[rb]
            rk = f"rope{rb}"
            rot = ROT[:, s * 768:(s + 1) * 768]
            rkey = f"rotk{s}"
            cp("act", VA[:, kt, :], bank(b1)[:, 256:512], (pk(b1),), (f"va{kt}",))
            cp("act", VB[:, kt, :], bank(b3), (pk(b3),), (f"vb{kt}",))
            for h in range(2):
                act(junk, bank(b1)[:, h * 128:(h + 1) * 128], AF.Square, (pk(b1),), (f"ssh{h}", "junk"),
                    accum_out=SSH[:, h:h + 1])
            ts("dve", MSH[:, 0:2], SSH[:, 0:2], 1.0 / 128, EPS, ALU.mult, ALU.add,
               ("ssh0", "ssh1"), ("msh",))
            tt("pool", RSH[:, 0:2], MSH[:, 0:2], neghalf.broadcast_to([128, 2]), ALU.pow,
               ("msh", "consts"), ("rsh",))
            tt("pool", GF[:, 0:128], rope[:, 0:128], kn_bc, ALU.mult, (rk, "consts"), ("gf",))
            tt("pool", GF[:, 128:256], rope[:, 128:256], kn_bc, ALU.mult, (rk, "consts"), ("gf",))
            for h in range(2):
                stt(TMP[0][:, h * 128:(h + 1) * 128], bank(b1)[:, h * 128:(h + 1) * 128],
                    RSH[:, h:h + 1], GF[:, 0:128], ALU.mult, ALU.mult, (pk(b1), "rsh", "gf"), ("tmp0",))
                stt(TMP[1][:, h * 128:(h + 1) * 128], bank(b1)[:, h * 128:(h + 1) * 128],
                    RSH[:, h:h + 1], GF[:, 128:256], ALU.mult, ALU.mult, (pk(b1), "rsh", "gf"), ("tmp1",))
            t1 = TMP[0][:, 0:256].rearrange("p (g two f) -> p g two f", two=2, f=32)
            t2 = TMP[1][:, 0:256].rearrange("p (g two f) -> p g two f", two=2, f=32)
            ro = rot[:, 0:256].rearrange("p (g two f) -> p g two f", two=2, f=32)
            tt("pool", ro[:, :, 0, :], t1[:, :, 0, :], t2[:, :, 1, :], ALU.subtract,
               ("tmp0", "tmp1"), (rkey,))
            tt("pool", ro[:, :, 1, :], t2[:, :, 0, :], t1[:, :, 1, :], ALU.add,
               ("tmp0", "tmp1"), (rkey,))
            xb = bank(b2).rearrange("p (g d) -> p g d", g=8)
            cbb = rope[:, 256:320].unsqueeze(1).broadcast_to([128, 8, 64])
            sbb = rope[:, 320:384].unsqueeze(1).broadcast_to([128, 8, 64])
            tt("dve", TMP[2].rearrange("p (g d) -> p g d", g=8), xb, cbb, ALU.mult, (pk(b2), rk), ("tmp2",))
            tt("dve", TMP[3].rearrange("p (g d) -> p g d", g=8), xb, sbb, ALU.mult, (pk(b2), rk), ("tmp3",))
            t1 = TMP[2].rearrange("p (g two f) -> p g two f", two=2, f=16)
            t2 = TMP[3].rearrange("p (g two f) -> p g two f", two=2, f=16)
            ro = rot[:, 256:768].rearrange("p (g two f) -> p g two f", two=2, f=16)
            tt("pool", ro[:, :, 0, :], t1[:, :, 0, :], t2[:, :, 1, :], ALU.subtract,
               ("tmp2", "tmp3"), (rkey,))
            tt("pool", ro[:, :, 1, :], t2[:, :, 0, :], t1[:, :, 1, :], ALU.add,
               ("tmp2", "tmp3"), (rkey,))

        def kv_trans(kt, s):
            b0 = 4 * s + KTBANK
            rot = ROT[:, s * 768:(s + 1) * 768]
            tb = bankbf(b0)
            for j in range(6):
                tr(tb[:, j * 128:(j + 1) * 128], rot[:, j * 128:(j + 1) * 128], (f"rotk{s}", "consts"),
                   (pk(b0),))
            if DBG2 >= 1:
                cp("dve", KTA[:, :, kt * 128:(kt + 1) * 128],
                   tb[:, 0:256].rearrange("p (h t) -> p h t", h=2), (pk(b0),), (f"kta{kt}",))
            if DBG2 >= 2:
                cp("dve", KTB[:, :, kt * 128:(kt + 1) * 128],
                   tb[:, 256:768].rearrange("p (h t) -> p h t", h=4), (pk(b0),), (f"ktb{kt}_0", f"ktb{kt}_1"))

        NK1 = NKT if stage != "S4" else 3

        p1_buf = {}

        def p1_A1(kt):
            src = ctx_d[kt * 128:(kt + 1) * 128, :] if kt < 2 else x_d[(kt - 2) * 128:(kt - 1) * 128, :]
            p1_buf[kt] = fe1(src)

        def p1_rope(kt):
            dma(ROPE[kt % 2], rope_d[kt], (), (f"rope{kt % 2}",))

        def p1_A2(kt):
            s_ = kt % 2
            fe2(p1_buf[kt], s_, 16 if kt < 2 else 0, 4 * s_)

        def p1_B(kt):
            s_ = kt % 2
            for j, bnk in enumerate((4 * s_ + 1, 4 * s_ + 2, 4 * s_ + 3)):
                for k in range(8):
                    mm(bank(bnk), HT3[:, k, s_ * 128:(s_ + 1) * 128], W1v[:, k, j * 512:(j + 1) * 512],
                       k == 0, k == 7, (f"hT{s_}_{k}", "w1"), (pk(bnk),))
            kv_post(kt, s_, kt % 2)

        p1_A1(0)
        p1_A1(1)
        p1_rope(0)
        p1_A2(0)
        for kt in range(NK1):
            if kt + 2 < NK1:
                p1_A1(kt + 2)
            if kt + 1 < NK1:
                p1_rope(kt + 1)
            if kt + 1 < NK1:
                p1_A2(kt + 1)
            p1_B(kt)
            if kt >= 1:
                kv_trans(kt - 1, (kt - 1) % 2)
        kv_trans(NK1 - 1, (NK1 - 1) % 2)

        if stage in ("P1", "S4"):
            return finish_dbg()

        P.barrier()
        load_cast_weights(win_d, 1536, 2048, W1v, 0, "w1")
        P.barrier()
        OG3 = HT3
        QTBf = QTB

        def q_post(j, rb, ba, bb):
            rope = ROPE[rb]
            rk = f"rope{rb}"
            rot = ROT[:, (j % 2) * 1024:(j % 2 + 1) * 1024]
            rqk = f"rotq{j % 2}"
            for h in range(4):
                act(junk, bank(ba)[:, h * 128:(h + 1) * 128], AF.Square, (pk(ba),), (f"ssh{h}", "junk"),
                    accum_out=SSH[:, h:h + 1])
            ts("dve", MSH[:, 0:4], SSH[:, 0:4], 1.0 / 128, EPS, ALU.mult, ALU.add,
               ("ssh0", "ssh1", "ssh2", "ssh3"), ("msh",))
            tt("pool", RSH[:, 0:4], MSH[:, 0:4], neghalf.broadcast_to([128, 4]), ALU.pow,
               ("msh", "consts"), ("rsh",))
            tt("pool", GF[:, 0:128], rope[:, 0:128], qn_bc, ALU.mult, (rk, "consts"), ("gf",))
            tt("pool", GF[:, 128:256], rope[:, 128:256], qn_bc, ALU.mult, (rk, "consts"), ("gf",))
            for h in range(4):
                stt(TMP[0][:, h * 128:(h + 1) * 128], bank(ba)[:, h * 128:(h + 1) * 128],
                    RSH[:, h:h + 1], GF[:, 0:128], ALU.mult, ALU.mult, (pk(ba), "rsh", "gf"), ("tmp0",))
                stt(TMP[1][:, h * 128:(h + 1) * 128], bank(ba)[:, h * 128:(h + 1) * 128],
                    RSH[:, h:h + 1], GF[:, 128:256], ALU.mult, ALU.mult, (pk(ba), "rsh", "gf"), ("tmp1",))
            t1 = TMP[0].rearrange("p (g two f) -> p g two f", two=2, f=32)
            t2 = TMP[1].rearrange("p (g two f) -> p g two f", two=2, f=32)
            ro = rot[:, 0:512].rearrange("p (g two f) -> p g two f", two=2, f=32)
            tt("pool", ro[:, :, 0, :], t1[:, :, 0, :], t2[:, :, 1, :], ALU.subtract,
               ("tmp0", "tmp1"), (rqk,))
            tt("pool", ro[:, :, 1, :], t2[:, :, 0, :], t1[:, :, 1, :], ALU.add,
               ("tmp0", "tmp1"), (rqk,))
            xb = bank(bb).rearrange("p (g d) -> p g d", g=8)
            cbb = rope[:, 256:320].unsqueeze(1).broadcast_to([128, 8, 64])
            sbb = rope[:, 320:384].unsqueeze(1).broadcast_to([128, 8, 64])
            tt("dve", TMP[2].rearrange("p (g d) -> p g d", g=8), xb, cbb, ALU.mult, (pk(bb), rk), ("tmp2",))
            tt("dve", TMP[3].rearrange("p (g d) -> p g d", g=8), xb, sbb, ALU.mult, (pk(bb), rk), ("tmp3",))
            t1 = TMP[2].rearrange("p (g two f) -> p g two f", two=2, f=16)
            t2 = TMP[3].rearrange("p (g two f) -> p g two f", two=2, f=16)
            ro = rot[:, 512:1024].rearrange("p (g two f) -> p g two f", two=2, f=16)
            tt("pool", ro[:, :, 0, :], t1[:, :, 0, :], t2[:, :, 1, :], ALU.subtract,
               ("tmp2", "tmp3"), (rqk,))
            tt("pool", ro[:, :, 1, :], t2[:, :, 0, :], t1[:, :, 1, :], ALU.add,
               ("tmp2", "tmp3"), (rqk,))

        def q_trans(j, bt):
            rot = ROT[:, (j % 2) * 1024:(j % 2 + 1) * 1024]
            rqk = f"rotq{j % 2}"
            tb = bankbf(bt)
            for c in range(8):
                tr(tb[:, c * 128:(c + 1) * 128], rot[:, c * 128:(c + 1) * 128], (rqk, "consts"),
                   (pk(bt),))
            cp("dve", QTA[:, :, j * 128:(j + 1) * 128],
               tb[:, 0:512].rearrange("p (h t) -> p h t", h=4), (pk(bt),), ("qta",))
            tbb = tb[:, 512:1024].rearrange("p (h t) -> p h t", h=4)
            cp("dve", QTB[0:64, :, 0, j * 128:(j + 1) * 128], tbb[0:64], (pk(bt),), ("qtb0",))
            cp("dve", QTB[64:128, :, 1, j * 128:(j + 1) * 128], tbb[64:128], (pk(bt),), ("qtb1",))

        maps = [("B", h, i) for h in range(4) for i in range(2)] + [("A", h, 0) for h in range(4)]
        SCALE_A = 128.0 ** -0.5
        SCALE_B = 64.0 ** -0.5

        def attention_block(qb):
            seq = [(mi, pr) for mi in range(len(maps)) for pr in range(NKT // 2)]
            NS = 3
            free = list(range(NS))
            spair = {}
            pending = []
            nextq = [0]

            def qk(idx, p):
                mi, pr = seq[idx]
                kind, h, i = maps[mi]
                spair[idx] = p
                sb = 2 * p
                for t in range(2):
                    kt = 2 * pr + t
                    if kind == "A":
                        lhsT = KTA[:, h // 2, kt * 128:(kt + 1) * 128]
                        rhs = QTA[:, h, :]
                        r = (f"kta{kt}", "qta")
                    else:
                        lhsT = KTB[:, h, kt * 128:(kt + 1) * 128]
                        rhs = QTB[:, h, i, :]
                        r = (f"ktb{kt}_{h % 2}", f"qtb{i}")
                    mm(bank(sb + t), lhsT, rhs, True, True, r, (pk(sb + t),))

            def ex(idx):
                mi, pr = seq[idx]
                kind = maps[mi][0]
                sb = 2 * spair[idx]
                pt = PT[idx % NPT]
                act(pt.rearrange("p (t n) -> p t n", t=2), ps[:, sb:sb + 2, :], AF.Exp,
                    (pk(sb), pk(sb + 1)), (f"pt{idx % NPT}",),
                    scale=(SCALE_A if kind == "A" else SCALE_B))

            def pv(idx):
                mi, pr = seq[idx]
                kind, h, i = maps[mi]
                ob = 6 + (mi % 2)
                pt = PT[idx % NPT]
                for t in range(2):
                    kt = 2 * pr + t
                    if kind == "A":
                        lhsT = VA[:, kt, (h // 2) * 128:(h // 2 + 1) * 128]
                        r = (f"va{kt}", f"pt{idx % NPT}")
                    else:
                        lhsT = VB[:, kt, h * 128:(h + 1) * 128]
                        r = (f"vb{kt}", f"pt{idx % NPT}")
                    mm(bank(ob), lhsT, pt[:, t * 512:(t + 1) * 512], kt == 0, kt == NKT - 1, r, (pk(ob),))

            def pool_acc(q, first):
                if first:
                    cp("pool", ACC, QS[q % 2], (f"qs{q % 2}",), ("acc",))
                else:
                    tt("pool", ACC, ACC, QS[q % 2], ALU.add, ("acc", f"qs{q % 2}"), ("acc",))

            def psum2(idx):
                mi, pr = seq[idx]
                pt = PT[idx % NPT]
                q = pr // 2
                if pr % 2 == 0:
                    tt("dve", QS[q % 2], pt[:, 0:512], pt[:, 512:1024], ALU.add, (f"pt{idx % NPT}",),
                       (f"qs{q % 2}",))
                    if pr == NKT // 2 - 1:
                        pool_acc(q, False)
                else:
                    tt("dve", QS[q % 2], QS[q % 2], pt[:, 0:512], ALU.add, (f"pt{idx % NPT}", f"qs{q % 2}"),
                       (f"qs{q % 2}",))
                    tt("dve", QS[q % 2], QS[q % 2], pt[:, 512:1024], ALU.add, (f"pt{idx % NPT}", f"qs{q % 2}"),
                       (f"qs{q % 2}",))
                    pool_acc(q, pr == 1)

            def finish(mi, db, idx):
                kind, h, i = maps[mi]
                ob = 6 + (mi % 2)
                T = TMP[mi % 2]
                tk = f"tmp{mi % 2}"
                mm(bank(db), onesf, ACC, True, True, ("acc", "consts"), (pk(db),))
                P.add("dve", lambda e: e.reciprocal(out=T, in_=bank(db)), (pk(db),), (tk,))
                tt("dve", T, bank(ob), T, ALU.mult, (pk(ob), tk), (tk,))
                if kind == "A":
                    tt("pool", SG[:, h, :], T, SG[:, h, :], ALU.mult, (tk, f"sg{h}"), (f"sg{h}",))
                elif i == 1:
                    T0, T1 = TMP[0], TMP[1]

                    def f1():
                        stt(T0, T1, neglam, T0, ALU.mult, ALU.add, ("tmp0", "tmp1", "consts2"), ("tmp0",))
                        tt("pool", SQ, T0, T0, ALU.mult, ("tmp0",), ("sq",))

                    def f2b(bssq):
                        mm(bank(bssq), onesb, SQ, True, True, ("sq", "consts"), (pk(bssq),))
                        ts("dve", T1, bank(bssq), 1.0 / 128, EPS, ALU.mult, ALU.add, (pk(bssq),), ("tmp1",))

                    def f2():
                        pending.append(f2b)

                    def f3():
                        act(T1, T1, AF.Ln, ("tmp1",), ("tmp1",))
                        act(T1, T1, AF.Exp, ("tmp1",), ("tmp1",), scale=-0.5)

                    def f4():
                        stt(T0, T0, gcol, T1, ALU.mult, ALU.mult, ("tmp0", "tmp1", "consts2"), ("tmp0",))
                        tt("pool", SG[:, 4 + h, :], T0, SG[:, 4 + h, :], ALU.mult, ("tmp0", f"sg{4 + h}"),
                           (f"sg{4 + h}",))

                    for k, f in enumerate([f1, f2, f3, f4]):
                        deferred.setdefault(idx + 2 + 2 * k, []).append(f)

            n = len(seq)
            deferred = {}

            def fill(cur):
                while free and nextq[0] < n and nextq[0] <= cur + 3:
                    qk(nextq[0], free.pop(0))
                    nextq[0] += 1

            def run_pending(idx):
                returned = []
                while pending and free:
                    p = free.pop(0)
                    pending.pop(0)(2 * p)
                    returned.append(p)
                return returned

            fill(-1)
            for idx in range(n):
                for f in deferred.pop(idx, ()):
                    f()
                ex(idx)
                free.append(spair[idx])
                psum2(idx)
                returned = run_pending(idx)
                fill(idx)
                mi, pr = seq[idx]
                pv(idx)
                free.extend(returned)
                if pr == NKT // 2 - 1:
                    pending.append(lambda db, mi=mi, idx=idx: finish(mi, db, idx + 1))
            last = n
            while pending or deferred:
                free.extend(run_pending(last))
                for k in sorted(deferred):
                    if k <= last:
                        for f in deferred.pop(k):
                            f()
                last += 1

        NQB = 8 if stage != "L0s" else 1
        fb2 = {}

        def A2a(j, qb):
            ti = qb * 4 + j
            fb2[(qb, j)] = fe1(x_d[ti * 128:(ti + 1) * 128, :])

        def R2(j, qb):
            ti = qb * 4 + j
            dma(ROPE[ti % 2], rope_d[2 + ti], (), (f"rope{ti % 2}",))

        for qb in range(NQB):
            def A2b(j, qb=qb):
                fe2(fb2[(qb, j)], j, 0, 4 + (j % 2))

            def B2(j, qb=qb):
                ti = qb * 4 + j
                ba, bb = 2 * (j % 2), 2 * (j % 2) + 1
                for jj, bnk in enumerate((ba, bb)):
                    for k in range(8):
                        mm(bank(bnk), HT3[:, k, j * 128:(j + 1) * 128], W1v[:, k, jj * 512:(jj + 1) * 512],
                           k == 0, k == 7, (f"hT{j}_{k}", "w1"), (pk(bnk),))
                q_post(j, ti % 2, ba, bb)

            def C2(j):
                q_trans(j, 6 + (j % 2))

            def G2(c0, c1):
                for c in range(c0, c1):
                    bnk = 4 + (c % 2)
                    for k in range(8):
                        mm(bank(bnk), W1v[:, k, 1024 + c * 128:1024 + (c + 1) * 128], HT3[:, k, :],
                           k == 0, k == 7, tuple(f"hT{j}_{k}" for j in range(4)) + ("w1",), (pk(bnk),))
                    act(SG[:, c, :], bank(bnk), AF.Silu, (pk(bnk),), (f"sg{c}",))

            if qb == 0:
                A2a(0, qb); R2(0, qb); A2a(1, qb); R2(1, qb)
            A2b(0); A2a(2, qb); A2b(1); B2(0); R2(2, qb); A2a(3, qb); A2b(2); B2(1); R2(3, qb)
            C2(0); A2b(3); B2(2); C2(1)
            G2(0, 4); B2(3); C2(2); G2(4, 8); C2(3)
            if qb + 1 < NQB:
                A2a(0, qb + 1); R2(0, qb + 1); A2a(1, qb + 1); R2(1, qb + 1)
            attention_block(qb)
            for c4 in range(4):
                dma(og_d.rearrange("(c p) n -> p c n", p=128)[:, 2 * c4:2 * c4 + 2, qb * 512:(qb + 1) * 512],
                    SG[:, 2 * c4:2 * c4 + 2, :], (f"sg{2 * c4}", f"sg{2 * c4 + 1}"), ("ogd",))

        if stage in ("L0", "L0s"):
            P.emit(nc, sems, dsems)
            return nc, P


        P.barrier()
        AR.off = mark_generic
        WO = AR.alloc(16384)
        WO3 = WO.rearrange("p (k c) -> p k c", k=8)
        FC = AR.alloc(2560)
        F1 = FC[:, 0:256]
        CS3 = FC[:, 256:1280].rearrange("p (c n) -> p c n", c=2)
        GRH = [TMP[0], TMP[1]]
        PB = [AR.alloc(1024) for _ in range(4)]
        GG = AR.alloc(16384).rearrange("p (c n) -> p c n", c=2)
        YY = AR.alloc(32768)
        FG = AR.alloc(65536).rearrange("p (c n) -> p c n", c=8)
        YYf = YY
        SCf1 = YYf[:, 0:2048].bitcast(F32)
        SCv1 = SCf1.rearrange("p (k m) -> p k m", k=8)
        MR1 = YYf[:, 2048:4096].bitcast(F32)
        ADB1 = YYf[:, 4096:6144].bitcast(F32)
        OGB = YYf[:, 6144:10240].rearrange("p (c t) -> p c t", c=8)
        X1T = [YYf[:, 10240 + i * 2048:10240 + (i + 1) * 2048].bitcast(F32) for i in range(2)]
        UO = [YYf[:, 14336 + i * 1024:14336 + (i + 1) * 1024] for i in range(2)]
        FGf = FG.rearrange("p c n -> p (c n)")
        GO = FGf[:, 0:4096].rearrange("p (c t) -> p c t", c=8)
        gen_off = mark_persist
        TTZ = AR.ap[:, (mark_persist + 63) // 64 * 64 // 2:(mark_persist + 63) // 64 * 64 // 2 + 8192]
        TT5 = TTZ.rearrange("p (a k w h) -> p a k w h", a=2, k=64, w=2)

        memset("pool", SCf1, 0.0, ("sc",))
        memset("pool", ADB1, 0.0, ("adb0", "adb32"))
        act(SCv1[:, :, 0], CT[:, 0:8], AF.Silu, ("sc",), ("sc",))
        fg_bc = AR.ap[:, (tmp_off + 4096) // 2:(tmp_off + 8192) // 2].bitcast(F32)
        dma(fg_bc, gbc_d[:, 256:1280], (), ("fgbc",))
        dma(FC[:, 0:256], f1_d, (), ("fc1",))
        dma(FC[:, 256:1280], cs_d, (), ("fc2",))

        def gate_weights(l, src_d):
            ada_third(l, 2, SCv1, MR1, ADB1)
            for hf in range(2):
                mm(bank(4 + hf), sel0, MR1[:, hf * 512:(hf + 1) * 512], True, True, ("mr",), (pk(4 + hf),))

        def fold_gate_into(src_d, keyw):
            for k in range(8):
                b = k % 2
                dma(XT[b], src_d[k * 128:(k + 1) * 128, :], (), (f"xt{b}",))
                for hf in range(2):
                    tt("dve", WO3[:, k, hf * 512:(hf + 1) * 512], XT[b][:, hf * 512:(hf + 1) * 512],
                       bank(4 + hf), ALU.mult, (f"xt{b}", pk(4 + hf)), (keyw,))

        for t in range(2):
            ada_third(1, t, SCv1, MR1, ADB1)
            cols_from_rows(MR1, t, False)
        mods_finish(8, False)
        ada_third(1, 2, SCv1, MR1, ADB1)
        for hf in range(2):
            cp("dve", GRH[hf], MR1[:, hf * 512:(hf + 1) * 512], ("mr",), ("gr",))
        gate_weights(0, wout_d)
        fold_gate_into(wout_d, "wo")
        load_cast_weights(fin_d, 0, 2048, W1v, 0, "w1")
        P.barrier()

        for qb in range(8):
            for c4 in range(4):
                dma(OGB[:, 2 * c4:2 * c4 + 2, :],
                    og_d.rearrange("(c p) n -> p c n", p=128)[:, 2 * c4:2 * c4 + 2, qb * 512:(qb + 1) * 512],
                    ("ogd",), (f"ogb{c4}",))

            fb1 = {}

            def F1b(j, fb1=fb1):
                fe2(fb1[j], j, 0, 2 + (j % 2))

            def O1(j, qb=qb, fb1=fb1):
                ti = qb * 4 + j
                xb = ti % 2
                dma(X1T[xb], x_d[ti * 128:(ti + 1) * 128, :], (), (f"x1t{xb}",))
                for hf in range(2):
                    for c in range(8):
                        mm(bank(hf), OGB[:, c, j * 128:(j + 1) * 128], WO3[:, c, hf * 512:(hf + 1) * 512],
                           c == 0, c == 7, (f"ogb{c // 2}", "wo"), (pk(hf),))
                for hf in range(2):
                    tt("dve", X1T[xb][:, hf * 512:(hf + 1) * 512], bank(hf), X1T[xb][:, hf * 512:(hf + 1) * 512],
                       ALU.add, (pk(hf), f"x1t{xb}"), (f"x1t{xb}",))
                dma(x1_d[ti * 128:(ti + 1) * 128, :], X1T[xb], (f"x1t{xb}",), ("x1d",), q="pool")
                fb1[j] = fe1(None, sb=(X1T[xb], f"x1t{xb}"))

            def B1(j, qb=qb):
                ti = qb * 4 + j
                xb = ti % 2
                for hf in range(2):
                    for k in range(8):
                        mm(bank(4 + hf), HT3[:, k, j * 128:(j + 1) * 128], W1v[:, k, hf * 512:(hf + 1) * 512],
                           k == 0, k == 7, (f"hT{j}_{k}", "w1"), (pk(4 + hf),))
                    cp("act" if hf == 0 else "dve", UO[xb][:, hf * 512:(hf + 1) * 512], bank(4 + hf),
                       (pk(4 + hf),), (f"uo{xb}_{hf}",))
                dma(u_d[ti * 128:(ti + 1) * 128, :], UO[xb], (f"uo{xb}_0", f"uo{xb}_1"), ("ud",), q="pool")

            def G1(c0, c1):
                for c in range(c0, c1):
                    bnk = 6 + (c % 2)
                    for k in range(8):
                        mm(bank(bnk), W1v[:, k, 1024 + c * 128:1024 + (c + 1) * 128], HT3[:, k, :],
                           k == 0, k == 7, tuple(f"hT{j}_{k}" for j in range(4)) + ("w1",), (pk(bnk),))
                    act(GO[:, c, :], bank(bnk), AF.Silu, (pk(bnk),), ("go",))

            O1(0); O1(1); F1b(0); O1(2); F1b(1); B1(0); O1(3); F1b(2); B1(1); F1b(3); B1(2)
            G1(0, 4); B1(3); G1(4, 8)
            for c4 in range(4):
                dma(g_d.rearrange("(c p) n -> p c n", p=128)[:, 2 * c4:2 * c4 + 2, qb * 512:(qb + 1) * 512],
                    GO[:, 2 * c4:2 * c4 + 2, :], ("go",), ("gd",))

        P.barrier()
        dma(TTZ, tt_d, (), ("ttz",))
        UG = [W1[:, i * 8192:(i + 1) * 8192].rearrange("p (l c) -> p l c", l=32) for i in range(2)]
        Y5 = YY.rearrange("p (c k l r) -> p c k l r", c=2, k=128, l=32)
        u_v = u_d.rearrange("(nh nl) c -> nh nl c", nl=32)
        g_v = g_d.rearrange("(c p) n -> p c n", p=128)
        ev = [0]
        for gr in range(4):
            ub = gr % 2
            for n8 in range(8):
                dma(UG[ub][:, 4 * n8:4 * n8 + 4, :], u_v[:, 4 * n8:4 * n8 + 4, gr * 256:(gr + 1) * 256],
                    ("ud",), (f"ug{ub}_{n8}",))
            dma(GG, g_v[:, 2 * gr:2 * gr + 2, :], ("gd",), ("gg",))
            for nl in range(32):
                bnk = nl % 2
                for cc in range(2):
                    mm(bank(bnk)[:, cc * 256:(cc + 1) * 256], UG[ub][:, nl, cc * 128:(cc + 1) * 128], F1,
                       True, True, (f"ug{ub}_{nl // 4}", "fc1"), (pk(bnk),))
                cp("act" if nl % 4 != 3 else "dve", Y5[:, :, :, nl, :],
                   bank(bnk).rearrange("p (c k r) -> p c k r", c=2, r=2), (pk(bnk),), ("yy",))
            def chdft(kp, gr=gr):
                pbk = (2, 3, 6)[kp % 3]
                pb = PB[kp % 4]
                for cc in range(2):
                    mm(bank(pbk), Y5[:, cc, 2 * kp:2 * kp + 2, :, :].rearrange("p a l r -> p (a l r)"),
                       CS3[:, cc, :], cc == 0, cc == 1, ("yy", "fc2"), (pk(pbk),))
                cp("act" if kp % 3 != 2 else "dve", pb, bank(pbk), (pk(pbk),), (f"pb{kp % 4}",))

            def stage2(kp, gr=gr):
                pb = PB[kp % 4]
                fb = 4 + ((kp // 4) % 2)
                for par in range(2):
                    for mc in range(2):
                        col = (((kp % 4) * 2 + par) * 2 + mc) * 32
                        mm(bank(fb)[:, col:col + 32], pb[:, mc * 128:(mc + 1) * 128], TT5[:, par, kp, 0, :],
                           True, False, (f"pb{kp % 4}", "ttz"), (pk(fb),))
                        mm(bank(fb)[:, col:col + 32], pb[:, 256 + mc * 128:256 + (mc + 1) * 128],
                           TT5[:, par, kp, 1, :], False, True, (f"pb{kp % 4}", "ttz"), (pk(fb),))
                if kp % 4 == 3:
                    k0 = 2 * (kp - 3)
                    fbv = bank(fb).rearrange("p (a m h) -> p a m h", m=2, h=32)
                    for mc in range(2):
                        gv = GG[:, mc, :].rearrange("p (h l) -> p l h", l=128)[:, k0:k0 + 8, :]
                        ov = FG[:, 2 * gr + mc, :].rearrange("p (h l) -> p l h", l=128)[:, k0:k0 + 8, :]
                        tt("dve", ov, fbv[:, :, mc, :], gv, ALU.mult, (pk(fb), "gg"), ("fg",))

            chdft(0)
            chdft(1)
            for kp in range(64):
                if kp + 2 < 64:
                    chdft(kp + 2)
                stage2(kp)

        P.barrier()
        for hf in range(2):
            mm(bank(4 + hf), sel0, GRH[hf], True, True, ("gr",), (pk(4 + hf),))
        fold_gate_into(fout_d, "wo")
        P.barrier()
        ZT = [YY[:, i * 2048:(i + 1) * 2048].bitcast(F32) for i in range(4)]

        def c_X(ti):
            zb = ti % 4
            bo = 2 * (ti % 2)
            dma(ZT[zb], x1_d[ti * 128:(ti + 1) * 128, :], ("x1d",), (f"zt{zb}",))
            for hf in range(2):
                for c in range(8):
                    mm(bank(bo + hf), FG[:, c, ti * 128:(ti + 1) * 128], WO3[:, c, hf * 512:(hf + 1) * 512],
                       c == 0, c == 7, ("fg", "wo"), (pk(bo + hf),))
            for hf in range(2):
                tt("dve", ZT[zb][:, hf * 512:(hf + 1) * 512], bank(bo + hf), ZT[zb][:, hf * 512:(hf + 1) * 512],
                   ALU.add, (pk(bo + hf), f"zt{zb}"), (f"zt{zb}",))

        def c_Ya(ti):
            zb = ti % 4
            sl = ti % 4
            act(XN[ti % 2], ZT[zb], AF.Square, (f"zt{zb}",), (f"xn{ti % 2}", f"ssx{sl}"), accum_out=SSX[:, sl:sl + 1])
            ts("dve", MSX[:, sl:sl + 1], SSX[:, sl:sl + 1], 1.0 / D, EPS, ALU.mult, ALU.add,
               (f"ssx{sl}",), (f"msx{sl}",))
            tt("pool", RSX[:, sl:sl + 1], MSX[:, sl:sl + 1], neghalf, ALU.pow, (f"msx{sl}",), (f"rsx{sl}",))

        def c_Yb(ti):
            zb = ti % 4
            sl = ti % 4
            stt(ZT[zb], ZT[zb], RSX[:, sl:sl + 1], fg_bc, ALU.mult, ALU.mult, (f"zt{zb}", f"rsx{sl}", "fgbc"), (f"zt{zb}",))
            dma(out_d[ti * 128:(ti + 1) * 128, :], ZT[zb], (f"zt{zb}",), ("outd",), q="pool")

        c_X(0)
        for ti in range(32):
            if ti + 1 < 32:
                c_X(ti + 1)
            c_Ya(ti)
            if ti >= 1:
                c_Yb(ti - 1)
        c_Yb(31)

        P.emit(nc, sems, dsems)
        return nc, P


def _in_maps(inp):
    C = _consts()
    f = lambda a: np.ascontiguousarray(np.asarray(a, dtype=np.float32))
    x = f(inp["x"]); c = f(inp["c"]); ctx = f(inp["ctx"]); c_ctx = f(inp["c_ctx"])
    norm_g = f(inp["norm_g"])
    ngT = np.concatenate([norm_g[0].reshape(8, 128).T, norm_g[1].reshape(8, 128).T], axis=1)
    gbc = np.concatenate([np.broadcast_to(f(inp["attn_qn_g"])[0][None, :], (128, 128)),
                          np.broadcast_to(f(inp["attn_kn_g"])[0][None, :], (128, 128)),
                          np.broadcast_to(f(inp["final_g"])[None, :], (128, 1024))], axis=1)
    lam = np.concatenate([f(inp["lam_q1"])[0], f(inp["lam_k1"])[0], f(inp["lam_q2"])[0],
                          f(inp["lam_k2"])[0]])[None, :]
    shared = dict(
        ada_w=f(inp["ada_w"]), ada_b=f(inp["ada_b"]), ngT=np.ascontiguousarray(ngT),
        win=f(inp["attn_in_w"])[0], wout=f(inp["attn_out_w"])[0], fin=f(inp["fourier_in_w"])[0],
        fout=f(inp["fourier_out_w"])[0], gbc=np.ascontiguousarray(gbc), lam=np.ascontiguousarray(lam),
        sgT=np.ascontiguousarray(f(inp["attn_subln_g"])[0][:, None]),
        cf32=C["cf32"], cbf=C["cbf"], rope=C["rope"], f1=C["f1"], cs=C["cs"], tt=C["tt"])
    maps = []
    for b in range(N_CORES):
        cT = np.concatenate([c[b].reshape(8, 128).T, c_ctx.reshape(8, 128).T], axis=1)
        m = dict(shared)
        m.update(x=x[b], ctx=ctx[b], cT=np.ascontiguousarray(cT))
        maps.append(m)
    return maps


_PROG = {}


def kernel(**inputs):
    if "full" not in _PROG:
        _PROG["full"] = build_program("full")[0]
    nc = _PROG["full"]
    res = run_bass_kernel_spmd(nc, _in_maps(inputs), core_ids=list(range(N_CORES)))
    out = np.stack([np.asarray(r["out"], dtype=np.float32) for r in res.results], axis=0)
    return out
```

```python
import math
import contextlib
import numpy as np
import ml_dtypes
import concourse.bass as bass
import concourse.mybir as mybir
from concourse.bass_utils import run_bass_kernel_spmd

F32 = mybir.dt.float32
BF16 = mybir.dt.bfloat16
AF = mybir.ActivationFunctionType
ALU = mybir.AluOpType
AX = mybir.AxisListType
NPBF = ml_dtypes.bfloat16

S = 4096
D = 1024
CTX = 256
NKT = 34
EPS = 1e-6
LAM_INIT0 = 0.8 - 0.6 * math.exp(-0.3 * 0)
N_CORES = 8
ARENA_BYTES = 212736
import os
DBGL = int(os.environ.get('KDBG', '9'))
DBG2 = int(os.environ.get('KDBG2', '2'))
KTBANK = int(os.environ.get('KTBANK', '0'))


class Prog:
    ENGS = ("pe", "act", "dve", "pool", "sp")

    def __init__(self):
        self.ops = []
        self.lw = {}
        self.rd = {}
        self.bar_deps = set()
        self.bar_done = set(self.ENGS)
        self.last_on = {}
        self.dma_since = []

    def add(self, eng, fn, r=(), w=(), dma=False):
        i = len(self.ops)
        deps = {}
        for k in r:
            j = self.lw.get(k)
            if j is not None:
                deps[j] = True
        for k in w:
            j = self.lw.get(k)
            if j is not None:
                deps.setdefault(j, False)
            for j in self.rd.get(k, ()):
                deps.setdefault(j, False)
        if eng not in self.bar_done:
            for j in self.bar_deps:
                deps.setdefault(j, True)
            self.bar_done.add(eng)
        self.ops.append([eng, fn, deps, dma])
        for k in r:
            lst = self.rd.setdefault(k, [])
            if not dma:
                lst[:] = [j for j in lst if self.ops[j][3] or self.ops[j][0] != eng]
            lst.append(i)
        for k in w:
            self.lw[k] = i
            self.rd[k] = []
        self.last_on[eng] = i
        if dma:
            self.dma_since.append(i)
        return i

    def barrier(self):
        deps = set(self.last_on.values()) | set(self.dma_since)
        if len(self.bar_done) < len(self.ENGS):
            deps |= self.bar_deps
        self.bar_deps = deps
        self.bar_done = set()
        self.dma_since = []

    def emit(self, nc, sems, dsems, final_wait_all=True):
        ops = self.ops
        ms = set()
        for i, op in enumerate(ops):
            eng, fn, deps, dma = op
            nd = []
            for j, raw in deps.items():
                ej, _, _, dj = ops[j]
                if (not dj) and ej == eng:
                    if eng == "pe":
                        continue
                nd.append(j)
            op[2] = nd
            for j in nd:
                ms.add(j)
        val = {}
        prev = {}
        cnt = {e: 0 for e in self.ENGS}
        dcnt = {e: 0 for e in self.ENGS}
        for i, (eng, fn, deps, dma) in enumerate(ops):
            if dma:
                n = dcnt[eng]
                K = len(dsems[eng])
                sem = dsems[eng][n % K]
                val[i] = (sem, 16 * (n // K + 1))
                if n >= K:
                    prev[i] = (sem, 16 * (n // K))
                dcnt[eng] = n + 1
            elif i in ms:
                cnt[eng] += 1
                val[i] = (sems[eng], cnt[eng])
        self.stats = dict(n_ops=len(ops), milestones=dict(cnt), dmas=dict(dcnt))

        def run(eng, e):
            waited = {}

            def wait(sem, v):
                if waited.get(id(sem), 0) < v:
                    e.wait_ge(sem, v)
                    waited[id(sem)] = v

            for i, (en, fn, deps, dma) in enumerate(ops):
                if en != eng:
                    continue
                for j in deps:
                    wait(*val[j])
                if i in prev:
                    wait(*prev[i])
                ins = fn(e)
                if i in val:
                    sem, v = val[i]
                    ins.then_inc(sem, 16 if dma else 1)
            if eng == "sp" and final_wait_all:
                for q in self.ENGS:
                    n = dcnt[q]
                    K = len(dsems[q])
                    for s_i in range(min(n, K)):
                        uses = (n - 1 - s_i) // K + 1
                        wait(dsems[q][s_i], 16 * uses)
                for q in ("pe", "act", "dve", "pool"):
                    if cnt[q] > 0:
                        wait(sems[q], cnt[q])

        with nc.Block() as block:
            @block.tensor
            def _(e):
                run("pe", e)

            @block.scalar
            def _(e):
                run("act", e)

            @block.vector
            def _(e):
                run("dve", e)

            @block.gpsimd
            def _(e):
                run("pool", e)

            @block.sync
            def _(e):
                run("sp", e)


class Arena:
    def __init__(self, ap, nbytes):
        self.ap = ap
        self.cap = nbytes
        self.off = 0
        self.peak = 0

    def alloc(self, nbytes, dtype=BF16):
        off = (self.off + 63) // 64 * 64
        assert off + nbytes <= self.cap, f"arena overflow: {off}+{nbytes} > {self.cap}"
        self.off = off + nbytes
        self.peak = max(self.peak, self.off)
        v = self.ap[:, off // 2:(off + nbytes) // 2]
        if dtype == F32:
            v = v.bitcast(F32)
        return v


def _rope_tables():
    tab = np.zeros((NKT, 128, 384), np.float32)
    tab[:2, :, 0:128] = 1.0
    tab[:2, :, 256:320] = 1.0
    n = np.arange(S)
    rows = (n // 64).astype(np.float32)
    cols = (n % 64).astype(np.float32)

    def cs(dim):
        q = dim // 4
        inv = (np.float32(10000.0) ** (-(np.arange(q, dtype=np.float32) / np.float32(q)))).astype(np.float32)
        ang = np.stack([rows[:, None] * inv, cols[:, None] * inv], axis=1).astype(np.float32)
        c = np.cos(ang).astype(np.float32)
        s = np.sin(ang).astype(np.float32)
        ce = np.broadcast_to(c[:, :, None, :], (S, 2, 2, q)).reshape(S, dim)
        se = np.broadcast_to(s[:, :, None, :], (S, 2, 2, q)).reshape(S, dim)
        return ce, se

    ca, sa = cs(128)
    cb, sb = cs(64)
    full = np.concatenate([ca, sa, cb, sb], axis=1).reshape(32, 128, 384)
    tab[2:] = full
    return tab


def _fourier_tables():
    nh = np.arange(128)[:, None].astype(np.float64)
    kl = np.arange(128)[None, :].astype(np.float64)
    ang = 2 * np.pi * nh * kl / 128.0
    norm = 1.0 / math.sqrt(4096.0 * 256.0)
    f1 = np.zeros((128, 128, 2))
    f1[:, :, 0] = np.cos(ang) * norm
    f1[:, :, 1] = -np.sin(ang) * norm
    f1 = f1.reshape(128, 256)
    j = (np.arange(2)[None, :, None] * 128 + np.arange(128)[:, None, None]).astype(np.float64)
    m = np.arange(256)[None, None, :].astype(np.float64)
    a2 = 2 * np.pi * j * m / 256.0
    cs = np.concatenate([np.cos(a2), np.sin(a2)], axis=2).reshape(128, 1024)
    par = np.arange(2)[:, None, None, None, None, None]
    nlo = np.arange(32)[None, :, None, None, None, None].astype(np.float64)
    ri = np.arange(2)[None, None, :, None, None, None]
    kp = np.arange(64)[None, None, None, :, None, None]
    wh = np.arange(2)[None, None, None, None, :, None]
    khi = np.arange(32)[None, None, None, None, None, :]
    k = (2 * kp + par) + 128 * khi
    ang3 = 2 * np.pi * nlo * k / 4096.0
    tr = np.cos(ang3)
    ti = -np.sin(ang3)
    shape = (2, 32, 2, 64, 2, 32)
    tr = np.broadcast_to(tr, shape)
    ti = np.broadcast_to(ti, shape)
    rib = np.broadcast_to(ri, shape)
    whb = np.broadcast_to(wh, shape)
    t = np.where(whb == 0, np.where(rib == 0, tr, -ti), np.where(rib == 0, ti, tr))
    tt = t.reshape(128, 64 * 2 * 32)
    ttz = np.zeros((128, 2, 64 * 2 * 32))
    ttz[0:64, 0, :] = tt[0:64]
    ttz[64:128, 1, :] = tt[64:128]
    return f1.astype(NPBF), cs.astype(NPBF), ttz.reshape(128, 8192).astype(NPBF)


_CONSTS = {}


def _consts():
    if _CONSTS:
        return _CONSTS
    cf32 = np.zeros((128, 388), np.float32)
    cf32[:, 260:388] = 1.0
    cf32[0, 0:128] = 1.0
    cf32[32, 128:256] = 1.0
    cf32[0, 256] = 1.0
    cf32[32, 258] = 1.0
    cbf = np.zeros((128, 256), np.float32)
    cbf[:, 0:128] = np.eye(128, dtype=np.float32)
    cbf[:, 128:256] = 1.0
    f1, cs, tt = _fourier_tables()
    _CONSTS.update(cf32=cf32, cbf=cbf.astype(NPBF), rope=_rope_tables(), f1=f1, cs=cs, tt=tt)
    return _CONSTS


def build_program(stage="full"):
    nc = bass.Bass("TRN2", target_bir_lowering=False)
    P = Prog()

    def din(name, shape, dt=F32):
        return nc.dram_tensor(name, list(shape), dt, kind="ExternalInput").ap()

    def dint(name, shape, dt, ext=False):
        return nc.dram_tensor(name, list(shape), dt,
                              kind=("ExternalOutput" if ext else "Internal")).ap()

    x_d = din("x", [S, D])
    ctx_d = din("ctx", [CTX, D])
    cT_d = din("cT", [128, 16])
    adaw_d = din("ada_w", [2, D, 3 * D])
    adab_d = din("ada_b", [2, 3 * D])
    ngT_d = din("ngT", [128, 16])
    win_d = din("win", [D, 3584])
    wout_d = din("wout", [D, D])
    fin_d = din("fin", [D, 2 * D])
    fout_d = din("fout", [D, D])
    gbc_d = din("gbc", [128, 1280])
    lam_d = din("lam", [1, 256])
    sgT_d = din("sgT", [128, 1])
    cf32_d = din("cf32", [128, 388])
    cbf_d = din("cbf", [128, 256], BF16)
    rope_d = din("rope", [NKT, 128, 384])
    f1_d = din("f1", [128, 256], BF16)
    cs_d = din("cs", [128, 1024], BF16)
    tt_d = din("tt", [128, 8192], BF16)
    out_d = nc.dram_tensor("out", [S, D], F32, kind="ExternalOutput").ap()
    og_d = dint("ogd", [D, S], BF16, ext=(stage in ("L0", "L0s")))
    x1_d = dint("x1d", [S, D], F32)
    u_d = dint("ud", [S, D], BF16)
    g_d = dint("gd", [D, S], BF16)
    dbg_d = dint("dbg", [128, 2048], F32, ext=True) if stage[0] in "PS" else None

    es = contextlib.ExitStack()
    with es:
        arena_t = es.enter_context(nc.sbuf_tensor("arena", [128, ARENA_BYTES // 2], BF16))
        ps = es.enter_context(nc.psum_tensor("ps", [128, 8, 512], F32))
        sems = {e: es.enter_context(nc.semaphore("s_" + e)) for e in ("pe", "act", "dve", "pool")}
        dsems = {e: [] for e in Prog.ENGS}
        dsems["sp"] = [es.enter_context(nc.semaphore(f"d_sp{i}")) for i in range(12)]
        dsems["pool"] = [es.enter_context(nc.semaphore(f"d_pl{i}")) for i in range(6)]
        AR = Arena(arena_t[:, :], ARENA_BYTES)

        def bank(i):
            return ps[:, i, :]

        def bankbf(i):
            return ps[:, i, :].bitcast(BF16)

        def pk(i):
            return f"ps{i}"

        CBF = AR.alloc(512)
        ident = CBF[:, 0:128]
        onesb = CBF[:, 128:256]
        CF = AR.alloc(388 * 4, F32)
        sel0 = CF[:, 0:128]
        sel32 = CF[:, 128:256]
        e0 = CF[:, 256:258]
        e32 = CF[:, 258:260]
        onesf = CF[:, 260:388]
        GB = AR.alloc(256 * 4, F32)
        qn_bc = GB[:, 0:128]
        kn_bc = GB[:, 128:256]
        SM = AR.alloc(128 * 4, F32)
        CT = SM[:, 0:16]
        NG = SM[:, 16:32]
        MODS = SM[:, 32:64]
        neghalf = SM[:, 64:65]
        sgcol = SM[:, 65:66]
        gcol = SM[:, 66:67]
        neglam = SM[:, 67:68]
        SSX = SM[:, 68:72]
        MSX = SM[:, 72:76]
        RSX = SM[:, 76:80]
        SSH = SM[:, 80:84]
        MSH = SM[:, 84:88]
        RSH = SM[:, 88:92]
        LR = SM[:, 92:94]
        LT = SM[0:1, 96:128]
        junk = AR.alloc(256)
        mark_persist = AR.off

        XT = [AR.alloc(4096, F32) for _ in range(2)]
        XN = [AR.alloc(2048) for _ in range(2)]
        ROPE = [AR.alloc(384 * 4, F32) for _ in range(2)]
        HT = AR.alloc(8192)
        HT3 = HT.rearrange("p (k t) -> p k t", k=8)
        tmp_off = (AR.off + 63) // 64 * 64
        TMP = [AR.alloc(2048, F32) for _ in range(4)]
        W1 = AR.alloc(32768)
        W1v = W1.rearrange("p (k c) -> p k c", k=8)
        mark_generic = AR.off

        def dma(out, in_, r, w, q="sp"):
            return P.add(q, lambda e: e.dma_start(out=out, in_=in_), r, w, dma=True)

        def act(out, in_, func, r, w, bias=0.0, scale=1.0, accum_out=None):
            if accum_out is None:
                return P.add("act", lambda e: e.activation(out=out, in_=in_, func=func,
                                                           bias=bias, scale=scale), r, w)
            return P.add("act", lambda e: e.activation(out=out, in_=in_, func=func, bias=bias,
                                                       scale=scale, accum_out=accum_out), r, w)

        def tt(eng, out, in0, in1, op, r, w):
            return P.add(eng, lambda e: e.tensor_tensor(out=out, in0=in0, in1=in1, op=op), r, w)

        def ts(eng, out, in0, s1, s2, op0, op1, r, w):
            if s2 is None:
                return P.add(eng, lambda e: e.tensor_scalar(out=out, in0=in0, scalar1=s1,
                                                            scalar2=None, op0=op0), r, w)
            return P.add(eng, lambda e: e.tensor_scalar(out=out, in0=in0, scalar1=s1, scalar2=s2,
                                                        op0=op0, op1=op1), r, w)

        def stt(out, in0, scalar, in1, op0, op1, r, w):
            return P.add("dve", lambda e: e.scalar_tensor_tensor(out=out, in0=in0, scalar=scalar,
                                                                 in1=in1, op0=op0, op1=op1), r, w)

        def cp(eng, out, in_, r, w):
            if eng == "act":
                return P.add("act", lambda e: e.copy(out=out, in_=in_), r, w)
            return P.add(eng, lambda e: e.tensor_copy(out=out, in_=in_), r, w)

        def mm(out, lhsT, rhs, start, stop, r, w):
            return P.add("pe", lambda e: e.matmul(out, lhsT, rhs, start=start, stop=stop), r, w)

        def tr(out, in_, r, w):
            return P.add("pe", lambda e: e.transpose(out, in_, ident), r, w)

        def memset(eng, ap, v, w):
            return P.add(eng, lambda e: e.memset(ap, v), (), w)

        fe_cnt = [0]

        def fe1(src_ap, sb=None):
            n = fe_cnt[0]
            fe_cnt[0] += 1
            b = n % 2
            sl = n % 4
            xt, xn = XT[b], XN[b]
            kx, kn = f"xt{b}", f"xn{b}"
            if sb is None:
                dma(xt, src_ap, (), (kx,))
            else:
                xt, kx = sb
            act(xn, xt, AF.Square, (kx,), (kn, f"ssx{sl}"), accum_out=SSX[:, sl:sl + 1])
            ts("dve", MSX[:, sl:sl + 1], SSX[:, sl:sl + 1], 1.0 / D, EPS, ALU.mult, ALU.add,
               (f"ssx{sl}",), (f"msx{sl}",))
            tt("pool", RSX[:, sl:sl + 1], MSX[:, sl:sl + 1], neghalf, ALU.pow,
               (f"msx{sl}", "consts"), (f"rsx{sl}",))
            act(xn, xt, AF.Identity, (kx, f"rsx{sl}"), (kn,), scale=RSX[:, sl:sl + 1])
            return b

        def fe2(b, hslot, moff, tbank):
            xn, kn = XN[b], f"xn{b}"
            tb = bankbf(tbank)
            for c in range(8):
                tr(tb[:, c * 128:(c + 1) * 128], xn[:, c * 128:(c + 1) * 128], (kn, "consts"),
                   (pk(tbank),))
            for c in range(8):
                ts("dve", HT3[:, c, hslot * 128:(hslot + 1) * 128], tb[:, c * 128:(c + 1) * 128],
                   MODS[:, moff + 8 + c:moff + 9 + c], MODS[:, moff + c:moff + c + 1],
                   ALU.mult, ALU.add, (pk(tbank), "mods"), (f"hT{hslot}_{c}",))

        def hkeys(slots):
            return tuple(f"hT{j}_{c}" for j in slots for c in range(8))

        STG = [(XT[0], ("xt0",)), (XT[1], ("xt1",)),
               (AR.ap[:, tmp_off // 2:(tmp_off + 4096) // 2].bitcast(F32), ("tmp0", "tmp1")),
               (AR.ap[:, (tmp_off + 4096) // 2:(tmp_off + 8192) // 2].bitcast(F32), ("tmp2", "tmp3"))]
        stg_n = [0]

        def stage():
            i = stg_n[0]
            stg_n[0] += 1
            return STG[i % len(STG)]

        def ada_third(l, t, SCv, MR, ADB):
            dma(ADB[0:1, :], adab_d[l:l + 1, t * 1024:(t + 1) * 1024], (), ("adb0",))
            dma(ADB[32:33, :], adab_d[l:l + 1, t * 1024:(t + 1) * 1024], (), ("adb32",))
            for k in range(8):
                sb_, sk_ = stage()
                dma(sb_, adaw_d[l, k * 128:(k + 1) * 128, t * 1024:(t + 1) * 1024], (), sk_)
                for hf in range(2):
                    mm(bank(hf), SCv[:, k, :], sb_[:, hf * 512:(hf + 1) * 512], k == 0, k == 7,
                       sk_ + ("sc",), (pk(hf),))
            for hf in range(2):
                tt("dve", MR[:, hf * 512:(hf + 1) * 512], bank(hf), ADB[:, hf * 512:(hf + 1) * 512],
                   ALU.add, (pk(hf), "adb0", "adb32"), ("mr",))

        def cols_from_rows(MR, which, with_ctx):
            pc = bank(2)
            for c in range(8):
                i0 = ((0 * 2 + which) * 8 + c) * 2
                mm(pc[:, i0:i0 + 2], MR[:, c * 128:(c + 1) * 128], e0, True, True,
                   ("mr", "c_cf"), (pk(2),))
                if with_ctx:
                    i1 = ((1 * 2 + which) * 8 + c) * 2
                    mm(pc[:, i1:i1 + 2], MR[:, c * 128:(c + 1) * 128], e32, True, True,
                       ("mr", "c_cf"), (pk(2),))

        def mods_finish(ngoff, with_ctx):
            pc = bank(2).rearrange("p (n two) -> p n two", two=2)
            for src in range(2 if with_ctx else 1):
                o = src * 16
                cp("dve", MODS[:, o:o + 8], pc[:, src * 16:src * 16 + 8, 0], (pk(2),), ("mods",))
                ts("dve", MODS[:, o + 8:o + 16], pc[:, src * 16 + 8:src * 16 + 16, 0], 1.0, None,
                   ALU.add, None, (pk(2),), ("mods",))
                tt("dve", MODS[:, o + 8:o + 16], MODS[:, o + 8:o + 16], NG[:, ngoff:ngoff + 8],
                   ALU.mult, ("mods", "c_ng"), ("mods",))

        def load_cast_weights(src_d, c0, ncols, dst3, dcol0, keyw):
            i = 0
            for k in range(8):
                for cc in range(0, ncols, 1024):
                    w = min(1024, ncols - cc)
                    sb_, sk_ = stage()
                    dma(sb_[:, 0:w], src_d[k * 128:(k + 1) * 128, c0 + cc:c0 + cc + w], (), sk_)
                    eng = ("act", "dve")[i % 2]
                    cp(eng, dst3[:, k, dcol0 + cc:dcol0 + cc + w], sb_[:, 0:w], sk_, (keyw,))
                    i += 1

        def finish_dbg():
            P.barrier()
            dma(dbg_d[:, 0:32], MODS, ("mods",), ())
            cp("pool", TMP[0][:, 0:512], KTA[:, 0, 0:512], (), ("tmp0",))
            dma(dbg_d[:, 512:1024], TMP[0][:, 0:512], ("tmp0",), ())
            cp("pool", TMP[1][:, 0:512], KTB[:, 1, 256:768], (), ("tmp1",))
            dma(dbg_d[:, 1024:1536], TMP[1][:, 0:512], ("tmp1",), ())
            cp("pool", TMP[2][:, 0:256], VA[:, 3, :], (), ("tmp2",))
            cp("pool", TMP[2][:, 256:512], VB[:, 3, 0:256], (), ("tmp2",))
            dma(dbg_d[:, 1536:2048], TMP[2][:, 0:512], ("tmp2",), ())
            memset("pool", TMP[3][:, 0:32], 0.0, ("tmp3",))
            cp("pool", TMP[3][:, 0:1], neglam, (), ("tmp3",))
            cp("pool", TMP[3][:, 1:2], gcol, (), ("tmp3",))
            dma(dbg_d[:, 32:64], TMP[3][:, 0:32], ("tmp3",), ())
            P.emit(nc, sems, dsems)
            return nc, P

        dma(CBF, cbf_d, (), ("c_cbf",))
        dma(CF, cf32_d, (), ("c_cf",))
        dma(GB, gbc_d[:, 0:256], (), ("c_gb",))
        dma(CT, cT_d, (), ("c_ct",))
        dma(NG, ngT_d, (), ("c_ng",))
        dma(sgcol, sgT_d, (), ("c_sg",))
        LAMT = ROPE[0][:, 0:256]
        dma(LAMT[0:1, :], lam_d, (), ("rope0",))
        memset("pool", neghalf, -0.5, ("c_nh",))
        memset("pool", LR, 0.0, ("lr",))
        if stage == "S0":
            cp("pool", MODS, GB[:, 0:32], ("c_gb",), ("mods",))
            P.emit(nc, sems, dsems)
            return nc, P

        KTA = AR.alloc(2 * 4352 * 2).rearrange("p (h n) -> p h n", h=2)
        KTB = AR.alloc(4 * 4352 * 2).rearrange("p (h n) -> p h n", h=4)
        VA = AR.alloc(NKT * 256 * 2).rearrange("p (t c) -> p t c", t=NKT)
        VB = AR.alloc(NKT * 512 * 2).rearrange("p (t c) -> p t c", t=NKT)
        ROT = AR.alloc(4096)
        QTA = AR.alloc(4096).rearrange("p (h t) -> p h t", h=4)
        QTB = AR.alloc(8192).rearrange("p (h i t) -> p h i t", h=4, i=2)
        SG = AR.alloc(8192).rearrange("p (c t) -> p c t", c=8)
        NPT = 4
        PT = [AR.alloc(2048) for _ in range(NPT)]
        GF = AR.alloc(1024, F32)
        SQ = AR.alloc(1024)
        PA = AR.alloc(1024)
        QS = [AR.alloc(1024) for _ in range(2)]
        l0_peak = AR.off
        SCf = QTB.rearrange("p h i t -> p (h i t)")[:, 0:2048].bitcast(F32)
        SCv = SCf.rearrange("p (k m) -> p k m", k=8)
        MR = SG.rearrange("p c t -> p (c t)")[:, 0:2048].bitcast(F32)
        ADB = SG.rearrange("p c t -> p (c t)")[:, 2048:4096].bitcast(F32)

        memset("pool", SCf, 0.0, ("sc",))
        memset("pool", ADB, 0.0, ("adb0", "adb32"))
        act(SCv[:, :, 0], CT[:, 0:8], AF.Silu, ("c_ct", "sc"), ("sc",))
        act(SCv[:, :, 32], CT[:, 8:16], AF.Silu, ("c_ct", "sc"), ("sc",))
        for t in range(2):
            ada_third(0, t, SCv, MR, ADB)
            cols_from_rows(MR, t, True)
        mods_finish(0, True)
        if stage == "S1":
            return finish_dbg()

        LV = LAMT[0:1, :].rearrange("p (a n) -> p a n", a=4)
        LP = LT[:, 0:2]
        tt("dve", LAMT[0:1, 0:64], LV[:, 0, :], LV[:, 1, :], ALU.mult, ("rope0",), ("rope0",))
        tt("dve", LAMT[0:1, 128:192], LV[:, 2, :], LV[:, 3, :], ALU.mult, ("rope0",), ("rope0",))
        P.add("dve", lambda e: e.reduce_sum(out=LP[:, 0:1], in_=LAMT[0:1, 0:64], axis=AX.X),
              ("rope0",), ("lp",))
        P.add("dve", lambda e: e.reduce_sum(out=LP[:, 1:2], in_=LAMT[0:1, 128:192], axis=AX.X),
              ("rope0",), ("lp",))
        act(LP, LP, AF.Exp, ("lp",), ("lp",))
        tt("dve", LR[0:1, 0:1], LP[:, 1:2], LP[:, 0:1], ALU.subtract, ("lp", "lr"), ("lr",))
        ts("dve", LR[0:1, 0:1], LR[0:1, 0:1], -LAM_INIT0, None, ALU.add, None, ("lr",), ("lr",))
        mm(bank(3)[:, 0:2], sel0, LR, True, True, ("lr", "c_cf"), (pk(3),))
        cp("dve", neglam, bank(3)[:, 0:1], (pk(3),), ("consts2",))
        ts("dve", gcol, sgcol, 1.0 - LAM_INIT0, None, ALU.mult, None, ("c_sg",), ("consts2",))

        if stage == "S2":
            return finish_dbg()
        load_cast_weights(win_d, 0, 1536, W1v, 0, "w1")
        if stage == "S3":
            return finish_dbg()

        P.barrier()
        memset("pool", QTB.rearrange("p h i t -> p (h i t)"), 0.0, ("qtb0", "qtb1"))

        def kv_post(kt, s, rb):
            b1, b2, b3 = 4 * s + 1, 4 * s + 2, 4 * s + 3
            rope = ROPE[rb]
            rk = f"rope{rb}"
            rot = ROT[:, s * 768:(s + 1) * 768]
            rkey = f"rotk{s}"
            cp("act", VA[:, kt, :], bank(b1)[:, 256:512], (pk(b1),), (f"va{kt}",))
            cp("act", VB[:, kt, :], bank(b3), (pk(b3),), (f"vb{kt}",))
            for h in range(2):
                act(junk, bank(b1)[:, h * 128:(h + 1) * 128], AF.Square, (pk(b1),), (f"ssh{h}", "junk"),
                    accum_out=SSH[:, h:h + 1])
            ts("dve", MSH[:, 0:2], SSH[:, 0:2], 1.0 / 128, EPS, ALU.mult, ALU.add,
               ("ssh0", "ssh1"), ("msh",))
            tt("pool", RSH[:, 0:2], MSH[:, 0:2], neghalf.broadcast_to([128, 2]), ALU.pow,
               ("msh", "consts"), ("rsh",))
            tt("pool", GF[:, 0:128], rope[:, 0:128], kn_bc, ALU.mult, (rk, "consts"), ("gf",))
            tt("pool", GF[:, 128:256], rope[:, 128:256], kn_bc, ALU.mult, (rk, "consts"), ("gf",))
            for h in range(2):
                stt(TMP[0][:, h * 128:(h + 1) * 128], bank(b1)[:, h * 128:(h + 1) * 128],
                    RSH[:, h:h + 1], GF[:, 0:128], ALU.mult, ALU.mult, (pk(b1), "rsh", "gf"), ("tmp0",))
                stt(TMP[1][:, h * 128:(h + 1) * 128], bank(b1)[:, h * 128:(h + 1) * 128],
                    RSH[:, h:h + 1], GF[:, 128:256], ALU.mult, ALU.mult, (pk(b1), "rsh", "gf"), ("tmp1",))
            t1 = TMP[0][:, 0:256].rearrange("p (g two f) -> p g two f", two=2, f=32)
            t2 = TMP[1][:, 0:256].rearrange("p (g two f) -> p g two f", two=2, f=32)
            ro = rot[:, 0:256].rearrange("p (g two f) -> p g two f", two=2, f=32)
            tt("pool", ro[:, :, 0, :], t1[:, :, 0, :], t2[:, :, 1, :], ALU.subtract,
               ("tmp0", "tmp1"), (rkey,))
            tt("pool", ro[:, :, 1, :], t2[:, :, 0, :], t1[:, :, 1, :], ALU.add,
               ("tmp0", "tmp1"), (rkey,))
            xb = bank(b2).rearrange("p (g d) -> p g d", g=8)
            cbb = rope[:, 256:320].unsqueeze(1).broadcast_to([128, 8, 64])
            sbb = rope[:, 320:384].unsqueeze(1).broadcast_to([128, 8, 64])
            tt("dve", TMP[2].rearrange("p (g d) -> p g d", g=8), xb, cbb, ALU.mult, (pk(b2), rk), ("tmp2",))
            tt("dve", TMP[3].rearrange("p (g d) -> p g d", g=8), xb, sbb, ALU.mult, (pk(b2), rk), ("tmp3",))
            t1 = TMP[2].rearrange("p (g two f) -> p g two f", two=2, f=16)
            t2 = TMP[3].rearrange("p (g two f) -> p g two f", two=2, f=16)
            ro = rot[:, 256:768].rearrange("p (g two f) -> p g two f", two=2, f=16)
            tt("pool", ro[:, :, 0, :], t1[:, :, 0, :], t2[:, :, 1, :], ALU.subtract,
               ("tmp2", "tmp3"), (rkey,))
            tt("pool", ro[:, :, 1, :], t2[:, :, 0, :], t1[:, :, 1, :], ALU.add,
               ("tmp2", "tmp3"), (rkey,))

        def kv_trans(kt, s):
            b0 = 4 * s + KTBANK
            rot = ROT[:, s * 768:(s + 1) * 768]
            tb = bankbf(b0)
            for j in range(6):
                tr(tb[:, j * 128:(j + 1) * 128], rot[:, j * 128:(j + 1) * 128], (f"rotk{s}", "consts"),
                   (pk(b0),))
            if DBG2 >= 1:
                cp("dve", KTA[:, :, kt * 128:(kt + 1) * 128],
                   tb[:, 0:256].rearrange("p (h t) -> p h t", h=2), (pk(b0),), (f"kta{kt}",))
            if DBG2 >= 2:
                cp("dve", KTB[:, :, kt * 128:(kt + 1) * 128],
                   tb[:, 256:768].rearrange("p (h t) -> p h t", h=4), (pk(b0),), (f"ktb{kt}_0", f"ktb{kt}_1"))

        NK1 = NKT if stage != "S4" else 3

        p1_buf = {}

        def p1_A1(kt):
            src = ctx_d[kt * 128:(kt + 1) * 128, :] if kt < 2 else x_d[(kt - 2) * 128:(kt - 1) * 128, :]
            p1_buf[kt] = fe1(src)

        def p1_rope(kt):
            dma(ROPE[kt % 2], rope_d[kt], (), (f"rope{kt % 2}",))

        def p1_A2(kt):
            s_ = kt % 2
            fe2(p1_buf[kt], s_, 16 if kt < 2 else 0, 4 * s_)

        def p1_B(kt):
            s_ = kt % 2
            for j, bnk in enumerate((4 * s_ + 1, 4 * s_ + 2, 4 * s_ + 3)):
                for k in range(8):
                    mm(bank(bnk), HT3[:, k, s_ * 128:(s_ + 1) * 128], W1v[:, k, j * 512:(j + 1) * 512],
                       k == 0, k == 7, (f"hT{s_}_{k}", "w1"), (pk(bnk),))
            kv_post(kt, s_, kt % 2)

        p1_A1(0)
        p1_A1(1)
        p1_rope(0)
        p1_A2(0)
        for kt in range(NK1):
            if kt + 2 < NK1:
                p1_A1(kt + 2)
            if kt + 1 < NK1:
                p1_rope(kt + 1)
            if kt + 1 < NK1:
                p1_A2(kt + 1)
            p1_B(kt)
            if kt >= 1:
                kv_trans(kt - 1, (kt - 1) % 2)
        kv_trans(NK1 - 1, (NK1 - 1) % 2)

        if stage in ("P1", "S4"):
            return finish_dbg()

        P.barrier()
        load_cast_weights(win_d, 1536, 2048, W1v, 0, "w1")
        P.barrier()
        OG3 = HT3
        QTBf = QTB

        def q_post(j, rb, ba, bb):
            rope = ROPE[rb]
            rk = f"rope{rb}"
            rot = ROT[:, (j % 2) * 1024:(j % 2 + 1) * 1024]
            rqk = f"rotq{j % 2}"
            for h in range(4):
                act(junk, bank(ba)[:, h * 128:(h + 1) * 128], AF.Square, (pk(ba),), (f"ssh{h}", "junk"),
                    accum_out=SSH[:, h:h + 1])
            ts("dve", MSH[:, 0:4], SSH[:, 0:4], 1.0 / 128, EPS, ALU.mult, ALU.add,
               ("ssh0", "ssh1", "ssh2", "ssh3"), ("msh",))
            tt("pool", RSH[:, 0:4], MSH[:, 0:4], neghalf.broadcast_to([128, 4]), ALU.pow,
               ("msh", "consts"), ("rsh",))
            tt("pool", GF[:, 0:128], rope[:, 0:128], qn_bc, ALU.mult, (rk, "consts"), ("gf",))
            tt("pool", GF[:, 128:256], rope[:, 128:256], qn_bc, ALU.mult, (rk, "consts"), ("gf",))
            for h in range(4):
                stt(TMP[0][:, h * 128:(h + 1) * 128], bank(ba)[:, h * 128:(h + 1) * 128],
                    RSH[:, h:h + 1], GF[:, 0:128], ALU.mult, ALU.mult, (pk(ba), "rsh", "gf"), ("tmp0",))
                stt(TMP[1][:, h * 128:(h + 1) * 128], bank(ba)[:, h * 128:(h + 1) * 128],
                    RSH[:, h:h + 1], GF[:, 128:256], ALU.mult, ALU.mult, (pk(ba), "rsh", "gf"), ("tmp1",))
            t1 = TMP[0].rearrange("p (g two f) -> p g two f", two=2, f=32)
            t2 = TMP[1].rearrange("p (g two f) -> p g two f", two=2, f=32)
            ro = rot[:, 0:512].rearrange("p (g two f) -> p g two f", two=2, f=32)
            tt("pool", ro[:, :, 0, :], t1[:, :, 0, :], t2[:, :, 1, :], ALU.subtract,
               ("tmp0", "tmp1"), (rqk,))
            tt("pool", ro[:, :, 1, :], t2[:, :, 0, :], t1[:, :, 1, :], ALU.add,
               ("tmp0", "tmp1"), (rqk,))
            xb = bank(bb).rearrange("p (g d) -> p g d", g=8)
            cbb = rope[:, 256:320].unsqueeze(1).broadcast_to([128, 8, 64])
            sbb = rope[:, 320:384].unsqueeze(1).broadcast_to([128, 8, 64])
            tt("dve", TMP[2].rearrange("p (g d) -> p g d", g=8), xb, cbb, ALU.mult, (pk(bb), rk), ("tmp2",))
            tt("dve", TMP[3].rearrange("p (g d) -> p g d", g=8), xb, sbb, ALU.mult, (pk(bb), rk), ("tmp3",))
            t1 = TMP[2].rearrange("p (g two f) -> p g two f", two=2, f=16)
            t2 = TMP[3].rearrange("p (g two f) -> p g two f", two=2, f=16)
            ro = rot[:, 512:1024].rearrange("p (g two f) -> p g two f", two=2, f=16)
            tt("pool", ro[:, :, 0, :], t1[:, :, 0, :], t2[:, :, 1, :], ALU.subtract,
               ("tmp2", "tmp3"), (rqk,))
            tt("pool", ro[:, :, 1, :], t2[:, :, 0, :], t1[:, :, 1, :], ALU.add,
               ("tmp2", "tmp3"), (rqk,))

        def q_trans(j, bt):
            rot = ROT[:, (j % 2) * 1024:(j % 2 + 1) * 1024]
            rqk = f"rotq{j % 2}"
            tb = bankbf(bt)
            for c in range(8):
                tr(tb[:, c * 128:(c + 1) * 128], rot[:, c * 128:(c + 1) * 128], (rqk, "consts"),
                   (pk(bt),))
            cp("dve", QTA[:, :, j * 128:(j + 1) * 128],
               tb[:, 0:512].rearrange("p (h t) -> p h t", h=4), (pk(bt),), ("qta",))
            tbb = tb[:, 512:1024].rearrange("p (h t) -> p h t", h=4)
            cp("dve", QTB[0:64, :, 0, j * 128:(j + 1) * 128], tbb[0:64], (pk(bt),), ("qtb0",))
            cp("dve", QTB[64:128, :, 1, j * 128:(j + 1) * 128], tbb[64:128], (pk(bt),), ("qtb1",))

        maps = [("B", h, i) for h in range(4) for i in range(2)] + [("A", h, 0) for h in range(4)]
        SCALE_A = 128.0 ** -0.5
        SCALE_B = 64.0 ** -0.5

        def attention_block(qb):
            seq = [(mi, pr) for mi in range(len(maps)) for pr in range(NKT // 2)]

            def qk(idx):
                mi, pr = seq[idx]
                kind, h, i = maps[mi]
                sb = 2 * (idx % 2)
                for t in range(2):
                    kt = 2 * pr + t
                    if kind == "A":
                        lhsT = KTA[:, h // 2, kt * 128:(kt + 1) * 128]
                        rhs = QTA[:, h, :]
                        r = (f"kta{kt}", "qta")
                    else:
                        lhsT = KTB[:, h, kt * 128:(kt + 1) * 128]
                        rhs = QTB[:, h, i, :]
                        r = (f"ktb{kt}_{h % 2}", f"qtb{i}")
                    mm(bank(sb + t), lhsT, rhs, True, True, r, (pk(sb + t),))

            def ex(idx):
                mi, pr = seq[idx]
                kind = maps[mi][0]
                sb = 2 * (idx % 2)
                pt = PT[idx % NPT]
                act(pt.rearrange("p (t n) -> p t n", t=2), ps[:, sb:sb + 2, :], AF.Exp,
                    (pk(sb), pk(sb + 1)), (f"pt{idx % NPT}",),
                    scale=(SCALE_A if kind == "A" else SCALE_B))

            def pv(idx):
                mi, pr = seq[idx]
                kind, h, i = maps[mi]
                ob = 4 + 2 * (mi % 2)
                pt = PT[idx % NPT]
                for t in range(2):
                    kt = 2 * pr + t
                    if kind == "A":
                        lhsT = VA[:, kt, (h // 2) * 128:(h // 2 + 1) * 128]
                        r = (f"va{kt}", f"pt{idx % NPT}")
                    else:
                        lhsT = VB[:, kt, h * 128:(h + 1) * 128]
                        r = (f"vb{kt}", f"pt{idx % NPT}")
                    mm(bank(ob), lhsT, pt[:, t * 512:(t + 1) * 512], kt == 0, kt == NKT - 1, r, (pk(ob),))

            def finish(mi):
                kind, h, i = maps[mi]
                ob = 4 + 2 * (mi % 2)
                T = TMP[mi % 2]
                tk = f"tmp{mi % 2}"
                P.add("dve", lambda e: e.reciprocal(out=T, in_=bank(ob + 1)), (pk(ob + 1),), (tk,))
                tt("dve", T, bank(ob), T, ALU.mult, (pk(ob), tk), (tk,))
                if kind == "A":
                    tt("pool", SG[:, h, :], T, SG[:, h, :], ALU.mult, (tk, f"sg{h}"), (f"sg{h}",))
                elif i == 1:
                    T0, T1 = TMP[0], TMP[1]
                    bssq = ob + 1

                    def f1():
                        stt(T0, T1, neglam, T0, ALU.mult, ALU.add, ("tmp0", "tmp1", "consts2"), ("tmp0",))
                        tt("pool", SQ, T0, T0, ALU.mult, ("tmp0",), ("sq",))

                    def f2():
                        mm(bank(bssq), onesb, SQ, True, True, ("sq", "consts"), (pk(bssq),))
                        ts("dve", T1, bank(bssq), 1.0 / 128, EPS, ALU.mult, ALU.add, (pk(bssq),), ("tmp1",))

                    def f3():
                        act(T1, T1, AF.Ln, ("tmp1",), ("tmp1",))
                        act(T1, T1, AF.Exp, ("tmp1",), ("tmp1",), scale=-0.5)

                    def f4():
                        stt(T0, T0, gcol, T1, ALU.mult, ALU.mult, ("tmp0", "tmp1", "consts2"), ("tmp0",))
                        tt("pool", SG[:, 4 + h, :], T0, SG[:, 4 + h, :], ALU.mult, ("tmp0", f"sg{4 + h}"),
                           (f"sg{4 + h}",))

                    return [f1, f2, f3, f4]
                return []

            def psum2(idx):
                mi, pr = seq[idx]
                pt = PT[idx % NPT]
                q = pr // 2
                if pr == NKT // 2 - 1:
                    tt("dve", QS[q % 2], pt[:, 0:512], pt[:, 512:1024], ALU.add, (f"pt{idx % NPT}",),
                       (f"qs{q % 2}",))
                elif pr % 2 == 0:
                    tt("dve", PA, pt[:, 0:512], pt[:, 512:1024], ALU.add, (f"pt{idx % NPT}",), ("pa",))
                else:
                    tt("dve", QS[q % 2], pt[:, 0:512], PA, ALU.add, (f"pt{idx % NPT}", "pa"), (f"qs{q % 2}",))
                    tt("dve", QS[q % 2], QS[q % 2], pt[:, 512:1024], ALU.add, (f"pt{idx % NPT}", f"qs{q % 2}"),
                       (f"qs{q % 2}",))

            def den(idx):
                mi, pr = seq[idx]
                ob = 4 + 2 * (mi % 2)
                q = pr // 2
                mm(bank(ob + 1), onesb, QS[q % 2], pr == 1, pr == NKT // 2 - 1,
                   (f"qs{q % 2}", "consts"), (pk(ob + 1),))

            n = len(seq)
            deferred = {}
            qk(0)
            qk(1)
            for idx in range(n):
                for f in deferred.pop(idx, ()):
                    f()
                ex(idx)
                psum2(idx)
                if idx + 2 < n:
                    qk(idx + 2)
                mi, pr = seq[idx]
                if pr >= 2 and pr % 2 == 0:
                    den(idx - 1)
                pv(idx)
                if pr == NKT // 2 - 1:
                    den(idx)
                    for k, f in enumerate(finish(mi)):
                        deferred.setdefault(idx + 2 + 2 * k, []).append(f)
            for k in sorted(deferred):
                for f in deferred[k]:
                    f()

        NQB = 8 if stage != "L0s" else 1
        fb2 = {}

        def A2a(j, qb):
            ti = qb * 4 + j
            fb2[(qb, j)] = fe1(x_d[ti * 128:(ti + 1) * 128, :])

        def R2(j, qb):
            ti = qb * 4 + j
            dma(ROPE[ti % 2], rope_d[2 + ti], (), (f"rope{ti % 2}",))

        for qb in range(NQB):
            def A2b(j, qb=qb):
                fe2(fb2[(qb, j)], j, 0, 4 + (j % 2))

            def B2(j, qb=qb):
                ti = qb * 4 + j
                ba, bb = 2 * (j % 2), 2 * (j % 2) + 1
                for jj, bnk in enumerate((ba, bb)):
                    for k in range(8):
                        mm(bank(bnk), HT3[:, k, j * 128:(j + 1) * 128], W1v[:, k, jj * 512:(jj + 1) * 512],
                           k == 0, k == 7, (f"hT{j}_{k}", "w1"), (pk(bnk),))
                q_post(j, ti % 2, ba, bb)

            def C2(j):
                q_trans(j, 6 + (j % 2))

            def G2(c0, c1):
                for c in range(c0, c1):
                    bnk = 4 + (c % 2)
                    for k in range(8):
                        mm(bank(bnk), W1v[:, k, 1024 + c * 128:1024 + (c + 1) * 128], HT3[:, k, :],
                           k == 0, k == 7, tuple(f"hT{j}_{k}" for j in range(4)) + ("w1",), (pk(bnk),))
                    act(SG[:, c, :], bank(bnk), AF.Silu, (pk(bnk),), (f"sg{c}",))

            if qb == 0:
                A2a(0, qb); R2(0, qb); A2a(1, qb); R2(1, qb)
            A2b(0); A2a(2, qb); A2b(1); B2(0); R2(2, qb); A2a(3, qb); A2b(2); B2(1); R2(3, qb)
            C2(0); A2b(3); B2(2); C2(1)
            G2(0, 4); B2(3); C2(2); G2(4, 8); C2(3)
            if qb + 1 < NQB:
                A2a(0, qb + 1); R2(0, qb + 1); A2a(1, qb + 1); R2(1, qb + 1)
            attention_block(qb)
            for c4 in range(4):
                dma(og_d.rearrange("(c p) n -> p c n", p=128)[:, 2 * c4:2 * c4 + 2, qb * 512:(qb + 1) * 512],
                    SG[:, 2 * c4:2 * c4 + 2, :], (f"sg{2 * c4}", f"sg{2 * c4 + 1}"), ("ogd",))

        if stage in ("L0", "L0s"):
            P.emit(nc, sems, dsems)
            return nc, P


        P.barrier()
        AR.off = mark_generic
        WO = AR.alloc(16384)
        WO3 = WO.rearrange("p (k c) -> p k c", k=8)
        FC = AR.alloc(2560)
        F1 = FC[:, 0:256]
        CS3 = FC[:, 256:1280].rearrange("p (c n) -> p c n", c=2)
        GRH = [TMP[0], TMP[1]]
        PB = [AR.alloc(1024) for _ in range(4)]
        GG = AR.alloc(16384).rearrange("p (c n) -> p c n", c=2)
        YY = AR.alloc(32768)
        FG = AR.alloc(65536).rearrange("p (c n) -> p c n", c=8)
        L1X = AR.alloc(4096, F32)
        STG[:] = [(XT[0], ("xt0",)), (XT[1], ("xt1",)), (L1X, ("l1x",))]
        YYf = YY
        SCf1 = YYf[:, 0:2048].bitcast(F32)
        SCv1 = SCf1.rearrange("p (k m) -> p k m", k=8)
        MR1 = YYf[:, 2048:4096].bitcast(F32)
        ADB1 = YYf[:, 4096:6144].bitcast(F32)
        OGB = YYf[:, 6144:10240].rearrange("p (c t) -> p c t", c=8)
        X1T = [YYf[:, 10240 + i * 2048:10240 + (i + 1) * 2048].bitcast(F32) for i in range(2)]
        UO = [YYf[:, 14336 + i * 1024:14336 + (i + 1) * 1024] for i in range(2)]
        FGf = FG.rearrange("p c n -> p (c n)")
        GO = FGf[:, 0:4096].rearrange("p (c t) -> p c t", c=8)
        gen_off = mark_persist
        TTZ = AR.ap[:, (mark_persist + 63) // 64 * 64 // 2:(mark_persist + 63) // 64 * 64 // 2 + 8192]
        TT5 = TTZ.rearrange("p (a k w h) -> p a k w h", a=2, k=64, w=2)

        memset("pool", SCf1, 0.0, ("sc",))
        memset("pool", ADB1, 0.0, ("adb0", "adb32"))
        act(SCv1[:, :, 0], CT[:, 0:8], AF.Silu, ("sc",), ("sc",))
        fg_bc = AR.ap[:, (tmp_off + 4096) // 2:(tmp_off + 8192) // 2].bitcast(F32)
        dma(fg_bc, gbc_d[:, 256:1280], (), ("fgbc",))
        dma(FC[:, 0:256], f1_d, (), ("fc1",))
        dma(FC[:, 256:1280], cs_d, (), ("fc2",))

        def gate_weights(l, src_d):
            ada_third(l, 2, SCv1, MR1, ADB1)
            for hf in range(2):
                mm(bank(4 + hf), sel0, MR1[:, hf * 512:(hf + 1) * 512], True, True, ("mr",), (pk(4 + hf),))

        def fold_gate_into(src_d, keyw):
            for k in range(8):
                sb_, sk_ = stage()
                dma(sb_, src_d[k * 128:(k + 1) * 128, :], (), sk_)
                for hf in range(2):
                    tt("dve", WO3[:, k, hf * 512:(hf + 1) * 512], sb_[:, hf * 512:(hf + 1) * 512],
                       bank(4 + hf), ALU.mult, sk_ + (pk(4 + hf),), (keyw,))

        for t in range(2):
            ada_third(1, t, SCv1, MR1, ADB1)
            cols_from_rows(MR1, t, False)
        mods_finish(8, False)
        ada_third(1, 2, SCv1, MR1, ADB1)
        for hf in range(2):
            cp("dve", GRH[hf], MR1[:, hf * 512:(hf + 1) * 512], ("mr",), ("gr",))
        gate_weights(0, wout_d)
        fold_gate_into(wout_d, "wo")
        load_cast_weights(fin_d, 0, 2048, W1v, 0, "w1")
        P.barrier()

        for qb in range(8):
            for c4 in range(4):
                dma(OGB[:, 2 * c4:2 * c4 + 2, :],
                    og_d.rearrange("(c p) n -> p c n", p=128)[:, 2 * c4:2 * c4 + 2, qb * 512:(qb + 1) * 512],
                    ("ogd",), (f"ogb{c4}",))

            fb1 = {}

            def F1b(j, fb1=fb1):
                fe2(fb1[j], j, 0, 2 + (j % 2))

            def O1(j, qb=qb, fb1=fb1):
                ti = qb * 4 + j
                xb = ti % 2
                dma(X1T[xb], x_d[ti * 128:(ti + 1) * 128, :], (), (f"x1t{xb}",))
                for hf in range(2):
                    for c in range(8):
                        mm(bank(hf), OGB[:, c, j * 128:(j + 1) * 128], WO3[:, c, hf * 512:(hf + 1) * 512],
                           c == 0, c == 7, (f"ogb{c // 2}", "wo"), (pk(hf),))
                for hf in range(2):
                    tt("dve", X1T[xb][:, hf * 512:(hf + 1) * 512], bank(hf), X1T[xb][:, hf * 512:(hf + 1) * 512],
                       ALU.add, (pk(hf), f"x1t{xb}"), (f"x1t{xb}",))
                dma(x1_d[ti * 128:(ti + 1) * 128, :], X1T[xb], (f"x1t{xb}",), ("x1d",), q="pool")
                fb1[j] = fe1(None, sb=(X1T[xb], f"x1t{xb}"))

            def B1(j, qb=qb):
                ti = qb * 4 + j
                xb = ti % 2
                for hf in range(2):
                    for k in range(8):
                        mm(bank(4 + hf), HT3[:, k, j * 128:(j + 1) * 128], W1v[:, k, hf * 512:(hf + 1) * 512],
                           k == 0, k == 7, (f"hT{j}_{k}", "w1"), (pk(4 + hf),))
                    cp("act" if hf == 0 else "dve", UO[xb][:, hf * 512:(hf + 1) * 512], bank(4 + hf),
                       (pk(4 + hf),), (f"uo{xb}_{hf}",))
                dma(u_d[ti * 128:(ti + 1) * 128, :], UO[xb], (f"uo{xb}_0", f"uo{xb}_1"), ("ud",), q="pool")

            def G1(c0, c1):
                for c in range(c0, c1):
                    bnk = 6 + (c % 2)
                    for k in range(8):
                        mm(bank(bnk), W1v[:, k, 1024 + c * 128:1024 + (c + 1) * 128], HT3[:, k, :],
                           k == 0, k == 7, tuple(f"hT{j}_{k}" for j in range(4)) + ("w1",), (pk(bnk),))
                    act(GO[:, c, :], bank(bnk), AF.Silu, (pk(bnk),), ("go",))

            O1(0); O1(1); F1b(0); O1(2); F1b(1); B1(0); O1(3); F1b(2); B1(1); F1b(3); B1(2)
            G1(0, 4); B1(3); G1(4, 8)
            for c4 in range(4):
                dma(g_d.rearrange("(c p) n -> p c n", p=128)[:, 2 * c4:2 * c4 + 2, qb * 512:(qb + 1) * 512],
                    GO[:, 2 * c4:2 * c4 + 2, :], ("go",), ("gd",))

        P.barrier()
        dma(TTZ, tt_d, (), ("ttz",))
        UG = [W1[:, i * 8192:(i + 1) * 8192].rearrange("p (l c) -> p l c", l=32) for i in range(2)]
        Y5 = YY.rearrange("p (c k l r) -> p c k l r", c=2, k=128, l=32)
        u_v = u_d.rearrange("(nh nl) c -> nh nl c", nl=32)
        g_v = g_d.rearrange("(c p) n -> p c n", p=128)
        ev = [0]
        for gr in range(4):
            ub = gr % 2
            for n8 in range(8):
                dma(UG[ub][:, 4 * n8:4 * n8 + 4, :], u_v[:, 4 * n8:4 * n8 + 4, gr * 256:(gr + 1) * 256],
                    ("ud",), (f"ug{ub}_{n8}",))
            dma(GG, g_v[:, 2 * gr:2 * gr + 2, :], ("gd",), ("gg",))
            for nl in range(32):
                bnk = nl % 2
                for cc in range(2):
                    mm(bank(bnk)[:, cc * 256:(cc + 1) * 256], UG[ub][:, nl, cc * 128:(cc + 1) * 128], F1,
                       True, True, (f"ug{ub}_{nl // 4}", "fc1"), (pk(bnk),))
                cp("act" if nl % 4 != 3 else "dve", Y5[:, :, :, nl, :],
                   bank(bnk).rearrange("p (c k r) -> p c k r", c=2, r=2), (pk(bnk),), ("yy",))
            def chdft(kp, gr=gr):
                pbk = (2, 3, 6)[kp % 3]
                pb = PB[kp % 4]
                for cc in range(2):
                    mm(bank(pbk), Y5[:, cc, 2 * kp:2 * kp + 2, :, :].rearrange("p a l r -> p (a l r)"),
                       CS3[:, cc, :], cc == 0, cc == 1, ("yy", "fc2"), (pk(pbk),))
                cp("act" if kp % 3 != 2 else "dve", pb, bank(pbk), (pk(pbk),), (f"pb{kp % 4}",))

            def stage2(kp, gr=gr):
                pb = PB[kp % 4]
                fb = 4 + ((kp // 4) % 2)
                for par in range(2):
                    for mc in range(2):
                        col = (((kp % 4) * 2 + par) * 2 + mc) * 32
                        mm(bank(fb)[:, col:col + 32], pb[:, mc * 128:(mc + 1) * 128], TT5[:, par, kp, 0, :],
                           True, False, (f"pb{kp % 4}", "ttz"), (pk(fb),))
                        mm(bank(fb)[:, col:col + 32], pb[:, 256 + mc * 128:256 + (mc + 1) * 128],
                           TT5[:, par, kp, 1, :], False, True, (f"pb{kp % 4}", "ttz"), (pk(fb),))
                if kp % 4 == 3:
                    k0 = 2 * (kp - 3)
                    fbv = bank(fb).rearrange("p (a m h) -> p a m h", m=2, h=32)
                    for mc in range(2):
                        gv = GG[:, mc, :].rearrange("p (h l) -> p l h", l=128)[:, k0:k0 + 8, :]
                        ov = FG[:, 2 * gr + mc, :].rearrange("p (h l) -> p l h", l=128)[:, k0:k0 + 8, :]
                        tt("dve", ov, fbv[:, :, mc, :], gv, ALU.mult, (pk(fb), "gg"), ("fg",))

            chdft(0)
            chdft(1)
            for kp in range(64):
                if kp + 2 < 64:
                    chdft(kp + 2)
                stage2(kp)

        P.barrier()
        for hf in range(2):
            mm(bank(4 + hf), sel0, GRH[hf], True, True, ("gr",), (pk(4 + hf),))
        fold_gate_into(fout_d, "wo")
        P.barrier()
        ZT = [YY[:, i * 2048:(i + 1) * 2048].bitcast(F32) for i in range(4)]

        def c_X(ti):
            zb = ti % 4
            bo = 2 * (ti % 2)
            dma(ZT[zb], x1_d[ti * 128:(ti + 1) * 128, :], ("x1d",), (f"zt{zb}",))
            for hf in range(2):
                for c in range(8):
                    mm(bank(bo + hf), FG[:, c, ti * 128:(ti + 1) * 128], WO3[:, c, hf * 512:(hf + 1) * 512],
                       c == 0, c == 7, ("fg", "wo"), (pk(bo + hf),))
            for hf in range(2):
                tt("dve", ZT[zb][:, hf * 512:(hf + 1) * 512], bank(bo + hf), ZT[zb][:, hf * 512:(hf + 1) * 512],
                   ALU.add, (pk(bo + hf), f"zt{zb}"), (f"zt{zb}",))

        def c_Ya(ti):
            zb = ti % 4
            sl = ti % 4
            act(XN[ti % 2], ZT[zb], AF.Square, (f"zt{zb}",), (f"xn{ti % 2}", f"ssx{sl}"), accum_out=SSX[:, sl:sl + 1])
            ts("dve", MSX[:, sl:sl + 1], SSX[:, sl:sl + 1], 1.0 / D, EPS, ALU.mult, ALU.add,
               (f"ssx{sl}",), (f"msx{sl}",))
            tt("pool", RSX[:, sl:sl + 1], MSX[:, sl:sl + 1], neghalf, ALU.pow, (f"msx{sl}",), (f"rsx{sl}",))

        def c_Yb(ti):
            zb = ti % 4
            sl = ti % 4
            stt(ZT[zb], ZT[zb], RSX[:, sl:sl + 1], fg_bc, ALU.mult, ALU.mult, (f"zt{zb}", f"rsx{sl}", "fgbc"), (f"zt{zb}",))
            dma(out_d[ti * 128:(ti + 1) * 128, :], ZT[zb], (f"zt{zb}",), ("outd",), q="pool")

        c_X(0)
        for ti in range(32):
            if ti + 1 < 32:
                c_X(ti + 1)
            c_Ya(ti)
            if ti >= 1:
                c_Yb(ti - 1)
        c_Yb(31)

        P.emit(nc, sems, dsems)
        return nc, P


def _in_maps(inp):
    C = _consts()
    f = lambda a: np.ascontiguousarray(np.asarray(a, dtype=np.float32))
    x = f(inp["x"]); c = f(inp["c"]); ctx = f(inp["ctx"]); c_ctx = f(inp["c_ctx"])
    norm_g = f(inp["norm_g"])
    ngT = np.concatenate([norm_g[0].reshape(8, 128).T, norm_g[1].reshape(8, 128).T], axis=1)
    gbc = np.concatenate([np.broadcast_to(f(inp["attn_qn_g"])[0][None, :], (128, 128)),
                          np.broadcast_to(f(inp["attn_kn_g"])[0][None, :], (128, 128)),
                          np.broadcast_to(f(inp["final_g"])[None, :], (128, 1024))], axis=1)
    lam = np.concatenate([f(inp["lam_q1"])[0], f(inp["lam_k1"])[0], f(inp["lam_q2"])[0],
                          f(inp["lam_k2"])[0]])[None, :]
    shared = dict(
        ada_w=f(inp["ada_w"]), ada_b=f(inp["ada_b"]), ngT=np.ascontiguousarray(ngT),
        win=f(inp["attn_in_w"])[0], wout=f(inp["attn_out_w"])[0], fin=f(inp["fourier_in_w"])[0],
        fout=f(inp["fourier_out_w"])[0], gbc=np.ascontiguousarray(gbc), lam=np.ascontiguousarray(lam),
        sgT=np.ascontiguousarray(f(inp["attn_subln_g"])[0][:, None]),
        cf32=C["cf32"], cbf=C["cbf"], rope=C["rope"], f1=C["f1"], cs=C["cs"], tt=C["tt"])
    maps = []
    for b in range(N_CORES):
        cT = np.concatenate([c[b].reshape(8, 128).T, c_ctx.reshape(8, 128).T], axis=1)
        m = dict(shared)
        m.update(x=x[b], ctx=ctx[b], cT=np.ascontiguousarray(cT))
        maps.append(m)
    return maps


_PROG = {}


def kernel(**inputs):
    if "full" not in _PROG:
        _PROG["full"] = build_program("full")[0]
    nc = _PROG["full"]
    res = run_bass_kernel_spmd(nc, _in_maps(inputs), core_ids=list(range(N_CORES)))
    out = np.stack([np.asarray(r["out"], dtype=np.float32) for r in res.results], axis=0)
    return out
```

```python
import math
import contextlib
import numpy as np
import ml_dtypes
import concourse.bass as bass
import concourse.mybir as mybir
from concourse.bass_utils import run_bass_kernel_spmd

F32 = mybir.dt.float32
BF16 = mybir.dt.bfloat16
AF = mybir.ActivationFunctionType
ALU = mybir.AluOpType
AX = mybir.AxisListType
NPBF = ml_dtypes.bfloat16

S = 4096
D = 1024
CTX = 256
NKT = 34
EPS = 1e-6
LAM_INIT0 = 0.8 - 0.6 * math.exp(-0.3 * 0)
N_CORES = 8
ARENA_BYTES = 212736
import os
DBGL = int(os.environ.get('KDBG', '9'))
DBG2 = int(os.environ.get('KDBG2', '2'))
KTBANK = int(os.environ.get('KTBANK', '0'))


class Prog:
    ENGS = ("pe", "act", "dve", "pool", "sp")

    def __init__(self):
        self.ops = []
        self.lw = {}
        self.rd = {}
        self.bar_deps = set()
        self.bar_done = set(self.ENGS)
        self.last_on = {}
        self.dma_since = []

    def add(self, eng, fn, r=(), w=(), dma=False):
        i = len(self.ops)
        deps = {}
        for k in r:
            j = self.lw.get(k)
            if j is not None:
                deps[j] = True
        for k in w:
            j = self.lw.get(k)
            if j is not None:
                deps.setdefault(j, False)
            for j in self.rd.get(k, ()):
                deps.setdefault(j, False)
        if eng not in self.bar_done:
            for j in self.bar_deps:
                deps.setdefault(j, True)
            self.bar_done.add(eng)
        self.ops.append([eng, fn, deps, dma])
        for k in r:
            lst = self.rd.setdefault(k, [])
            if not dma:
                lst[:] = [j for j in lst if self.ops[j][3] or self.ops[j][0] != eng]
            lst.append(i)
        for k in w:
            self.lw[k] = i
            self.rd[k] = []
        self.last_on[eng] = i
        if dma:
            self.dma_since.append(i)
        return i

    def barrier(self):
        deps = set(self.last_on.values()) | set(self.dma_since)
        if len(self.bar_done) < len(self.ENGS):
            deps |= self.bar_deps
        self.bar_deps = deps
        self.bar_done = set()
        self.dma_since = []

    def emit(self, nc, sems, dsems, final_wait_all=True):
        ops = self.ops
        ms = set()
        for i, op in enumerate(ops):
            eng, fn, deps, dma = op
            nd = []
            for j, raw in deps.items():
                ej, _, _, dj = ops[j]
                if (not dj) and ej == eng:
                    if eng == "pe":
                        continue
                nd.append(j)
            op[2] = nd
            for j in nd:
                ms.add(j)
        val = {}
        prev = {}
        cnt = {e: 0 for e in self.ENGS}
        dcnt = {e: 0 for e in self.ENGS}
        for i, (eng, fn, deps, dma) in enumerate(ops):
            if dma:
                n = dcnt[eng]
                K = len(dsems[eng])
                sem = dsems[eng][n % K]
                val[i] = (sem, 16 * (n // K + 1))
                if n >= K:
                    prev[i] = (sem, 16 * (n // K))
                dcnt[eng] = n + 1
            elif i in ms:
                cnt[eng] += 1
                val[i] = (sems[eng], cnt[eng])
        self.stats = dict(n_ops=len(ops), milestones=dict(cnt), dmas=dict(dcnt))

        def run(eng, e):
            waited = {}

            def wait(sem, v):
                if waited.get(id(sem), 0) < v:
                    e.wait_ge(sem, v)
                    waited[id(sem)] = v

            for i, (en, fn, deps, dma) in enumerate(ops):
                if en != eng:
                    continue
                for j in deps:
                    wait(*val[j])
                if i in prev:
                    wait(*prev[i])
                ins = fn(e)
                if i in val:
                    sem, v = val[i]
                    ins.then_inc(sem, 16 if dma else 1)
            if eng == "sp" and final_wait_all:
                for q in self.ENGS:
                    n = dcnt[q]
                    K = len(dsems[q])
                    for s_i in range(min(n, K)):
                        uses = (n - 1 - s_i) // K + 1
                        wait(dsems[q][s_i], 16 * uses)
                for q in ("pe", "act", "dve", "pool"):
                    if cnt[q] > 0:
                        wait(sems[q], cnt[q])

        with nc.Block() as block:
            @block.tensor
            def _(e):
                run("pe", e)

            @block.scalar
            def _(e):
                run("act", e)

            @block.vector
            def _(e):
                run("dve", e)

            @block.gpsimd
            def _(e):
                run("pool", e)

            @block.sync
            def _(e):
                run("sp", e)


class Arena:
    def __init__(self, ap, nbytes):
        self.ap = ap
        self.cap = nbytes
        self.off = 0
        self.peak = 0

    def alloc(self, nbytes, dtype=BF16):
        off = (self.off + 63) // 64 * 64
        assert off + nbytes <= self.cap, f"arena overflow: {off}+{nbytes} > {self.cap}"
        self.off = off + nbytes
        self.peak = max(self.peak, self.off)
        v = self.ap[:, off // 2:(off + nbytes) // 2]
        if dtype == F32:
            v = v.bitcast(F32)
        return v


def _rope_tables():
    tab = np.zeros((NKT, 128, 384), np.float32)
    tab[:2, :, 0:128] = 1.0
    tab[:2, :, 256:320] = 1.0
    n = np.arange(S)
    rows = (n // 64).astype(np.float32)
    cols = (n % 64).astype(np.float32)

    def cs(dim):
        q = dim // 4
        inv = (np.float32(10000.0) ** (-(np.arange(q, dtype=np.float32) / np.float32(q)))).astype(np.float32)
        ang = np.stack([rows[:, None] * inv, cols[:, None] * inv], axis=1).astype(np.float32)
        c = np.cos(ang).astype(np.float32)
        s = np.sin(ang).astype(np.float32)
        ce = np.broadcast_to(c[:, :, None, :], (S, 2, 2, q)).reshape(S, dim)
        se = np.broadcast_to(s[:, :, None, :], (S, 2, 2, q)).reshape(S, dim)
        return ce, se

    ca, sa = cs(128)
    cb, sb = cs(64)
    full = np.concatenate([ca, sa, cb, sb], axis=1).reshape(32, 128, 384)
    tab[2:] = full
    return tab


def _fourier_tables():
    nh = np.arange(128)[:, None].astype(np.float64)
    kl = np.arange(128)[None, :].astype(np.float64)
    ang = 2 * np.pi * nh * kl / 128.0
    norm = 1.0 / math.sqrt(4096.0 * 256.0)
    f1 = np.zeros((128, 128, 2))
    f1[:, :, 0] = np.cos(ang) * norm
    f1[:, :, 1] = -np.sin(ang) * norm
    f1 = f1.reshape(128, 256)
    j = (np.arange(2)[None, :, None] * 128 + np.arange(128)[:, None, None]).astype(np.float64)
    m = np.arange(256)[None, None, :].astype(np.float64)
    a2 = 2 * np.pi * j * m / 256.0
    cs = np.concatenate([np.cos(a2), np.sin(a2)], axis=2).reshape(128, 1024)
    par = np.arange(2)[:, None, None, None, None, None]
    nlo = np.arange(32)[None, :, None, None, None, None].astype(np.float64)
    ri = np.arange(2)[None, None, :, None, None, None]
    kp = np.arange(64)[None, None, None, :, None, None]
    wh = np.arange(2)[None, None, None, None, :, None]
    khi = np.arange(32)[None, None, None, None, None, :]
    k = (2 * kp + par) + 128 * khi
    ang3 = 2 * np.pi * nlo * k / 4096.0
    tr = np.cos(ang3)
    ti = -np.sin(ang3)
    shape = (2, 32, 2, 64, 2, 32)
    tr = np.broadcast_to(tr, shape)
    ti = np.broadcast_to(ti, shape)
    rib = np.broadcast_to(ri, shape)
    whb = np.broadcast_to(wh, shape)
    t = np.where(whb == 0, np.where(rib == 0, tr, -ti), np.where(rib == 0, ti, tr))
    tt = t.reshape(128, 64 * 2 * 32)
    ttz = np.zeros((128, 2, 64 * 2 * 32))
    ttz[0:64, 0, :] = tt[0:64]
    ttz[64:128, 1, :] = tt[64:128]
    return f1.astype(NPBF), cs.astype(NPBF), ttz.reshape(128, 8192).astype(NPBF)


_CONSTS = {}


def _consts():
    if _CONSTS:
        return _CONSTS
    cf32 = np.zeros((128, 388), np.float32)
    cf32[:, 260:388] = 1.0
    cf32[0, 0:128] = 1.0
    cf32[32, 128:256] = 1.0
    cf32[0, 256] = 1.0
    cf32[32, 258] = 1.0
    cbf = np.zeros((128, 256), np.float32)
    cbf[:, 0:128] = np.eye(128, dtype=np.float32)
    cbf[:, 128:256] = 1.0
    f1, cs, tt = _fourier_tables()
    _CONSTS.update(cf32=cf32, cbf=cbf.astype(NPBF), rope=_rope_tables(), f1=f1, cs=cs, tt=tt)
    return _CONSTS


def build_program(stage="full"):
    nc = bass.Bass("TRN2", target_bir_lowering=False)
    P = Prog()

    def din(name, shape, dt=F32):
        return nc.dram_tensor(name, list(shape), dt, kind="ExternalInput").ap()

    def dint(name, shape, dt, ext=False):
        return nc.dram_tensor(name, list(shape), dt,
                              kind=("ExternalOutput" if ext else "Internal")).ap()

    x_d = din("x", [S, D])
    ctx_d = din("ctx", [CTX, D])
    cT_d = din("cT", [128, 16])
    adaw_d = din("ada_w", [2, D, 3 * D])
    adab_d = din("ada_b", [2, 3 * D])
    ngT_d = din("ngT", [128, 16])
    win_d = din("win", [D, 3584])
    wout_d = din("wout", [D, D])
    fin_d = din("fin", [D, 2 * D])
    fout_d = din("fout", [D, D])
    gbc_d = din("gbc", [128, 1280])
    lam_d = din("lam", [1, 256])
    sgT_d = din("sgT", [128, 1])
    cf32_d = din("cf32", [128, 388])
    cbf_d = din("cbf", [128, 256], BF16)
    rope_d = din("rope", [NKT, 128, 384])
    f1_d = din("f1", [128, 256], BF16)
    cs_d = din("cs", [128, 1024], BF16)
    tt_d = din("tt", [128, 8192], BF16)
    out_d = nc.dram_tensor("out", [S, D], F32, kind="ExternalOutput").ap()
    og_d = dint("ogd", [D, S], BF16, ext=(stage in ("L0", "L0s")))
    x1_d = dint("x1d", [S, D], F32)
    u_d = dint("ud", [S, D], BF16)
    g_d = dint("gd", [D, S], BF16)
    dbg_d = dint("dbg", [128, 2048], F32, ext=True) if stage[0] in "PS" else None

    es = contextlib.ExitStack()
    with es:
        arena_t = es.enter_context(nc.sbuf_tensor("arena", [128, ARENA_BYTES // 2], BF16))
        ps = es.enter_context(nc.psum_tensor("ps", [128, 8, 512], F32))
        sems = {e: es.enter_context(nc.semaphore("s_" + e)) for e in ("pe", "act", "dve", "pool")}
        dsems = {e: [] for e in Prog.ENGS}
        dsems["sp"] = [es.enter_context(nc.semaphore(f"d_sp{i}")) for i in range(12)]
        dsems["pool"] = [es.enter_context(nc.semaphore(f"d_pl{i}")) for i in range(6)]
        AR = Arena(arena_t[:, :], ARENA_BYTES)

        def bank(i):
            return ps[:, i, :]

        def bankbf(i):
            return ps[:, i, :].bitcast(BF16)

        def pk(i):
            return f"ps{i}"

        CBF = AR.alloc(512)
        ident = CBF[:, 0:128]
        onesb = CBF[:, 128:256]
        CF = AR.alloc(388 * 4, F32)
        sel0 = CF[:, 0:128]
        sel32 = CF[:, 128:256]
        e0 = CF[:, 256:258]
        e32 = CF[:, 258:260]
        onesf = CF[:, 260:388]
        GB = AR.alloc(256 * 4, F32)
        qn_bc = GB[:, 0:128]
        kn_bc = GB[:, 128:256]
        SM = AR.alloc(128 * 4, F32)
        CT = SM[:, 0:16]
        NG = SM[:, 16:32]
        MODS = SM[:, 32:64]
        neghalf = SM[:, 64:65]
        sgcol = SM[:, 65:66]
        gcol = SM[:, 66:67]
        neglam = SM[:, 67:68]
        SSX = SM[:, 68:72]
        MSX = SM[:, 72:76]
        RSX = SM[:, 76:80]
        SSH = SM[:, 80:84]
        MSH = SM[:, 84:88]
        RSH = SM[:, 88:92]
        LR = SM[:, 92:94]
        LT = SM[0:1, 96:128]
        junk = AR.alloc(256)
        mark_persist = AR.off

        XT = [AR.alloc(4096, F32) for _ in range(2)]
        XN = [AR.alloc(2048) for _ in range(2)]
        ROPE = [AR.alloc(384 * 4, F32) for _ in range(2)]
        HT = AR.alloc(8192)
        HT3 = HT.rearrange("p (k t) -> p k t", k=8)
        tmp_off = (AR.off + 63) // 64 * 64
        TMP = [AR.alloc(2048, F32) for _ in range(4)]
        W1 = AR.alloc(32768)
        W1v = W1.rearrange("p (k c) -> p k c", k=8)
        mark_generic = AR.off

        def dma(out, in_, r, w, q="sp"):
            return P.add(q, lambda e: e.dma_start(out=out, in_=in_), r, w, dma=True)

        def act(out, in_, func, r, w, bias=0.0, scale=1.0, accum_out=None):
            if accum_out is None:
                return P.add("act", lambda e: e.activation(out=out, in_=in_, func=func,
                                                           bias=bias, scale=scale), r, w)
            return P.add("act", lambda e: e.activation(out=out, in_=in_, func=func, bias=bias,
                                                       scale=scale, accum_out=accum_out), r, w)

        def tt(eng, out, in0, in1, op, r, w):
            return P.add(eng, lambda e: e.tensor_tensor(out=out, in0=in0, in1=in1, op=op), r, w)

        def ts(eng, out, in0, s1, s2, op0, op1, r, w):
            if s2 is None:
                return P.add(eng, lambda e: e.tensor_scalar(out=out, in0=in0, scalar1=s1,
                                                            scalar2=None, op0=op0), r, w)
            return P.add(eng, lambda e: e.tensor_scalar(out=out, in0=in0, scalar1=s1, scalar2=s2,
                                                        op0=op0, op1=op1), r, w)

        def stt(out, in0, scalar, in1, op0, op1, r, w):
            return P.add("dve", lambda e: e.scalar_tensor_tensor(out=out, in0=in0, scalar=scalar,
                                                                 in1=in1, op0=op0, op1=op1), r, w)

        def cp(eng, out, in_, r, w):
            if eng == "act":
                return P.add("act", lambda e: e.copy(out=out, in_=in_), r, w)
            return P.add(eng, lambda e: e.tensor_copy(out=out, in_=in_), r, w)

        def mm(out, lhsT, rhs, start, stop, r, w):
            return P.add("pe", lambda e: e.matmul(out, lhsT, rhs, start=start, stop=stop), r, w)

        def tr(out, in_, r, w):
            return P.add("pe", lambda e: e.transpose(out, in_, ident), r, w)

        def memset(eng, ap, v, w):
            return P.add(eng, lambda e: e.memset(ap, v), (), w)

        fe_cnt = [0]

        def fe1(src_ap, sb=None):
            n = fe_cnt[0]
            fe_cnt[0] += 1
            b = n % 2
            sl = n % 4
            xt, xn = XT[b], XN[b]
            kx, kn = f"xt{b}", f"xn{b}"
            if sb is None:
                dma(xt, src_ap, (), (kx,))
            else:
                xt, kx = sb
            act(xn, xt, AF.Square, (kx,), (kn, f"ssx{sl}"), accum_out=SSX[:, sl:sl + 1])
            ts("dve", MSX[:, sl:sl + 1], SSX[:, sl:sl + 1], 1.0 / D, EPS, ALU.mult, ALU.add,
               (f"ssx{sl}",), (f"msx{sl}",))
            tt("pool", RSX[:, sl:sl + 1], MSX[:, sl:sl + 1], neghalf, ALU.pow,
               (f"msx{sl}", "consts"), (f"rsx{sl}",))
            act(xn, xt, AF.Identity, (kx, f"rsx{sl}"), (kn,), scale=RSX[:, sl:sl + 1])
            return b

        def fe2(b, hslot, moff, tbank):
            xn, kn = XN[b], f"xn{b}"
            tb = bankbf(tbank)
            for c in range(8):
                tr(tb[:, c * 128:(c + 1) * 128], xn[:, c * 128:(c + 1) * 128], (kn, "consts"),
                   (pk(tbank),))
            for c in range(8):
                ts("dve", HT3[:, c, hslot * 128:(hslot + 1) * 128], tb[:, c * 128:(c + 1) * 128],
                   MODS[:, moff + 8 + c:moff + 9 + c], MODS[:, moff + c:moff + c + 1],
                   ALU.mult, ALU.add, (pk(tbank), "mods"), (f"hT{hslot}_{c}",))

        def hkeys(slots):
            return tuple(f"hT{j}_{c}" for j in slots for c in range(8))

        STG = [(XT[0], ("xt0",)), (XT[1], ("xt1",)),
               (AR.ap[:, tmp_off // 2:(tmp_off + 4096) // 2].bitcast(F32), ("tmp0", "tmp1")),
               (AR.ap[:, (tmp_off + 4096) // 2:(tmp_off + 8192) // 2].bitcast(F32), ("tmp2", "tmp3"))]
        stg_n = [0]

        def stage():
            i = stg_n[0]
            stg_n[0] += 1
            return STG[i % len(STG)]

        def ada_third(l, t, SCv, MR, ADB):
            dma(ADB[0:1, :], adab_d[l:l + 1, t * 1024:(t + 1) * 1024], (), ("adb0",))
            dma(ADB[32:33, :], adab_d[l:l + 1, t * 1024:(t + 1) * 1024], (), ("adb32",))
            for k in range(8):
                sb_, sk_ = stage()
                dma(sb_, adaw_d[l, k * 128:(k + 1) * 128, t * 1024:(t + 1) * 1024], (), sk_)
                for hf in range(2):
                    mm(bank(hf), SCv[:, k, :], sb_[:, hf * 512:(hf + 1) * 512], k == 0, k == 7,
                       sk_ + ("sc",), (pk(hf),))
            for hf in range(2):
                tt("dve", MR[:, hf * 512:(hf + 1) * 512], bank(hf), ADB[:, hf * 512:(hf + 1) * 512],
                   ALU.add, (pk(hf), "adb0", "adb32"), ("mr",))

        def cols_from_rows(MR, which, with_ctx):
            pc = bank(2)
            for c in range(8):
                i0 = ((0 * 2 + which) * 8 + c) * 2
                mm(pc[:, i0:i0 + 2], MR[:, c * 128:(c + 1) * 128], e0, True, True,
                   ("mr", "c_cf"), (pk(2),))
                if with_ctx:
                    i1 = ((1 * 2 + which) * 8 + c) * 2
                    mm(pc[:, i1:i1 + 2], MR[:, c * 128:(c + 1) * 128], e32, True, True,
                       ("mr", "c_cf"), (pk(2),))

        def mods_finish(ngoff, with_ctx):
            pc = bank(2).rearrange("p (n two) -> p n two", two=2)
            for src in range(2 if with_ctx else 1):
                o = src * 16
                cp("dve", MODS[:, o:o + 8], pc[:, src * 16:src * 16 + 8, 0], (pk(2),), ("mods",))
                ts("dve", MODS[:, o + 8:o + 16], pc[:, src * 16 + 8:src * 16 + 16, 0], 1.0, None,
                   ALU.add, None, (pk(2),), ("mods",))
                tt("dve", MODS[:, o + 8:o + 16], MODS[:, o + 8:o + 16], NG[:, ngoff:ngoff + 8],
                   ALU.mult, ("mods", "c_ng"), ("mods",))

        def load_cast_weights(src_d, c0, ncols, dst3, dcol0, keyw):
            i = 0
            for k in range(8):
                for cc in range(0, ncols, 1024):
                    w = min(1024, ncols - cc)
                    sb_, sk_ = stage()
                    dma(sb_[:, 0:w], src_d[k * 128:(k + 1) * 128, c0 + cc:c0 + cc + w], (), sk_)
                    eng = ("act", "dve")[i % 2]
                    cp(eng, dst3[:, k, dcol0 + cc:dcol0 + cc + w], sb_[:, 0:w], sk_, (keyw,))
                    i += 1

        def finish_dbg():
            P.barrier()
            dma(dbg_d[:, 0:32], MODS, ("mods",), ())
            cp("pool", TMP[0][:, 0:512], KTA[:, 0, 0:512], (), ("tmp0",))
            dma(dbg_d[:, 512:1024], TMP[0][:, 0:512], ("tmp0",), ())
            cp("pool", TMP[1][:, 0:512], KTB[:, 1, 256:768], (), ("tmp1",))
            dma(dbg_d[:, 1024:1536], TMP[1][:, 0:512], ("tmp1",), ())
            cp("pool", TMP[2][:, 0:256], VA[:, 3, :], (), ("tmp2",))
            cp("pool", TMP[2][:, 256:512], VB[:, 3, 0:256], (), ("tmp2",))
            dma(dbg_d[:, 1536:2048], TMP[2][:, 0:512], ("tmp2",), ())
            memset("pool", TMP[3][:, 0:32], 0.0, ("tmp3",))
            cp("pool", TMP[3][:, 0:1], neglam, (), ("tmp3",))
            cp("pool", TMP[3][:, 1:2], gcol, (), ("tmp3",))
            dma(dbg_d[:, 32:64], TMP[3][:, 0:32], ("tmp3",), ())
            P.emit(nc, sems, dsems)
            return nc, P

        dma(CBF, cbf_d, (), ("c_cbf",))
        dma(CF, cf32_d, (), ("c_cf",))
        dma(GB, gbc_d[:, 0:256], (), ("c_gb",))
        dma(CT, cT_d, (), ("c_ct",))
        dma(NG, ngT_d, (), ("c_ng",))
        dma(sgcol, sgT_d, (), ("c_sg",))
        LAMT = ROPE[0][:, 0:256]
        dma(LAMT[0:1, :], lam_d, (), ("rope0",))
        memset("pool", neghalf, -0.5, ("c_nh",))
        memset("pool", LR, 0.0, ("lr",))
        if stage == "S0":
            cp("pool", MODS, GB[:, 0:32], ("c_gb",), ("mods",))
            P.emit(nc, sems, dsems)
            return nc, P

        KTA = AR.alloc(2 * 4352 * 2).rearrange("p (h n) -> p h n", h=2)
        KTB = AR.alloc(4 * 4352 * 2).rearrange("p (h n) -> p h n", h=4)
        VA = AR.alloc(NKT * 256 * 2).rearrange("p (t c) -> p t c", t=NKT)
        VB = AR.alloc(NKT * 512 * 2).rearrange("p (t c) -> p t c", t=NKT)
        ROT = AR.alloc(4096)
        QTA = AR.alloc(4096).rearrange("p (h t) -> p h t", h=4)
        QTB = AR.alloc(8192).rearrange("p (h i t) -> p h i t", h=4, i=2)
        SG = AR.alloc(8192).rearrange("p (c t) -> p c t", c=8)
        NPT = 4
        PT = [AR.alloc(2048) for _ in range(NPT)]
        GF = AR.alloc(1024, F32)
        SQ = AR.alloc(1024)
        PA = AR.alloc(1024)
        QS = [AR.alloc(1024) for _ in range(2)]
        l0_peak = AR.off
        SCf = QTB.rearrange("p h i t -> p (h i t)")[:, 0:2048].bitcast(F32)
        SCv = SCf.rearrange("p (k m) -> p k m", k=8)
        MR = SG.rearrange("p c t -> p (c t)")[:, 0:2048].bitcast(F32)
        ADB = SG.rearrange("p c t -> p (c t)")[:, 2048:4096].bitcast(F32)

        memset("pool", SCf, 0.0, ("sc",))
        memset("pool", ADB, 0.0, ("adb0", "adb32"))
        act(SCv[:, :, 0], CT[:, 0:8], AF.Silu, ("c_ct", "sc"), ("sc",))
        act(SCv[:, :, 32], CT[:, 8:16], AF.Silu, ("c_ct", "sc"), ("sc",))
        for t in range(2):
            ada_third(0, t, SCv, MR, ADB)
            cols_from_rows(MR, t, True)
        mods_finish(0, True)
        if stage == "S1":
            return finish_dbg()

        LV = LAMT[0:1, :].rearrange("p (a n) -> p a n", a=4)
        LP = LT[:, 0:2]
        tt("dve", LAMT[0:1, 0:64], LV[:, 0, :], LV[:, 1, :], ALU.mult, ("rope0",), ("rope0",))
        tt("dve", LAMT[0:1, 128:192], LV[:, 2, :], LV[:, 3, :], ALU.mult, ("rope0",), ("rope0",))
        P.add("dve", lambda e: e.reduce_sum(out=LP[:, 0:1], in_=LAMT[0:1, 0:64], axis=AX.X),
              ("rope0",), ("lp",))
        P.add("dve", lambda e: e.reduce_sum(out=LP[:, 1:2], in_=LAMT[0:1, 128:192], axis=AX.X),
              ("rope0",), ("lp",))
        act(LP, LP, AF.Exp, ("lp",), ("lp",))
        tt("dve", LR[0:1, 0:1], LP[:, 1:2], LP[:, 0:1], ALU.subtract, ("lp", "lr"), ("lr",))
        ts("dve", LR[0:1, 0:1], LR[0:1, 0:1], -LAM_INIT0, None, ALU.add, None, ("lr",), ("lr",))
        mm(bank(3)[:, 0:2], sel0, LR, True, True, ("lr", "c_cf"), (pk(3),))
        cp("dve", neglam, bank(3)[:, 0:1], (pk(3),), ("consts2",))
        ts("dve", gcol, sgcol, 1.0 - LAM_INIT0, None, ALU.mult, None, ("c_sg",), ("consts2",))

        if stage == "S2":
            return finish_dbg()
        load_cast_weights(win_d, 0, 1536, W1v, 0, "w1")
        if stage == "S3":
            return finish_dbg()

        P.barrier()
        memset("pool", QTB.rearrange("p h i t -> p (h i t)"), 0.0, ("qtb0", "qtb1"))

        def kv_post(kt, s, rb):
            b1, b2, b3 = 4 * s + 1, 4 * s + 2, 4 * s + 3
            rope = ROPE[rb]
            rk = f"rope{rb}"
            rot = ROT[:, s * 768:(s + 1) * 768]
            rkey = f"rotk{s}"
            cp("act", VA[:, kt, :], bank(b1)[:, 256:512], (pk(b1),), (f"va{kt}",))
            cp("act", VB[:, kt, :], bank(b3), (pk(b3),), (f"vb{kt}",))
            for h in range(2):
                act(junk, bank(b1)[:, h * 128:(h + 1) * 128], AF.Square, (pk(b1),), (f"ssh{h}", "junk"),
                    accum_out=SSH[:, h:h + 1])
            ts("dve", MSH[:, 0:2], SSH[:, 0:2], 1.0 / 128, EPS, ALU.mult, ALU.add,
               ("ssh0", "ssh1"), ("msh",))
            tt("pool", RSH[:, 0:2], MSH[:, 0:2], neghalf.broadcast_to([128, 2]), ALU.pow,
               ("msh", "consts"), ("rsh",))
            tt("pool", GF[:, 0:128], rope[:, 0:128], kn_bc, ALU.mult, (rk, "consts"), ("gf",))
            tt("pool", GF[:, 128:256], rope[:, 128:256], kn_bc, ALU.mult, (rk, "consts"), ("gf",))
            for h in range(2):
                stt(TMP[0][:, h * 128:(h + 1) * 128], bank(b1)[:, h * 128:(h + 1) * 128],
                    RSH[:, h:h + 1], GF[:, 0:128], ALU.mult, ALU.mult, (pk(b1), "rsh", "gf"), ("tmp0",))
                stt(TMP[1][:, h * 128:(h + 1) * 128], bank(b1)[:, h * 128:(h + 1) * 128],
                    RSH[:, h:h + 1], GF[:, 128:256], ALU.mult, ALU.mult, (pk(b1), "rsh", "gf"), ("tmp1",))
            t1 = TMP[0][:, 0:256].rearrange("p (g two f) -> p g two f", two=2, f=32)
            t2 = TMP[1][:, 0:256].rearrange("p (g two f) -> p g two f", two=2, f=32)
            ro = rot[:, 0:256].rearrange("p (g two f) -> p g two f", two=2, f=32)
            tt("pool", ro[:, :, 0, :], t1[:, :, 0, :], t2[:, :, 1, :], ALU.subtract,
               ("tmp0", "tmp1"), (rkey,))
            tt("pool", ro[:, :, 1, :], t2[:, :, 0, :], t1[:, :, 1, :], ALU.add,
               ("tmp0", "tmp1"), (rkey,))
            xb = bank(b2).rearrange("p (g d) -> p g d", g=8)
            cbb = rope[:, 256:320].unsqueeze(1).broadcast_to([128, 8, 64])
            sbb = rope[:, 320:384].unsqueeze(1).broadcast_to([128, 8, 64])
            tt("dve", TMP[2].rearrange("p (g d) -> p g d", g=8), xb, cbb, ALU.mult, (pk(b2), rk), ("tmp2",))
            tt("dve", TMP[3].rearrange("p (g d) -> p g d", g=8), xb, sbb, ALU.mult, (pk(b2), rk), ("tmp3",))
            t1 = TMP[2].rearrange("p (g two f) -> p g two f", two=2, f=16)
            t2 = TMP[3].rearrange("p (g two f) -> p g two f", two=2, f=16)
            ro = rot[:, 256:768].rearrange("p (g two f) -> p g two f", two=2, f=16)
            tt("pool", ro[:, :, 0, :], t1[:, :, 0, :], t2[:, :, 1, :], ALU.subtract,
               ("tmp2", "tmp3"), (rkey,))
            tt("pool", ro[:, :, 1, :], t2[:, :, 0, :], t1[:, :, 1, :], ALU.add,
               ("tmp2", "tmp3"), (rkey,))

        def kv_trans(kt, s):
            b0 = 4 * s + KTBANK
            rot = ROT[:, s * 768:(s + 1) * 768]
            tb = bankbf(b0)
            for j in range(6):
                tr(tb[:, j * 128:(j + 1) * 128], rot[:, j * 128:(j + 1) * 128], (f"rotk{s}", "consts"),
                   (pk(b0),))
            if DBG2 >= 1:
                cp("dve", KTA[:, :, kt * 128:(kt + 1) * 128],
                   tb[:, 0:256].rearrange("p (h t) -> p h t", h=2), (pk(b0),), (f"kta{kt}",))
            if DBG2 >= 2:
                cp("dve", KTB[:, :, kt * 128:(kt + 1) * 128],
                   tb[:, 256:768].rearrange("p (h t) -> p h t", h=4), (pk(b0),), (f"ktb{kt}_0", f"ktb{kt}_1"))

        NK1 = NKT if stage != "S4" else 3

        p1_buf = {}

        def p1_A1(kt):
            src = ctx_d[kt * 128:(kt + 1) * 128, :] if kt < 2 else x_d[(kt - 2) * 128:(kt - 1) * 128, :]
            p1_buf[kt] = fe1(src)

        def p1_rope(kt):
            dma(ROPE[kt % 2], rope_d[kt], (), (f"rope{kt % 2}",))

        def p1_A2(kt):
            s_ = kt % 2
            fe2(p1_buf[kt], s_, 16 if kt < 2 else 0, 4 * s_)

        def p1_B(kt):
            s_ = kt % 2
            for j, bnk in enumerate((4 * s_ + 1, 4 * s_ + 2, 4 * s_ + 3)):
                for k in range(8):
                    mm(bank(bnk), HT3[:, k, s_ * 128:(s_ + 1) * 128], W1v[:, k, j * 512:(j + 1) * 512],
                       k == 0, k == 7, (f"hT{s_}_{k}", "w1"), (pk(bnk),))
            kv_post(kt, s_, kt % 2)

        p1_A1(0)
        p1_A1(1)
        p1_rope(0)
        p1_A2(0)
        for kt in range(NK1):
            if kt + 2 < NK1:
                p1_A1(kt + 2)
            if kt + 1 < NK1:
                p1_rope(kt + 1)
            if kt + 1 < NK1:
                p1_A2(kt + 1)
            p1_B(kt)
            if kt >= 1:
                kv_trans(kt - 1, (kt - 1) % 2)
        kv_trans(NK1 - 1, (NK1 - 1) % 2)

        if stage in ("P1", "S4"):
            return finish_dbg()

        P.barrier()
        load_cast_weights(win_d, 1536, 2048, W1v, 0, "w1")
        P.barrier()
        OG3 = HT3
        QTBf = QTB

        def q_post(j, rb, ba, bb):
            rope = ROPE[rb]
            rk = f"rope{rb}"
            rot = ROT[:, (j % 2) * 1024:(j % 2 + 1) * 1024]
            rqk = f"rotq{j % 2}"
            for h in range(4):
                act(junk, bank(ba)[:, h * 128:(h + 1) * 128], AF.Square, (pk(ba),), (f"ssh{h}", "junk"),
                    accum_out=SSH[:, h:h + 1])
            ts("dve", MSH[:, 0:4], SSH[:, 0:4], 1.0 / 128, EPS, ALU.mult, ALU.add,
               ("ssh0", "ssh1", "ssh2", "ssh3"), ("msh",))
            tt("pool", RSH[:, 0:4], MSH[:, 0:4], neghalf.broadcast_to([128, 4]), ALU.pow,
               ("msh", "consts"), ("rsh",))
            tt("pool", GF[:, 0:128], rope[:, 0:128], qn_bc, ALU.mult, (rk, "consts"), ("gf",))
            tt("pool", GF[:, 128:256], rope[:, 128:256], qn_bc, ALU.mult, (rk, "consts"), ("gf",))
            for h in range(4):
                stt(TMP[0][:, h * 128:(h + 1) * 128], bank(ba)[:, h * 128:(h + 1) * 128],
                    RSH[:, h:h + 1], GF[:, 0:128], ALU.mult, ALU.mult, (pk(ba), "rsh", "gf"), ("tmp0",))
                stt(TMP[1][:, h * 128:(h + 1) * 128], bank(ba)[:, h * 128:(h + 1) * 128],
                    RSH[:, h:h + 1], GF[:, 128:256], ALU.mult, ALU.mult, (pk(ba), "rsh", "gf"), ("tmp1",))
            t1 = TMP[0].rearrange("p (g two f) -> p g two f", two=2, f=32)
            t2 = TMP[1].rearrange("p (g two f) -> p g two f", two=2, f=32)
            ro = rot[:, 0:512].rearrange("p (g two f) -> p g two f", two=2, f=32)
            tt("pool", ro[:, :, 0, :], t1[:, :, 0, :], t2[:, :, 1, :], ALU.subtract,
               ("tmp0", "tmp1"), (rqk,))
            tt("pool", ro[:, :, 1, :], t2[:, :, 0, :], t1[:, :, 1, :], ALU.add,
               ("tmp0", "tmp1"), (rqk,))
            xb = bank(bb).rearrange("p (g d) -> p g d", g=8)
            cbb = rope[:, 256:320].unsqueeze(1).broadcast_to([128, 8, 64])
            sbb = rope[:, 320:384].unsqueeze(1).broadcast_to([128, 8, 64])
            tt("dve", TMP[2].rearrange("p (g d) -> p g d", g=8), xb, cbb, ALU.mult, (pk(bb), rk), ("tmp2",))
            tt("dve", TMP[3].rearrange("p (g d) -> p g d", g=8), xb, sbb, ALU.mult, (pk(bb), rk), ("tmp3",))
            t1 = TMP[2].rearrange("p (g two f) -> p g two f", two=2, f=16)
            t2 = TMP[3].rearrange("p (g two f) -> p g two f", two=2, f=16)
            ro = rot[:, 512:1024].rearrange("p (g two f) -> p g two f", two=2, f=16)
            tt("pool", ro[:, :, 0, :], t1[:, :, 0, :], t2[:, :, 1, :], ALU.subtract,
               ("tmp2", "tmp3"), (rqk,))
            tt("pool", ro[:, :, 1, :], t2[:, :, 0, :], t1[:, :, 1, :], ALU.add,
               ("tmp2", "tmp3"), (rqk,))

        def q_trans(j, bt):
            rot = ROT[:, (j % 2) * 1024:(j % 2 + 1) * 1024]
            rqk = f"rotq{j % 2}"
            tb = bankbf(bt)
            for c in range(8):
                tr(tb[:, c * 128:(c + 1) * 128], rot[:, c * 128:(c + 1) * 128], (rqk, "consts"),
                   (pk(bt),))
            cp("dve", QTA[:, :, j * 128:(j + 1) * 128],
               tb[:, 0:512].rearrange("p (h t) -> p h t", h=4), (pk(bt),), ("qta",))
            tbb = tb[:, 512:1024].rearrange("p (h t) -> p h t", h=4)
            cp("dve", QTB[0:64, :, 0, j * 128:(j + 1) * 128], tbb[0:64], (pk(bt),), ("qtb0",))
            cp("dve", QTB[64:128, :, 1, j * 128:(j + 1) * 128], tbb[64:128], (pk(bt),), ("qtb1",))

        maps = [("B", h, i) for h in range(4) for i in range(2)] + [("A", h, 0) for h in range(4)]
        SCALE_A = 128.0 ** -0.5
        SCALE_B = 64.0 ** -0.5

        def attention_block(qb):
            seq = [(mi, pr) for mi in range(len(maps)) for pr in range(NKT // 2)]

            def qk(idx):
                mi, pr = seq[idx]
                kind, h, i = maps[mi]
                sb = 2 * (idx % 2)
                for t in range(2):
                    kt = 2 * pr + t
                    if kind == "A":
                        lhsT = KTA[:, h // 2, kt * 128:(kt + 1) * 128]
                        rhs = QTA[:, h, :]
                        r = (f"kta{kt}", "qta")
                    else:
                        lhsT = KTB[:, h, kt * 128:(kt + 1) * 128]
                        rhs = QTB[:, h, i, :]
                        r = (f"ktb{kt}_{h % 2}", f"qtb{i}")
                    mm(bank(sb + t), lhsT, rhs, True, True, r, (pk(sb + t),))

            def ex(idx):
                mi, pr = seq[idx]
                kind = maps[mi][0]
                sb = 2 * (idx % 2)
                pt = PT[idx % NPT]
                act(pt.rearrange("p (t n) -> p t n", t=2), ps[:, sb:sb + 2, :], AF.Exp,
                    (pk(sb), pk(sb + 1)), (f"pt{idx % NPT}",),
                    scale=(SCALE_A if kind == "A" else SCALE_B))

            def pv(idx):
                mi, pr = seq[idx]
                kind, h, i = maps[mi]
                ob = 4 + 2 * (mi % 2)
                pt = PT[idx % NPT]
                for t in range(2):
                    kt = 2 * pr + t
                    if kind == "A":
                        lhsT = VA[:, kt, (h // 2) * 128:(h // 2 + 1) * 128]
                        r = (f"va{kt}", f"pt{idx % NPT}")
                    else:
                        lhsT = VB[:, kt, h * 128:(h + 1) * 128]
                        r = (f"vb{kt}", f"pt{idx % NPT}")
                    mm(bank(ob), lhsT, pt[:, t * 512:(t + 1) * 512], kt == 0, kt == NKT - 1, r, (pk(ob),))

            def finish(mi):
                kind, h, i = maps[mi]
                ob = 4 + 2 * (mi % 2)
                T = TMP[mi % 2]
                tk = f"tmp{mi % 2}"
                P.add("dve", lambda e: e.reciprocal(out=T, in_=bank(ob + 1)), (pk(ob + 1),), (tk,))
                tt("dve", T, bank(ob), T, ALU.mult, (pk(ob), tk), (tk,))
                if kind == "A":
                    tt("pool", SG[:, h, :], T, SG[:, h, :], ALU.mult, (tk, f"sg{h}"), (f"sg{h}",))
                elif i == 1:
                    T0, T1 = TMP[0], TMP[1]
                    bssq = ob + 1

                    def f1():
                        stt(T0, T1, neglam, T0, ALU.mult, ALU.add, ("tmp0", "tmp1", "consts2"), ("tmp0",))
                        tt("pool", SQ, T0, T0, ALU.mult, ("tmp0",), ("sq",))

                    def f2():
                        mm(bank(bssq), onesb, SQ, True, True, ("sq", "consts"), (pk(bssq),))
                        ts("dve", T1, bank(bssq), 1.0 / 128, EPS, ALU.mult, ALU.add, (pk(bssq),), ("tmp1",))

                    def f3():
                        act(T1, T1, AF.Ln, ("tmp1",), ("tmp1",))
                        act(T1, T1, AF.Exp, ("tmp1",), ("tmp1",), scale=-0.5)

                    def f4():
                        stt(T0, T0, gcol, T1, ALU.mult, ALU.mult, ("tmp0", "tmp1", "consts2"), ("tmp0",))
                        tt("pool", SG[:, 4 + h, :], T0, SG[:, 4 + h, :], ALU.mult, ("tmp0", f"sg{4 + h}"),
                           (f"sg{4 + h}",))

                    return [f1, f2, f3, f4]
                return []

            def psum2(idx):
                mi, pr = seq[idx]
                pt = PT[idx % NPT]
                q = pr // 2
                if pr == NKT // 2 - 1:
                    tt("dve", QS[q % 2], pt[:, 0:512], pt[:, 512:1024], ALU.add, (f"pt{idx % NPT}",),
                       (f"qs{q % 2}",))
                elif pr % 2 == 0:
                    tt("dve", PA, pt[:, 0:512], pt[:, 512:1024], ALU.add, (f"pt{idx % NPT}",), ("pa",))
                else:
                    tt("dve", QS[q % 2], pt[:, 0:512], PA, ALU.add, (f"pt{idx % NPT}", "pa"), (f"qs{q % 2}",))
                    tt("dve", QS[q % 2], QS[q % 2], pt[:, 512:1024], ALU.add, (f"pt{idx % NPT}", f"qs{q % 2}"),
                       (f"qs{q % 2}",))

            def den(idx):
                mi, pr = seq[idx]
                ob = 4 + 2 * (mi % 2)
                q = pr // 2
                mm(bank(ob + 1), onesb, QS[q % 2], pr == 1, pr == NKT // 2 - 1,
                   (f"qs{q % 2}", "consts"), (pk(ob + 1),))

            n = len(seq)
            deferred = {}
            qk(0)
            qk(1)
            for idx in range(n):
                for f in deferred.pop(idx, ()):
                    f()
                ex(idx)
                psum2(idx)
                if idx + 2 < n:
                    qk(idx + 2)
                mi, pr = seq[idx]
                if pr >= 2 and pr % 2 == 0:
                    den(idx - 1)
                pv(idx)
                if pr == NKT // 2 - 1:
                    den(idx)
                    for k, f in enumerate(finish(mi)):
                        deferred.setdefault(idx + 2 + 2 * k, []).append(f)
            for k in sorted(deferred):
                for f in deferred[k]:
                    f()

        NQB = 8 if stage != "L0s" else 1
        fb2 = {}

        def A2a(j, qb):
            ti = qb * 4 + j
            fb2[(qb, j)] = fe1(x_d[ti * 128:(ti + 1) * 128, :])

        def R2(j, qb):
            ti = qb * 4 + j
            dma(ROPE[ti % 2], rope_d[2 + ti], (), (f"rope{ti % 2}",))

        for qb in range(NQB):
            def A2b(j, qb=qb):
                fe2(fb2[(qb, j)], j, 0, 4 + (j % 2))

            def B2(j, qb=qb):
                ti = qb * 4 + j
                ba, bb = 2 * (j % 2), 2 * (j % 2) + 1
                for jj, bnk in enumerate((ba, bb)):
                    for k in range(8):
                        mm(bank(bnk), HT3[:, k, j * 128:(j + 1) * 128], W1v[:, k, jj * 512:(jj + 1) * 512],
                           k == 0, k == 7, (f"hT{j}_{k}", "w1"), (pk(bnk),))
                q_post(j, ti % 2, ba, bb)

            def C2(j):
                q_trans(j, 6 + (j % 2))

            def G2(c0, c1):
                for c in range(c0, c1):
                    bnk = 4 + (c % 2)
                    for k in range(8):
                        mm(bank(bnk), W1v[:, k, 1024 + c * 128:1024 + (c + 1) * 128], HT3[:, k, :],
                           k == 0, k == 7, tuple(f"hT{j}_{k}" for j in range(4)) + ("w1",), (pk(bnk),))
                    act(SG[:, c, :], bank(bnk), AF.Silu, (pk(bnk),), (f"sg{c}",))

            if qb == 0:
                A2a(0, qb); R2(0, qb); A2a(1, qb); R2(1, qb)
            A2b(0); A2a(2, qb); A2b(1); B2(0); R2(2, qb); A2a(3, qb); A2b(2); B2(1); R2(3, qb)
            C2(0); A2b(3); B2(2); C2(1)
            G2(0, 4); B2(3); C2(2); G2(4, 8); C2(3)
            if qb + 1 < NQB:
                A2a(0, qb + 1); R2(0, qb + 1); A2a(1, qb + 1); R2(1, qb + 1)
            attention_block(qb)
            for c4 in range(4):
                dma(og_d.rearrange("(c p) n -> p c n", p=128)[:, 2 * c4:2 * c4 + 2, qb * 512:(qb + 1) * 512],
                    SG[:, 2 * c4:2 * c4 + 2, :], (f"sg{2 * c4}", f"sg{2 * c4 + 1}"), ("ogd",))

        if stage in ("L0", "L0s"):
            P.emit(nc, sems, dsems)
            return nc, P


        P.barrier()
        AR.off = mark_generic
        WO = AR.alloc(16384)
        WO3 = WO.rearrange("p (k c) -> p k c", k=8)
        FC = AR.alloc(2560)
        F1 = FC[:, 0:256]
        CS3 = FC[:, 256:1280].rearrange("p (c n) -> p c n", c=2)
        GRH = [TMP[0], TMP[1]]
        PB = [AR.alloc(1024) for _ in range(4)]
        GG = AR.alloc(16384).rearrange("p (c n) -> p c n", c=2)
        YY = AR.alloc(32768)
        FG = AR.alloc(65536).rearrange("p (c n) -> p c n", c=8)
        L1X = AR.alloc(4096, F32)
        STG[:] = [(XT[0], ("xt0",)), (XT[1], ("xt1",)), (L1X, ("l1x",))]
        YYf = YY
        SCf1 = YYf[:, 0:2048].bitcast(F32)
        SCv1 = SCf1.rearrange("p (k m) -> p k m", k=8)
        MR1 = YYf[:, 2048:4096].bitcast(F32)
        ADB1 = YYf[:, 4096:6144].bitcast(F32)
        OGB = YYf[:, 6144:10240].rearrange("p (c t) -> p c t", c=8)
        X1T = [YYf[:, 10240 + i * 2048:10240 + (i + 1) * 2048].bitcast(F32) for i in range(2)]
        UO = [YYf[:, 14336 + i * 1024:14336 + (i + 1) * 1024] for i in range(2)]
        FGf = FG.rearrange("p c n -> p (c n)")
        GO = FGf[:, 0:4096].rearrange("p (c t) -> p c t", c=8)
        X1T = [FGf[:, 4096 + i * 2048:4096 + (i + 1) * 2048].bitcast(F32) for i in range(4)]
        OGBS = [OGB, FGf[:, 12288:16384].rearrange("p (c t) -> p c t", c=8)]
        gen_off = mark_persist
        TTZ = AR.ap[:, (mark_persist + 63) // 64 * 64 // 2:(mark_persist + 63) // 64 * 64 // 2 + 8192]
        TT5 = TTZ.rearrange("p (a k w h) -> p a k w h", a=2, k=64, w=2)

        memset("pool", SCf1, 0.0, ("sc",))
        memset("pool", ADB1, 0.0, ("adb0", "adb32"))
        act(SCv1[:, :, 0], CT[:, 0:8], AF.Silu, ("sc",), ("sc",))
        fg_bc = AR.ap[:, (tmp_off + 4096) // 2:(tmp_off + 8192) // 2].bitcast(F32)
        dma(fg_bc, gbc_d[:, 256:1280], (), ("fgbc",))
        dma(FC[:, 0:256], f1_d, (), ("fc1",))
        dma(FC[:, 256:1280], cs_d, (), ("fc2",))

        def gate_weights(l, src_d):
            ada_third(l, 2, SCv1, MR1, ADB1)
            for hf in range(2):
                mm(bank(4 + hf), sel0, MR1[:, hf * 512:(hf + 1) * 512], True, True, ("mr",), (pk(4 + hf),))

        def fold_gate_into(src_d, keyw):
            for k in range(8):
                sb_, sk_ = stage()
                dma(sb_, src_d[k * 128:(k + 1) * 128, :], (), sk_)
                for hf in range(2):
                    tt("dve", WO3[:, k, hf * 512:(hf + 1) * 512], sb_[:, hf * 512:(hf + 1) * 512],
                       bank(4 + hf), ALU.mult, sk_ + (pk(4 + hf),), (keyw,))

        for t in range(2):
            ada_third(1, t, SCv1, MR1, ADB1)
            cols_from_rows(MR1, t, False)
        mods_finish(8, False)
        ada_third(1, 2, SCv1, MR1, ADB1)
        for hf in range(2):
            cp("dve", GRH[hf], MR1[:, hf * 512:(hf + 1) * 512], ("mr",), ("gr",))
        gate_weights(0, wout_d)
        fold_gate_into(wout_d, "wo")
        load_cast_weights(fin_d, 0, 2048, W1v, 0, "w1")
        P.barrier()

        def ogb_load(qb):
            for c4 in range(4):
                dma(OGBS[qb % 2][:, 2 * c4:2 * c4 + 2, :],
                    og_d.rearrange("(c p) n -> p c n", p=128)[:, 2 * c4:2 * c4 + 2, qb * 512:(qb + 1) * 512],
                    ("ogd",), (f"ogb{qb % 2}_{c4}",))

        def x1_load(ti):
            dma(X1T[ti % 4], x_d[ti * 128:(ti + 1) * 128, :], (), (f"x1t{ti % 4}",))

        ogb_load(0)
        x1_load(0)
        x1_load(1)
        for qb in range(8):
            if qb + 1 < 8:
                ogb_load(qb + 1)
            OGB = OGBS[qb % 2]

            fb1 = {}

            def F1b(j, fb1=fb1):
                fe2(fb1[j], j, 0, 2 + (j % 2))

            def O1(j, qb=qb, fb1=fb1, OGB=OGB):
                ti = qb * 4 + j
                xb = ti % 4
                if ti + 2 < 32:
                    x1_load(ti + 2)
                for hf in range(2):
                    for c in range(8):
                        mm(bank(hf), OGB[:, c, j * 128:(j + 1) * 128], WO3[:, c, hf * 512:(hf + 1) * 512],
                           c == 0, c == 7, (f"ogb{qb % 2}_{c // 2}", "wo"), (pk(hf),))
                for hf in range(2):
                    tt("dve", X1T[xb][:, hf * 512:(hf + 1) * 512], bank(hf), X1T[xb][:, hf * 512:(hf + 1) * 512],
                       ALU.add, (pk(hf), f"x1t{xb}"), (f"x1t{xb}",))
                dma(x1_d[ti * 128:(ti + 1) * 128, :], X1T[xb], (f"x1t{xb}",), ("x1d",), q="pool")
                fb1[j] = fe1(None, sb=(X1T[xb], f"x1t{xb}"))

            def B1(j, qb=qb):
                ti = qb * 4 + j
                xb = ti % 2
                for hf in range(2):
                    for k in range(8):
                        mm(bank(4 + hf), HT3[:, k, j * 128:(j + 1) * 128], W1v[:, k, hf * 512:(hf + 1) * 512],
                           k == 0, k == 7, (f"hT{j}_{k}", "w1"), (pk(4 + hf),))
                    cp("act" if hf == 0 else "dve", UO[xb][:, hf * 512:(hf + 1) * 512], bank(4 + hf),
                       (pk(4 + hf),), (f"uo{xb}_{hf}",))
                dma(u_d[ti * 128:(ti + 1) * 128, :], UO[xb], (f"uo{xb}_0", f"uo{xb}_1"), ("ud",), q="pool")

            def G1(c0, c1):
                for c in range(c0, c1):
                    bnk = 6 + (c % 2)
                    for k in range(8):
                        mm(bank(bnk), W1v[:, k, 1024 + c * 128:1024 + (c + 1) * 128], HT3[:, k, :],
                           k == 0, k == 7, tuple(f"hT{j}_{k}" for j in range(4)) + ("w1",), (pk(bnk),))
                    act(GO[:, c, :], bank(bnk), AF.Silu, (pk(bnk),), ("go",))

            O1(0); O1(1); F1b(0); O1(2); F1b(1); B1(0); O1(3); F1b(2); B1(1); F1b(3); B1(2)
            G1(0, 4); B1(3); G1(4, 8)
            for c4 in range(4):
                dma(g_d.rearrange("(c p) n -> p c n", p=128)[:, 2 * c4:2 * c4 + 2, qb * 512:(qb + 1) * 512],
                    GO[:, 2 * c4:2 * c4 + 2, :], ("go",), ("gd",))

        P.barrier()
        dma(TTZ, tt_d, (), ("ttz",))
        UG = [W1[:, i * 8192:(i + 1) * 8192].rearrange("p (l c) -> p l c", l=32) for i in range(2)]
        Y5 = YY.rearrange("p (c k l r) -> p c k l r", c=2, k=128, l=32)
        u_v = u_d.rearrange("(nh nl) c -> nh nl c", nl=32)
        g_v = g_d.rearrange("(c p) n -> p c n", p=128)
        ev = [0]
        for gr in range(4):
            ub = gr % 2
            for n8 in range(8):
                dma(UG[ub][:, 4 * n8:4 * n8 + 4, :], u_v[:, 4 * n8:4 * n8 + 4, gr * 256:(gr + 1) * 256],
                    ("ud",), (f"ug{ub}_{n8}",))
            dma(GG, g_v[:, 2 * gr:2 * gr + 2, :], ("gd",), ("gg",))
            for nl in range(32):
                bnk = nl % 2
                for cc in range(2):
                    mm(bank(bnk)[:, cc * 256:(cc + 1) * 256], UG[ub][:, nl, cc * 128:(cc + 1) * 128], F1,
                       True, True, (f"ug{ub}_{nl // 4}", "fc1"), (pk(bnk),))
                cp("act" if nl % 4 != 3 else "dve", Y5[:, :, :, nl, :],
                   bank(bnk).rearrange("p (c k r) -> p c k r", c=2, r=2), (pk(bnk),), ("yy",))
            def chdft(kp, gr=gr):
                pbk = (2, 3, 6)[kp % 3]
                pb = PB[kp % 4]
                for cc in range(2):
                    mm(bank(pbk), Y5[:, cc, 2 * kp:2 * kp + 2, :, :].rearrange("p a l r -> p (a l r)"),
                       CS3[:, cc, :], cc == 0, cc == 1, ("yy", "fc2"), (pk(pbk),))
                cp("act" if kp % 3 != 2 else "dve", pb, bank(pbk), (pk(pbk),), (f"pb{kp % 4}",))

            def stage2(kp, gr=gr):
                pb = PB[kp % 4]
                fb = 4 + ((kp // 4) % 2)
                for par in range(2):
                    for mc in range(2):
                        col = (((kp % 4) * 2 + par) * 2 + mc) * 32
                        mm(bank(fb)[:, col:col + 32], pb[:, mc * 128:(mc + 1) * 128], TT5[:, par, kp, 0, :],
                           True, False, (f"pb{kp % 4}", "ttz"), (pk(fb),))
                        mm(bank(fb)[:, col:col + 32], pb[:, 256 + mc * 128:256 + (mc + 1) * 128],
                           TT5[:, par, kp, 1, :], False, True, (f"pb{kp % 4}", "ttz"), (pk(fb),))
                if kp % 4 == 3:
                    k0 = 2 * (kp - 3)
                    fbv = bank(fb).rearrange("p (a m h) -> p a m h", m=2, h=32)
                    for mc in range(2):
                        gv = GG[:, mc, :].rearrange("p (h l) -> p l h", l=128)[:, k0:k0 + 8, :]
                        ov = FG[:, 2 * gr + mc, :].rearrange("p (h l) -> p l h", l=128)[:, k0:k0 + 8, :]
                        tt("dve", ov, fbv[:, :, mc, :], gv, ALU.mult, (pk(fb), "gg"), ("fg",))

            chdft(0)
            chdft(1)
            for kp in range(64):
                if kp + 2 < 64:
                    chdft(kp + 2)
                stage2(kp)

        P.barrier()
        for hf in range(2):
            mm(bank(4 + hf), sel0, GRH[hf], True, True, ("gr",), (pk(4 + hf),))
        fold_gate_into(fout_d, "wo")
        P.barrier()
        ZT = [YY[:, i * 2048:(i + 1) * 2048].bitcast(F32) for i in range(4)]

        def c_L(ti):
            zb = ti % 4
            dma(ZT[zb], x1_d[ti * 128:(ti + 1) * 128, :], ("x1d",), (f"zt{zb}",))

        def c_X(ti):
            zb = ti % 4
            bo = 2 * (ti % 2)
            for hf in range(2):
                for c in range(8):
                    mm(bank(bo + hf), FG[:, c, ti * 128:(ti + 1) * 128], WO3[:, c, hf * 512:(hf + 1) * 512],
                       c == 0, c == 7, ("fg", "wo"), (pk(bo + hf),))
            for hf in range(2):
                tt("dve", ZT[zb][:, hf * 512:(hf + 1) * 512], bank(bo + hf), ZT[zb][:, hf * 512:(hf + 1) * 512],
                   ALU.add, (pk(bo + hf), f"zt{zb}"), (f"zt{zb}",))

        def c_Ya(ti):
            zb = ti % 4
            sl = ti % 4
            act(XN[ti % 2], ZT[zb], AF.Square, (f"zt{zb}",), (f"xn{ti % 2}", f"ssx{sl}"), accum_out=SSX[:, sl:sl + 1])
            ts("dve", MSX[:, sl:sl + 1], SSX[:, sl:sl + 1], 1.0 / D, EPS, ALU.mult, ALU.add,
               (f"ssx{sl}",), (f"msx{sl}",))
            tt("pool", RSX[:, sl:sl + 1], MSX[:, sl:sl + 1], neghalf, ALU.pow, (f"msx{sl}",), (f"rsx{sl}",))

        def c_Yb(ti):
            zb = ti % 4
            sl = ti % 4
            stt(ZT[zb], ZT[zb], RSX[:, sl:sl + 1], fg_bc, ALU.mult, ALU.mult, (f"zt{zb}", f"rsx{sl}", "fgbc"), (f"zt{zb}",))
            dma(out_d[ti * 128:(ti + 1) * 128, :], ZT[zb], (f"zt{zb}",), ("outd",), q="pool")

        c_L(0)
        c_L(1)
        c_X(0)
        for ti in range(32):
            if ti + 2 < 32:
                c_L(ti + 2)
            if ti + 1 < 32:
                c_X(ti + 1)
            c_Ya(ti)
            if ti >= 1:
                c_Yb(ti - 1)
        c_Yb(31)

        P.emit(nc, sems, dsems)
        return nc, P


def _in_maps(inp):
    C = _consts()
    f = lambda a: np.ascontiguousarray(np.asarray(a, dtype=np.float32))
    x = f(inp["x"]); c = f(inp["c"]); ctx = f(inp["ctx"]); c_ctx = f(inp["c_ctx"])
    norm_g = f(inp["norm_g"])
    ngT = np.concatenate([norm_g[0].reshape(8, 128).T, norm_g[1].reshape(8, 128).T], axis=1)
    gbc = np.concatenate([np.broadcast_to(f(inp["attn_qn_g"])[0][None, :], (128, 128)),
                          np.broadcast_to(f(inp["attn_kn_g"])[0][None, :], (128, 128)),
                          np.broadcast_to(f(inp["final_g"])[None, :], (128, 1024))], axis=1)
    lam = np.concatenate([f(inp["lam_q1"])[0], f(inp["lam_k1"])[0], f(inp["lam_q2"])[0],
                          f(inp["lam_k2"])[0]])[None, :]
    shared = dict(
        ada_w=f(inp["ada_w"]), ada_b=f(inp["ada_b"]), ngT=np.ascontiguousarray(ngT),
        win=f(inp["attn_in_w"])[0], wout=f(inp["attn_out_w"])[0], fin=f(inp["fourier_in_w"])[0],
        fout=f(inp["fourier_out_w"])[0], gbc=np.ascontiguousarray(gbc), lam=np.ascontiguousarray(lam),
        sgT=np.ascontiguousarray(f(inp["attn_subln_g"])[0][:, None]),
        cf32=C["cf32"], cbf=C["cbf"], rope=C["rope"], f1=C["f1"], cs=C["cs"], tt=C["tt"])
    maps = []
    for b in range(N_CORES):
        cT = np.concatenate([c[b].reshape(8, 128).T, c_ctx.reshape(8, 128).T], axis=1)
        m = dict(shared)
        m.update(x=x[b], ctx=ctx[b], cT=np.ascontiguousarray(cT))
        maps.append(m)
    return maps


_PROG = {}


def kernel(**inputs):
    if "full" not in _PROG:
        _PROG["full"] = build_program("full")[0]
    nc = _PROG["full"]
    res = run_bass_kernel_spmd(nc, _in_maps(inputs), core_ids=list(range(N_CORES)))
    out = np.stack([np.asarray(r["out"], dtype=np.float32) for r in res.results], axis=0)
    return out
```

```python
import math
import contextlib
import numpy as np
import ml_dtypes
import concourse.bass as bass
import concourse.mybir as mybir
from concourse.bass_utils import run_bass_kernel_spmd

F32 = mybir.dt.float32
BF16 = mybir.dt.bfloat16
AF = mybir.ActivationFunctionType
ALU = mybir.AluOpType
AX = mybir.AxisListType
NPBF = ml_dtypes.bfloat16

S = 4096
D = 1024
CTX = 256
NKT = 34
EPS = 1e-6
LAM_INIT0 = 0.8 - 0.6 * math.exp(-0.3 * 0)
N_CORES = 8
ARENA_BYTES = 212736
import os
DBGL = int(os.environ.get('KDBG', '9'))
DBG2 = int(os.environ.get('KDBG2', '2'))
KTBANK = int(os.environ.get('KTBANK', '0'))


class Prog:
    ENGS = ("pe", "act", "dve", "pool", "sp")

    def __init__(self):
        self.ops = []
        self.lw = {}
        self.rd = {}
        self.bar_deps = set()
        self.bar_done = set(self.ENGS)
        self.last_on = {}
        self.dma_since = []

    def add(self, eng, fn, r=(), w=(), dma=False):
        i = len(self.ops)
        deps = {}
        for k in r:
            j = self.lw.get(k)
            if j is not None:
                deps[j] = True
        for k in w:
            j = self.lw.get(k)
            if j is not None:
                deps.setdefault(j, False)
            for j in self.rd.get(k, ()):
                deps.setdefault(j, False)
        if eng not in self.bar_done:
            for j in self.bar_deps:
                deps.setdefault(j, True)
            self.bar_done.add(eng)
        self.ops.append([eng, fn, deps, dma])
        for k in r:
            lst = self.rd.setdefault(k, [])
            if not dma:
                lst[:] = [j for j in lst if self.ops[j][3] or self.ops[j][0] != eng]
            lst.append(i)
        for k in w:
            self.lw[k] = i
            self.rd[k] = []
        self.last_on[eng] = i
        if dma:
            self.dma_since.append(i)
        return i

    def barrier(self):
        deps = set(self.last_on.values()) | set(self.dma_since)
        if len(self.bar_done) < len(self.ENGS):
            deps |= self.bar_deps
        self.bar_deps = deps
        self.bar_done = set()
        self.dma_since = []

    def emit(self, nc, sems, dsems, final_wait_all=True):
        ops = self.ops
        ms = set()
        for i, op in enumerate(ops):
            eng, fn, deps, dma = op
            nd = []
            for j, raw in deps.items():
                ej, _, _, dj = ops[j]
                if (not dj) and ej == eng:
                    if eng == "pe":
                        continue
                nd.append(j)
            op[2] = nd
            for j in nd:
                ms.add(j)
        val = {}
        prev = {}
        cnt = {e: 0 for e in self.ENGS}
        dcnt = {e: 0 for e in self.ENGS}
        for i, (eng, fn, deps, dma) in enumerate(ops):
            if dma:
                n = dcnt[eng]
                K = len(dsems[eng])
                sem = dsems[eng][n % K]
                val[i] = (sem, 16 * (n // K + 1))
                if n >= K:
                    prev[i] = (sem, 16 * (n // K))
                dcnt[eng] = n + 1
            elif i in ms:
                cnt[eng] += 1
                val[i] = (sems[eng], cnt[eng])
        self.stats = dict(n_ops=len(ops), milestones=dict(cnt), dmas=dict(dcnt))

        def run(eng, e):
            waited = {}

            def wait(sem, v):
                if waited.get(id(sem), 0) < v:
                    e.wait_ge(sem, v)
                    waited[id(sem)] = v

            for i, (en, fn, deps, dma) in enumerate(ops):
                if en != eng:
                    continue
                for j in deps:
                    wait(*val[j])
                if i in prev:
                    wait(*prev[i])
                ins = fn(e)
                if i in val:
                    sem, v = val[i]
                    ins.then_inc(sem, 16 if dma else 1)
            if eng == "sp" and final_wait_all:
                for q in self.ENGS:
                    n = dcnt[q]
                    K = len(dsems[q])
                    for s_i in range(min(n, K)):
                        uses = (n - 1 - s_i) // K + 1
                        wait(dsems[q][s_i], 16 * uses)
                for q in ("pe", "act", "dve", "pool"):
                    if cnt[q] > 0:
                        wait(sems[q], cnt[q])

        with nc.Block() as block:
            @block.tensor
            def _(e):
                run("pe", e)

            @block.scalar
            def _(e):
                run("act", e)

            @block.vector
            def _(e):
                run("dve", e)

            @block.gpsimd
            def _(e):
                run("pool", e)

            @block.sync
            def _(e):
                run("sp", e)


class Arena:
    def __init__(self, ap, nbytes):
        self.ap = ap
        self.cap = nbytes
        self.off = 0
        self.peak = 0

    def alloc(self, nbytes, dtype=BF16):
        off = (self.off + 63) // 64 * 64
        assert off + nbytes <= self.cap, f"arena overflow: {off}+{nbytes} > {self.cap}"
        self.off = off + nbytes
        self.peak = max(self.peak, self.off)
        v = self.ap[:, off // 2:(off + nbytes) // 2]
        if dtype == F32:
            v = v.bitcast(F32)
        return v


def _rope_tables():
    tab = np.zeros((NKT, 128, 384), np.float32)
    tab[:2, :, 0:128] = 1.0
    tab[:2, :, 256:320] = 1.0
    n = np.arange(S)
    rows = (n // 64).astype(np.float32)
    cols = (n % 64).astype(np.float32)

    def cs(dim):
        q = dim // 4
        inv = (np.float32(10000.0) ** (-(np.arange(q, dtype=np.float32) / np.float32(q)))).astype(np.float32)
        ang = np.stack([rows[:, None] * inv, cols[:, None] * inv], axis=1).astype(np.float32)
        c = np.cos(ang).astype(np.float32)
        s = np.sin(ang).astype(np.float32)
        ce = np.broadcast_to(c[:, :, None, :], (S, 2, 2, q)).reshape(S, dim)
        se = np.broadcast_to(s[:, :, None, :], (S, 2, 2, q)).reshape(S, dim)
        return ce, se

    ca, sa = cs(128)
    cb, sb = cs(64)
    full = np.concatenate([ca, sa, cb, sb], axis=1).reshape(32, 128, 384)
    tab[2:] = full
    return tab


def _fourier_tables():
    nh = np.arange(128)[:, None].astype(np.float64)
    kl = np.arange(128)[None, :].astype(np.float64)
    ang = 2 * np.pi * nh * kl / 128.0
    norm = 1.0 / math.sqrt(4096.0 * 256.0)
    f1 = np.zeros((128, 128, 2))
    f1[:, :, 0] = np.cos(ang) * norm
    f1[:, :, 1] = -np.sin(ang) * norm
    f1 = f1.reshape(128, 256)
    j = (np.arange(2)[None, :, None] * 128 + np.arange(128)[:, None, None]).astype(np.float64)
    m = np.arange(256)[None, None, :].astype(np.float64)
    a2 = 2 * np.pi * j * m / 256.0
    cs = np.concatenate([np.cos(a2), np.sin(a2)], axis=2).reshape(128, 1024)
    par = np.arange(2)[:, None, None, None, None, None]
    nlo = np.arange(32)[None, :, None, None, None, None].astype(np.float64)
    ri = np.arange(2)[None, None, :, None, None, None]
    kp = np.arange(64)[None, None, None, :, None, None]
    wh = np.arange(2)[None, None, None, None, :, None]
    khi = np.arange(32)[None, None, None, None, None, :]
    k = (2 * kp + par) + 128 * khi
    ang3 = 2 * np.pi * nlo * k / 4096.0
    tr = np.cos(ang3)
    ti = -np.sin(ang3)
    shape = (2, 32, 2, 64, 2, 32)
    tr = np.broadcast_to(tr, shape)
    ti = np.broadcast_to(ti, shape)
    rib = np.broadcast_to(ri, shape)
    whb = np.broadcast_to(wh, shape)
    t = np.where(whb == 0, np.where(rib == 0, tr, -ti), np.where(rib == 0, ti, tr))
    tt = t.reshape(128, 64 * 2 * 32)
    ttz = np.zeros((128, 2, 64 * 2 * 32))
    ttz[0:64, 0, :] = tt[0:64]
    ttz[64:128, 1, :] = tt[64:128]
    return f1.astype(NPBF), cs.astype(NPBF), ttz.reshape(128, 8192).astype(NPBF)


_CONSTS = {}


def _consts():
    if _CONSTS:
        return _CONSTS
    cf32 = np.zeros((128, 388), np.float32)
    cf32[:, 260:388] = 1.0
    cf32[0, 0:128] = 1.0
    cf32[32, 128:256] = 1.0
    cf32[0, 256] = 1.0
    cf32[32, 258] = 1.0
    cbf = np.zeros((128, 256), np.float32)
    cbf[:, 0:128] = np.eye(128, dtype=np.float32)
    cbf[:, 128:256] = 1.0
    f1, cs, tt = _fourier_tables()
    _CONSTS.update(cf32=cf32, cbf=cbf.astype(NPBF), rope=_rope_tables(), f1=f1, cs=cs, tt=tt)
    return _CONSTS


def build_program(stage="full"):
    nc = bass.Bass("TRN2", target_bir_lowering=False)
    P = Prog()

    def din(name, shape, dt=F32):
        return nc.dram_tensor(name, list(shape), dt, kind="ExternalInput").ap()

    def dint(name, shape, dt, ext=False):
        return nc.dram_tensor(name, list(shape), dt,
                              kind=("ExternalOutput" if ext else "Internal")).ap()

    x_d = din("x", [S, D])
    ctx_d = din("ctx", [CTX, D])
    cT_d = din("cT", [128, 16])
    adaw_d = din("ada_w", [2, D, 3 * D])
    adab_d = din("ada_b", [2, 3 * D])
    ngT_d = din("ngT", [128, 16])
    win_d = din("win", [D, 3584])
    wout_d = din("wout", [D, D])
    fin_d = din("fin", [D, 2 * D])
    fout_d = din("fout", [D, D])
    gbc_d = din("gbc", [128, 1280])
    lam_d = din("lam", [1, 256])
    sgT_d = din("sgT", [128, 1])
    cf32_d = din("cf32", [128, 388])
    cbf_d = din("cbf", [128, 256], BF16)
    rope_d = din("rope", [NKT, 128, 384])
    f1_d = din("f1", [128, 256], BF16)
    cs_d = din("cs", [128, 1024], BF16)
    tt_d = din("tt", [128, 8192], BF16)
    out_d = nc.dram_tensor("out", [S, D], F32, kind="ExternalOutput").ap()
    og_d = dint("ogd", [D, S], BF16, ext=(stage in ("L0", "L0s")))
    x1_d = dint("x1d", [S, D], F32)
    u_d = dint("ud", [S, D], BF16)
    g_d = dint("gd", [D, S], BF16)
    dbg_d = dint("dbg", [128, 2048], F32, ext=True) if stage[0] in "PS" else None

    es = contextlib.ExitStack()
    with es:
        arena_t = es.enter_context(nc.sbuf_tensor("arena", [128, ARENA_BYTES // 2], BF16))
        ps = es.enter_context(nc.psum_tensor("ps", [128, 8, 512], F32))
        sems = {e: es.enter_context(nc.semaphore("s_" + e)) for e in ("pe", "act", "dve", "pool")}
        dsems = {e: [] for e in Prog.ENGS}
        dsems["sp"] = [es.enter_context(nc.semaphore(f"d_sp{i}")) for i in range(12)]
        dsems["pool"] = [es.enter_context(nc.semaphore(f"d_pl{i}")) for i in range(6)]
        AR = Arena(arena_t[:, :], ARENA_BYTES)

        def bank(i):
            return ps[:, i, :]

        def bankbf(i):
            return ps[:, i, :].bitcast(BF16)

        def pk(i):
            return f"ps{i}"

        CBF = AR.alloc(512)
        ident = CBF[:, 0:128]
        onesb = CBF[:, 128:256]
        CF = AR.alloc(388 * 4, F32)
        sel0 = CF[:, 0:128]
        sel32 = CF[:, 128:256]
        e0 = CF[:, 256:258]
        e32 = CF[:, 258:260]
        onesf = CF[:, 260:388]
        GB = AR.alloc(256 * 4, F32)
        qn_bc = GB[:, 0:128]
        kn_bc = GB[:, 128:256]
        SM = AR.alloc(128 * 4, F32)
        CT = SM[:, 0:16]
        NG = SM[:, 16:32]
        MODS = SM[:, 32:64]
        neghalf = SM[:, 64:65]
        sgcol = SM[:, 65:66]
        gcol = SM[:, 66:67]
        neglam = SM[:, 67:68]
        SSX = SM[:, 68:72]
        MSX = SM[:, 72:76]
        RSX = SM[:, 76:80]
        SSH = SM[:, 80:84]
        MSH = SM[:, 84:88]
        RSH = SM[:, 88:92]
        LR = SM[:, 92:94]
        LT = SM[0:1, 96:128]
        junk = AR.alloc(256)
        mark_persist = AR.off

        XT = [AR.alloc(4096, F32) for _ in range(2)]
        XN = [AR.alloc(2048) for _ in range(2)]
        ROPE = [AR.alloc(384 * 4, F32) for _ in range(2)]
        HT = AR.alloc(8192)
        HT3 = HT.rearrange("p (k t) -> p k t", k=8)
        tmp_off = (AR.off + 63) // 64 * 64
        TMP = [AR.alloc(2048, F32) for _ in range(4)]
        W1 = AR.alloc(32768)
        W1v = W1.rearrange("p (k c) -> p k c", k=8)
        mark_generic = AR.off

        def dma(out, in_, r, w, q="sp"):
            return P.add(q, lambda e: e.dma_start(out=out, in_=in_), r, w, dma=True)

        def act(out, in_, func, r, w, bias=0.0, scale=1.0, accum_out=None):
            if accum_out is None:
                return P.add("act", lambda e: e.activation(out=out, in_=in_, func=func,
                                                           bias=bias, scale=scale), r, w)
            return P.add("act", lambda e: e.activation(out=out, in_=in_, func=func, bias=bias,
                                                       scale=scale, accum_out=accum_out), r, w)

        def tt(eng, out, in0, in1, op, r, w):
            return P.add(eng, lambda e: e.tensor_tensor(out=out, in0=in0, in1=in1, op=op), r, w)

        def ts(eng, out, in0, s1, s2, op0, op1, r, w):
            if s2 is None:
                return P.add(eng, lambda e: e.tensor_scalar(out=out, in0=in0, scalar1=s1,
                                                            scalar2=None, op0=op0), r, w)
            return P.add(eng, lambda e: e.tensor_scalar(out=out, in0=in0, scalar1=s1, scalar2=s2,
                                                        op0=op0, op1=op1), r, w)

        def stt(out, in0, scalar, in1, op0, op1, r, w):
            return P.add("dve", lambda e: e.scalar_tensor_tensor(out=out, in0=in0, scalar=scalar,
                                                                 in1=in1, op0=op0, op1=op1), r, w)

        def cp(eng, out, in_, r, w):
            if eng == "act":
                return P.add("act", lambda e: e.copy(out=out, in_=in_), r, w)
            return P.add(eng, lambda e: e.tensor_copy(out=out, in_=in_), r, w)

        def mm(out, lhsT, rhs, start, stop, r, w):
            return P.add("pe", lambda e: e.matmul(out, lhsT, rhs, start=start, stop=stop), r, w)

        def tr(out, in_, r, w):
            return P.add("pe", lambda e: e.transpose(out, in_, ident), r, w)

        def memset(eng, ap, v, w):
            return P.add(eng, lambda e: e.memset(ap, v), (), w)

        fe_cnt = [0]

        def fe1(src_ap, sb=None):
            n = fe_cnt[0]
            fe_cnt[0] += 1
            b = n % 2
            sl = n % 4
            xt, xn = XT[b], XN[b]
            kx, kn = f"xt{b}", f"xn{b}"
            if sb is None:
                dma(xt, src_ap, (), (kx,))
            else:
                xt, kx = sb
            act(xn, xt, AF.Square, (kx,), (kn, f"ssx{sl}"), accum_out=SSX[:, sl:sl + 1])
            ts("dve", MSX[:, sl:sl + 1], SSX[:, sl:sl + 1], 1.0 / D, EPS, ALU.mult, ALU.add,
               (f"ssx{sl}",), (f"msx{sl}",))
            tt("pool", RSX[:, sl:sl + 1], MSX[:, sl:sl + 1], neghalf, ALU.pow,
               (f"msx{sl}", "consts"), (f"rsx{sl}",))
            act(xn, xt, AF.Identity, (kx, f"rsx{sl}"), (kn,), scale=RSX[:, sl:sl + 1])
            return b

        def fe2(b, hslot, moff, tbank):
            xn, kn = XN[b], f"xn{b}"
            tb = bankbf(tbank)
            for c in range(8):
                tr(tb[:, c * 128:(c + 1) * 128], xn[:, c * 128:(c + 1) * 128], (kn, "consts"),
                   (pk(tbank),))
            for c in range(8):
                ts("dve", HT3[:, c, hslot * 128:(hslot + 1) * 128], tb[:, c * 128:(c + 1) * 128],
                   MODS[:, moff + 8 + c:moff + 9 + c], MODS[:, moff + c:moff + c + 1],
                   ALU.mult, ALU.add, (pk(tbank), "mods"), (f"hT{hslot}_{c}",))

        def hkeys(slots):
            return tuple(f"hT{j}_{c}" for j in slots for c in range(8))

        STG = [(XT[0], ("xt0",)), (XT[1], ("xt1",)),
               (AR.ap[:, tmp_off // 2:(tmp_off + 4096) // 2].bitcast(F32), ("tmp0", "tmp1")),
               (AR.ap[:, (tmp_off + 4096) // 2:(tmp_off + 8192) // 2].bitcast(F32), ("tmp2", "tmp3"))]
        stg_n = [0]

        def stage():
            i = stg_n[0]
            stg_n[0] += 1
            return STG[i % len(STG)]

        def ada_third(l, t, SCv, MR, ADB):
            dma(ADB[0:1, :], adab_d[l:l + 1, t * 1024:(t + 1) * 1024], (), ("adb0",))
            dma(ADB[32:33, :], adab_d[l:l + 1, t * 1024:(t + 1) * 1024], (), ("adb32",))
            for k in range(8):
                sb_, sk_ = stage()
                dma(sb_, adaw_d[l, k * 128:(k + 1) * 128, t * 1024:(t + 1) * 1024], (), sk_)
                for hf in range(2):
                    mm(bank(hf), SCv[:, k, :], sb_[:, hf * 512:(hf + 1) * 512], k == 0, k == 7,
                       sk_ + ("sc",), (pk(hf),))
            for hf in range(2):
                tt("dve", MR[:, hf * 512:(hf + 1) * 512], bank(hf), ADB[:, hf * 512:(hf + 1) * 512],
                   ALU.add, (pk(hf), "adb0", "adb32"), ("mr",))

        def cols_from_rows(MR, which, with_ctx):
            pc = bank(2)
            for c in range(8):
                i0 = ((0 * 2 + which) * 8 + c) * 2
                mm(pc[:, i0:i0 + 2], MR[:, c * 128:(c + 1) * 128], e0, True, True,
                   ("mr", "c_cf"), (pk(2),))
                if with_ctx:
                    i1 = ((1 * 2 + which) * 8 + c) * 2
                    mm(pc[:, i1:i1 + 2], MR[:, c * 128:(c + 1) * 128], e32, True, True,
                       ("mr", "c_cf"), (pk(2),))

        def mods_finish(ngoff, with_ctx):
            pc = bank(2).rearrange("p (n two) -> p n two", two=2)
            for src in range(2 if with_ctx else 1):
                o = src * 16
                cp("dve", MODS[:, o:o + 8], pc[:, src * 16:src * 16 + 8, 0], (pk(2),), ("mods",))
                ts("dve", MODS[:, o + 8:o + 16], pc[:, src * 16 + 8:src * 16 + 16, 0], 1.0, None,
                   ALU.add, None, (pk(2),), ("mods",))
                tt("dve", MODS[:, o + 8:o + 16], MODS[:, o + 8:o + 16], NG[:, ngoff:ngoff + 8],
                   ALU.mult, ("mods", "c_ng"), ("mods",))

        def load_cast_weights(src_d, c0, ncols, dst3, dcol0, keyw):
            i = 0
            for k in range(8):
                for cc in range(0, ncols, 1024):
                    w = min(1024, ncols - cc)
                    sb_, sk_ = stage()
                    dma(sb_[:, 0:w], src_d[k * 128:(k + 1) * 128, c0 + cc:c0 + cc + w], (), sk_)
                    eng = ("act", "dve")[i % 2]
                    cp(eng, dst3[:, k, dcol0 + cc:dcol0 + cc + w], sb_[:, 0:w], sk_, (keyw,))
                    i += 1

        def finish_dbg():
            P.barrier()
            dma(dbg_d[:, 0:32], MODS, ("mods",), ())
            cp("pool", TMP[0][:, 0:512], KTA[:, 0, 0:512], (), ("tmp0",))
            dma(dbg_d[:, 512:1024], TMP[0][:, 0:512], ("tmp0",), ())
            cp("pool", TMP[1][:, 0:512], KTB[:, 1, 256:768], (), ("tmp1",))
            dma(dbg_d[:, 1024:1536], TMP[1][:, 0:512], ("tmp1",), ())
            cp("pool", TMP[2][:, 0:256], VA[:, 3, :], (), ("tmp2",))
            cp("pool", TMP[2][:, 256:512], VB[:, 3, 0:256], (), ("tmp2",))
            dma(dbg_d[:, 1536:2048], TMP[2][:, 0:512], ("tmp2",), ())
            memset("pool", TMP[3][:, 0:32], 0.0, ("tmp3",))
            cp("pool", TMP[3][:, 0:1], neglam, (), ("tmp3",))
            cp("pool", TMP[3][:, 1:2], gcol, (), ("tmp3",))
            dma(dbg_d[:, 32:64], TMP[3][:, 0:32], ("tmp3",), ())
            P.emit(nc, sems, dsems)
            return nc, P

        dma(CBF, cbf_d, (), ("c_cbf",))
        dma(CF, cf32_d, (), ("c_cf",))
        dma(GB, gbc_d[:, 0:256], (), ("c_gb",))
        dma(CT, cT_d, (), ("c_ct",))
        dma(NG, ngT_d, (), ("c_ng",))
        dma(sgcol, sgT_d, (), ("c_sg",))
        LAMT = ROPE[0][:, 0:256]
        dma(LAMT[0:1, :], lam_d, (), ("rope0",))
        memset("pool", neghalf, -0.5, ("c_nh",))
        memset("pool", LR, 0.0, ("lr",))
        if stage == "S0":
            cp("pool", MODS, GB[:, 0:32], ("c_gb",), ("mods",))
            P.emit(nc, sems, dsems)
            return nc, P

        KTA = AR.alloc(2 * 4352 * 2).rearrange("p (h n) -> p h n", h=2)
        KTB = AR.alloc(4 * 4352 * 2).rearrange("p (h n) -> p h n", h=4)
        VA = AR.alloc(NKT * 256 * 2).rearrange("p (t c) -> p t c", t=NKT)
        VB = AR.alloc(NKT * 512 * 2).rearrange("p (t c) -> p t c", t=NKT)
        ROT = AR.alloc(4096)
        QTA = AR.alloc(4096).rearrange("p (h t) -> p h t", h=4)
        QTB = AR.alloc(8192).rearrange("p (h i t) -> p h i t", h=4, i=2)
        SG = AR.alloc(8192).rearrange("p (c t) -> p c t", c=8)
        NPT = 4
        PT = [AR.alloc(2048) for _ in range(NPT)]
        GF = AR.alloc(1024, F32)
        SQ = AR.alloc(1024)
        PA = AR.alloc(1024)
        QS = [AR.alloc(1024) for _ in range(2)]
        l0_peak = AR.off
        SCf = QTB.rearrange("p h i t -> p (h i t)")[:, 0:2048].bitcast(F32)
        SCv = SCf.rearrange("p (k m) -> p k m", k=8)
        MR = SG.rearrange("p c t -> p (c t)")[:, 0:2048].bitcast(F32)
        ADB = SG.rearrange("p c t -> p (c t)")[:, 2048:4096].bitcast(F32)

        memset("pool", SCf, 0.0, ("sc",))
        memset("pool", ADB, 0.0, ("adb0", "adb32"))
        act(SCv[:, :, 0], CT[:, 0:8], AF.Silu, ("c_ct", "sc"), ("sc",))
        act(SCv[:, :, 32], CT[:, 8:16], AF.Silu, ("c_ct", "sc"), ("sc",))
        for t in range(2):
            ada_third(0, t, SCv, MR, ADB)
            cols_from_rows(MR, t, True)
        mods_finish(0, True)
        if stage == "S1":
            return finish_dbg()

        LV = LAMT[0:1, :].rearrange("p (a n) -> p a n", a=4)
        LP = LT[:, 0:2]
        tt("dve", LAMT[0:1, 0:64], LV[:, 0, :], LV[:, 1, :], ALU.mult, ("rope0",), ("rope0",))
        tt("dve", LAMT[0:1, 128:192], LV[:, 2, :], LV[:, 3, :], ALU.mult, ("rope0",), ("rope0",))
        P.add("dve", lambda e: e.reduce_sum(out=LP[:, 0:1], in_=LAMT[0:1, 0:64], axis=AX.X),
              ("rope0",), ("lp",))
        P.add("dve", lambda e: e.reduce_sum(out=LP[:, 1:2], in_=LAMT[0:1, 128:192], axis=AX.X),
              ("rope0",), ("lp",))
        act(LP, LP, AF.Exp, ("lp",), ("lp",))
        tt("dve", LR[0:1, 0:1], LP[:, 1:2], LP[:, 0:1], ALU.subtract, ("lp", "lr"), ("lr",))
        ts("dve", LR[0:1, 0:1], LR[0:1, 0:1], -LAM_INIT0, None, ALU.add, None, ("lr",), ("lr",))
        mm(bank(3)[:, 0:2], sel0, LR, True, True, ("lr", "c_cf"), (pk(3),))
        cp("dve", neglam, bank(3)[:, 0:1], (pk(3),), ("consts2",))
        ts("dve", gcol, sgcol, 1.0 - LAM_INIT0, None, ALU.mult, None, ("c_sg",), ("consts2",))

        if stage == "S2":
            return finish_dbg()
        load_cast_weights(win_d, 0, 1536, W1v, 0, "w1")
        if stage == "S3":
            return finish_dbg()

        P.barrier()
        memset("pool", QTB.rearrange("p h i t -> p (h i t)"), 0.0, ("qtb0", "qtb1"))

        def kv_post(kt, s, rb):
            b1, b2, b3 = 4 * s + 1, 4 * s + 2, 4 * s + 3
            rope = ROPE[rb]
            rk = f"rope{rb}"
            rot = ROT[:, s * 768:(s + 1) * 768]
            rkey = f"rotk{s}"
            cp("act", VA[:, kt, :], bank(b1)[:, 256:512], (pk(b1),), (f"va{kt}",))
            cp("act", VB[:, kt, :], bank(b3), (pk(b3),), (f"vb{kt}",))
            for h in range(2):
                act(junk, bank(b1)[:, h * 128:(h + 1) * 128], AF.Square, (pk(b1),), (f"ssh{h}", "junk"),
                    accum_out=SSH[:, h:h + 1])
            ts("dve", MSH[:, 0:2], SSH[:, 0:2], 1.0 / 128, EPS, ALU.mult, ALU.add,
               ("ssh0", "ssh1"), ("msh",))
            tt("pool", RSH[:, 0:2], MSH[:, 0:2], neghalf.broadcast_to([128, 2]), ALU.pow,
               ("msh", "consts"), ("rsh",))
            tt("pool", GF[:, 0:128], rope[:, 0:128], kn_bc, ALU.mult, (rk, "consts"), ("gf",))
            tt("pool", GF[:, 128:256], rope[:, 128:256], kn_bc, ALU.mult, (rk, "consts"), ("gf",))
            for h in range(2):
                stt(TMP[0][:, h * 128:(h + 1) * 128], bank(b1)[:, h * 128:(h + 1) * 128],
                    RSH[:, h:h + 1], GF[:, 0:128], ALU.mult, ALU.mult, (pk(b1), "rsh", "gf"), ("tmp0",))
                stt(TMP[1][:, h * 128:(h + 1) * 128], bank(b1)[:, h * 128:(h + 1) * 128],
                    RSH[:, h:h + 1], GF[:, 128:256], ALU.mult, ALU.mult, (pk(b1), "rsh", "gf"), ("tmp1",))
            t1 = TMP[0][:, 0:256].rearrange("p (g two f) -> p g two f", two=2, f=32)
            t2 = TMP[1][:, 0:256].rearrange("p (g two f) -> p g two f", two=2, f=32)
            ro = rot[:, 0:256].rearrange("p (g two f) -> p g two f", two=2, f=32)
            tt("pool", ro[:, :, 0, :], t1[:, :, 0, :], t2[:, :, 1, :], ALU.subtract,
               ("tmp0", "tmp1"), (rkey,))
            tt("pool", ro[:, :, 1, :], t2[:, :, 0, :], t1[:, :, 1, :], ALU.add,
               ("tmp0", "tmp1"), (rkey,))
            xb = bank(b2).rearrange("p (g d) -> p g d", g=8)
            cbb = rope[:, 256:320].unsqueeze(1).broadcast_to([128, 8, 64])
            sbb = rope[:, 320:384].unsqueeze(1).broadcast_to([128, 8, 64])
            tt("dve", TMP[2].rearrange("p (g d) -> p g d", g=8), xb, cbb, ALU.mult, (pk(b2), rk), ("tmp2",))
            tt("dve", TMP[3].rearrange("p (g d) -> p g d", g=8), xb, sbb, ALU.mult, (pk(b2), rk), ("tmp3",))
            t1 = TMP[2].rearrange("p (g two f) -> p g two f", two=2, f=16)
            t2 = TMP[3].rearrange("p (g two f) -> p g two f", two=2, f=16)
            ro = rot[:, 256:768].rearrange("p (g two f) -> p g two f", two=2, f=16)
            tt("pool", ro[:, :, 0, :], t1[:, :, 0, :], t2[:, :, 1, :], ALU.subtract,
               ("tmp2", "tmp3"), (rkey,))
            tt("pool", ro[:, :, 1, :], t2[:, :, 0, :], t1[:, :, 1, :], ALU.add,
               ("tmp2", "tmp3"), (rkey,))

        def kv_trans(kt, s):
            b0 = 4 * s + KTBANK
            rot = ROT[:, s * 768:(s + 1) * 768]
            tb = bankbf(b0)
            for j in range(6):
                tr(tb[:, j * 128:(j + 1) * 128], rot[:, j * 128:(j + 1) * 128], (f"rotk{s}", "consts"),
                   (pk(b0),))
            if DBG2 >= 1:
                cp("dve", KTA[:, :, kt * 128:(kt + 1) * 128],
                   tb[:, 0:256].rearrange("p (h t) -> p h t", h=2), (pk(b0),), (f"kta{kt}",))
            if DBG2 >= 2:
                cp("dve", KTB[:, :, kt * 128:(kt + 1) * 128],
                   tb[:, 256:768].rearrange("p (h t) -> p h t", h=4), (pk(b0),), (f"ktb{kt}_0", f"ktb{kt}_1"))

        NK1 = NKT if stage != "S4" else 3

        p1_buf = {}

        def p1_A1(kt):
            src = ctx_d[kt * 128:(kt + 1) * 128, :] if kt < 2 else x_d[(kt - 2) * 128:(kt - 1) * 128, :]
            p1_buf[kt] = fe1(src)

        def p1_rope(kt):
            dma(ROPE[kt % 2], rope_d[kt], (), (f"rope{kt % 2}",))

        def p1_A2(kt):
            s_ = kt % 2
            fe2(p1_buf[kt], s_, 16 if kt < 2 else 0, 4 * s_)

        def p1_B(kt):
            s_ = kt % 2
            for j, bnk in enumerate((4 * s_ + 1, 4 * s_ + 2, 4 * s_ + 3)):
                for k in range(8):
                    mm(bank(bnk), HT3[:, k, s_ * 128:(s_ + 1) * 128], W1v[:, k, j * 512:(j + 1) * 512],
                       k == 0, k == 7, (f"hT{s_}_{k}", "w1"), (pk(bnk),))
            kv_post(kt, s_, kt % 2)

        p1_A1(0)
        p1_A1(1)
        p1_rope(0)
        p1_A2(0)
        for kt in range(NK1):
            if kt + 2 < NK1:
                p1_A1(kt + 2)
            if kt + 1 < NK1:
                p1_rope(kt + 1)
            if kt + 1 < NK1:
                p1_A2(kt + 1)
            p1_B(kt)
            if kt >= 1:
                kv_trans(kt - 1, (kt - 1) % 2)
        kv_trans(NK1 - 1, (NK1 - 1) % 2)

        if stage in ("P1", "S4"):
            return finish_dbg()

        P.barrier()
        load_cast_weights(win_d, 1536, 2048, W1v, 0, "w1")
        P.barrier()
        OG3 = HT3
        QTBf = QTB

        def q_post(j, rb, ba, bb):
            rope = ROPE[rb]
            rk = f"rope{rb}"
            rot = ROT[:, (j % 2) * 1024:(j % 2 + 1) * 1024]
            rqk = f"rotq{j % 2}"
            for h in range(4):
                act(junk, bank(ba)[:, h * 128:(h + 1) * 128], AF.Square, (pk(ba),), (f"ssh{h}", "junk"),
                    accum_out=SSH[:, h:h + 1])
            ts("dve", MSH[:, 0:4], SSH[:, 0:4], 1.0 / 128, EPS, ALU.mult, ALU.add,
               ("ssh0", "ssh1", "ssh2", "ssh3"), ("msh",))
            tt("pool", RSH[:, 0:4], MSH[:, 0:4], neghalf.broadcast_to([128, 4]), ALU.pow,
               ("msh", "consts"), ("rsh",))
            tt("pool", GF[:, 0:128], rope[:, 0:128], qn_bc, ALU.mult, (rk, "consts"), ("gf",))
            tt("pool", GF[:, 128:256], rope[:, 128:256], qn_bc, ALU.mult, (rk, "consts"), ("gf",))
            for h in range(4):
                stt(TMP[0][:, h * 128:(h + 1) * 128], bank(ba)[:, h * 128:(h + 1) * 128],
                    RSH[:, h:h + 1], GF[:, 0:128], ALU.mult, ALU.mult, (pk(ba), "rsh", "gf"), ("tmp0",))
                stt(TMP[1][:, h * 128:(h + 1) * 128], bank(ba)[:, h * 128:(h + 1) * 128],
                    RSH[:, h:h + 1], GF[:, 128:256], ALU.mult, ALU.mult, (pk(ba), "rsh", "gf"), ("tmp1",))
            t1 = TMP[0].rearrange("p (g two f) -> p g two f", two=2, f=32)
            t2 = TMP[1].rearrange("p (g two f) -> p g two f", two=2, f=32)
            ro = rot[:, 0:512].rearrange("p (g two f) -> p g two f", two=2, f=32)
            tt("pool", ro[:, :, 0, :], t1[:, :, 0, :], t2[:, :, 1, :], ALU.subtract,
               ("tmp0", "tmp1"), (rqk,))
            tt("pool", ro[:, :, 1, :], t2[:, :, 0, :], t1[:, :, 1, :], ALU.add,
               ("tmp0", "tmp1"), (rqk,))
            xb = bank(bb).rearrange("p (g d) -> p g d", g=8)
            cbb = rope[:, 256:320].unsqueeze(1).broadcast_to([128, 8, 64])
            sbb = rope[:, 320:384].unsqueeze(1).broadcast_to([128, 8, 64])
            tt("dve", TMP[2].rearrange("p (g d) -> p g d", g=8), xb, cbb, ALU.mult, (pk(bb), rk), ("tmp2",))
            tt("dve", TMP[3].rearrange("p (g d) -> p g d", g=8), xb, sbb, ALU.mult, (pk(bb), rk), ("tmp3",))
            t1 = TMP[2].rearrange("p (g two f) -> p g two f", two=2, f=16)
            t2 = TMP[3].rearrange("p (g two f) -> p g two f", two=2, f=16)
            ro = rot[:, 512:1024].rearrange("p (g two f) -> p g two f", two=2, f=16)
            tt("pool", ro[:, :, 0, :], t1[:, :, 0, :], t2[:, :, 1, :], ALU.subtract,
               ("tmp2", "tmp3"), (rqk,))
            tt("pool", ro[:, :, 1, :], t2[:, :, 0, :], t1[:, :, 1, :], ALU.add,
               ("tmp2", "tmp3"), (rqk,))

        def q_trans(j, bt):
            rot = ROT[:, (j % 2) * 1024:(j % 2 + 1) * 1024]
            rqk = f"rotq{j % 2}"
            tb = bankbf(bt)
            for c in range(8):
                tr(tb[:, c * 128:(c + 1) * 128], rot[:, c * 128:(c + 1) * 128], (rqk, "consts"),
                   (pk(bt),))
            cp("dve", QTA[:, :, j * 128:(j + 1) * 128],
               tb[:, 0:512].rearrange("p (h t) -> p h t", h=4), (pk(bt),), ("qta",))
            tbb = tb[:, 512:1024].rearrange("p (h t) -> p h t", h=4)
            cp("dve", QTB[0:64, :, 0, j * 128:(j + 1) * 128], tbb[0:64], (pk(bt),), ("qtb0",))
            cp("dve", QTB[64:128, :, 1, j * 128:(j + 1) * 128], tbb[64:128], (pk(bt),), ("qtb1",))

        maps = [("B", h, i) for h in range(4) for i in range(2)] + [("A", h, 0) for h in range(4)]
        SCALE_A = 128.0 ** -0.5
        SCALE_B = 64.0 ** -0.5

        def attention_block(qb):
            seq = [(mi, pr) for mi in range(len(maps)) for pr in range(NKT // 2)]

            def qk(idx):
                mi, pr = seq[idx]
                kind, h, i = maps[mi]
                sb = 2 * (idx % 2)
                for t in range(2):
                    kt = 2 * pr + t
                    if kind == "A":
                        lhsT = KTA[:, h // 2, kt * 128:(kt + 1) * 128]
                        rhs = QTA[:, h, :]
                        r = (f"kta{kt}", "qta")
                    else:
                        lhsT = KTB[:, h, kt * 128:(kt + 1) * 128]
                        rhs = QTB[:, h, i, :]
                        r = (f"ktb{kt}_{h % 2}", f"qtb{i}")
                    mm(bank(sb + t), lhsT, rhs, True, True, r, (pk(sb + t),))

            def ex(idx):
                mi, pr = seq[idx]
                kind = maps[mi][0]
                sb = 2 * (idx % 2)
                pt = PT[idx % NPT]
                act(pt.rearrange("p (t n) -> p t n", t=2), ps[:, sb:sb + 2, :], AF.Exp,
                    (pk(sb), pk(sb + 1)), (f"pt{idx % NPT}",),
                    scale=(SCALE_A if kind == "A" else SCALE_B))

            def pv(idx):
                mi, pr = seq[idx]
                kind, h, i = maps[mi]
                ob = 4 + 2 * (mi % 2)
                pt = PT[idx % NPT]
                for t in range(2):
                    kt = 2 * pr + t
                    if kind == "A":
                        lhsT = VA[:, kt, (h // 2) * 128:(h // 2 + 1) * 128]
                        r = (f"va{kt}", f"pt{idx % NPT}")
                    else:
                        lhsT = VB[:, kt, h * 128:(h + 1) * 128]
                        r = (f"vb{kt}", f"pt{idx % NPT}")
                    mm(bank(ob), lhsT, pt[:, t * 512:(t + 1) * 512], kt == 0, kt == NKT - 1, r, (pk(ob),))

            def finish(mi):
                kind, h, i = maps[mi]
                ob = 4 + 2 * (mi % 2)
                T = TMP[mi % 2]
                tk = f"tmp{mi % 2}"
                P.add("dve", lambda e: e.reciprocal(out=T, in_=bank(ob + 1)), (pk(ob + 1),), (tk,))
                tt("dve", T, bank(ob), T, ALU.mult, (pk(ob), tk), (tk,))
                if kind == "A":
                    tt("pool", SG[:, h, :], T, SG[:, h, :], ALU.mult, (tk, f"sg{h}"), (f"sg{h}",))
                elif i == 1:
                    T0, T1 = TMP[0], TMP[1]
                    bssq = ob + 1

                    def f1():
                        stt(T0, T1, neglam, T0, ALU.mult, ALU.add, ("tmp0", "tmp1", "consts2"), ("tmp0",))
                        tt("pool", SQ, T0, T0, ALU.mult, ("tmp0",), ("sq",))

                    def f2():
                        mm(bank(bssq), onesb, SQ, True, True, ("sq", "consts"), (pk(bssq),))
                        ts("dve", T1, bank(bssq), 1.0 / 128, EPS, ALU.mult, ALU.add, (pk(bssq),), ("tmp1",))

                    def f3():
                        act(T1, T1, AF.Ln, ("tmp1",), ("tmp1",))
                        act(T1, T1, AF.Exp, ("tmp1",), ("tmp1",), scale=-0.5)

                    def f4():
                        stt(T0, T0, gcol, T1, ALU.mult, ALU.mult, ("tmp0", "tmp1", "consts2"), ("tmp0",))
                        tt("pool", SG[:, 4 + h, :], T0, SG[:, 4 + h, :], ALU.mult, ("tmp0", f"sg{4 + h}"),
                           (f"sg{4 + h}",))

                    return [f1, f2, f3, f4]
                return []

            def psum2(idx):
                mi, pr = seq[idx]
                pt = PT[idx % NPT]
                q = pr // 2
                if pr == NKT // 2 - 1:
                    tt("dve", QS[q % 2], pt[:, 0:512], pt[:, 512:1024], ALU.add, (f"pt{idx % NPT}",),
                       (f"qs{q % 2}",))
                elif pr % 2 == 0:
                    tt("dve", PA, pt[:, 0:512], pt[:, 512:1024], ALU.add, (f"pt{idx % NPT}",), ("pa",))
                else:
                    tt("dve", QS[q % 2], pt[:, 0:512], PA, ALU.add, (f"pt{idx % NPT}", "pa"), (f"qs{q % 2}",))
                    tt("dve", QS[q % 2], QS[q % 2], pt[:, 512:1024], ALU.add, (f"pt{idx % NPT}", f"qs{q % 2}"),
                       (f"qs{q % 2}",))

            def den(idx):
                mi, pr = seq[idx]
                ob = 4 + 2 * (mi % 2)
                q = pr // 2
                mm(bank(ob + 1), onesb, QS[q % 2], pr == 1, pr == NKT // 2 - 1,
                   (f"qs{q % 2}", "consts"), (pk(ob + 1),))

            n = len(seq)
            deferred = {}
            qk(0)
            qk(1)
            for idx in range(n):
                for f in deferred.pop(idx, ()):
                    f()
                ex(idx)
                psum2(idx)
                if idx + 2 < n:
                    qk(idx + 2)
                mi, pr = seq[idx]
                if pr >= 2 and pr % 2 == 0:
                    den(idx - 1)
                pv(idx)
                if pr == NKT // 2 - 1:
                    den(idx)
                    for k, f in enumerate(finish(mi)):
                        deferred.setdefault(idx + 2 + 2 * k, []).append(f)
            for k in sorted(deferred):
                for f in deferred[k]:
                    f()

        NQB = 8 if stage != "L0s" else 1
        fb2 = {}

        def A2a(j, qb):
            ti = qb * 4 + j
            fb2[(qb, j)] = fe1(x_d[ti * 128:(ti + 1) * 128, :])

        def R2(j, qb):
            ti = qb * 4 + j
            dma(ROPE[ti % 2], rope_d[2 + ti], (), (f"rope{ti % 2}",))

        for qb in range(NQB):
            def A2b(j, qb=qb):
                fe2(fb2[(qb, j)], j, 0, 4 + (j % 2))

            def B2(j, qb=qb):
                ti = qb * 4 + j
                ba, bb = 2 * (j % 2), 2 * (j % 2) + 1
                for jj, bnk in enumerate((ba, bb)):
                    for k in range(8):
                        mm(bank(bnk), HT3[:, k, j * 128:(j + 1) * 128], W1v[:, k, jj * 512:(jj + 1) * 512],
                           k == 0, k == 7, (f"hT{j}_{k}", "w1"), (pk(bnk),))
                q_post(j, ti % 2, ba, bb)

            def C2(j):
                q_trans(j, 6 + (j % 2))

            def G2(c0, c1):
                for c in range(c0, c1):
                    bnk = 4 + (c % 2)
                    for k in range(8):
                        mm(bank(bnk), W1v[:, k, 1024 + c * 128:1024 + (c + 1) * 128], HT3[:, k, :],
                           k == 0, k == 7, tuple(f"hT{j}_{k}" for j in range(4)) + ("w1",), (pk(bnk),))
                    act(SG[:, c, :], bank(bnk), AF.Silu, (pk(bnk),), (f"sg{c}",))

            if qb == 0:
                A2a(0, qb); R2(0, qb); A2a(1, qb); R2(1, qb)
            A2b(0); A2a(2, qb); A2b(1); B2(0); R2(2, qb); A2a(3, qb); A2b(2); B2(1); R2(3, qb)
            C2(0); A2b(3); B2(2); C2(1)
            G2(0, 4); B2(3); C2(2); G2(4, 8); C2(3)
            if qb + 1 < NQB:
                A2a(0, qb + 1); R2(0, qb + 1); A2a(1, qb + 1); R2(1, qb + 1)
            attention_block(qb)
            for c4 in range(4):
                dma(og_d.rearrange("(c p) n -> p c n", p=128)[:, 2 * c4:2 * c4 + 2, qb * 512:(qb + 1) * 512],
                    SG[:, 2 * c4:2 * c4 + 2, :], (f"sg{2 * c4}", f"sg{2 * c4 + 1}"), ("ogd",))

        if stage in ("L0", "L0s"):
            P.emit(nc, sems, dsems)
            return nc, P


        P.barrier()
        AR.off = mark_generic
        WO = AR.alloc(16384)
        WO3 = WO.rearrange("p (k c) -> p k c", k=8)
        FC = AR.alloc(2560)
        F1 = FC[:, 0:256]
        CS3 = FC[:, 256:1280].rearrange("p (c n) -> p c n", c=2)
        GRH = [TMP[0], TMP[1]]
        PB = [AR.alloc(1024) for _ in range(4)]
        GG = AR.alloc(16384).rearrange("p (c n) -> p c n", c=2)
        YY = AR.alloc(32768)
        FG = AR.alloc(65536).rearrange("p (c n) -> p c n", c=8)
        L1X = AR.alloc(4096, F32)
        STG[:] = [(XT[0], ("xt0",)), (XT[1], ("xt1",)), (L1X, ("l1x",))]
        YYf = YY
        SCf1 = YYf[:, 0:2048].bitcast(F32)
        SCv1 = SCf1.rearrange("p (k m) -> p k m", k=8)
        MR1 = YYf[:, 2048:4096].bitcast(F32)
        ADB1 = YYf[:, 4096:6144].bitcast(F32)
        OGB = YYf[:, 6144:10240].rearrange("p (c t) -> p c t", c=8)
        X1T = [YYf[:, 10240 + i * 2048:10240 + (i + 1) * 2048].bitcast(F32) for i in range(2)]
        UO = [YYf[:, 14336 + i * 1024:14336 + (i + 1) * 1024] for i in range(2)]
        FGf = FG.rearrange("p c n -> p (c n)")
        GO = FGf[:, 0:4096].rearrange("p (c t) -> p c t", c=8)
        X1T = [FGf[:, 4096 + i * 2048:4096 + (i + 1) * 2048].bitcast(F32) for i in range(4)]
        OGBS = [OGB, FGf[:, 12288:16384].rearrange("p (c t) -> p c t", c=8)]
        gen_off = mark_persist
        TTZ = AR.ap[:, (mark_persist + 63) // 64 * 64 // 2:(mark_persist + 63) // 64 * 64 // 2 + 8192]
        TT5 = TTZ.rearrange("p (a k w h) -> p a k w h", a=2, k=64, w=2)

        memset("pool", SCf1, 0.0, ("sc",))
        memset("pool", ADB1, 0.0, ("adb0", "adb32"))
        act(SCv1[:, :, 0], CT[:, 0:8], AF.Silu, ("sc",), ("sc",))
        fg_bc = AR.ap[:, (tmp_off + 4096) // 2:(tmp_off + 8192) // 2].bitcast(F32)
        dma(fg_bc, gbc_d[:, 256:1280], (), ("fgbc",))
        dma(FC[:, 0:256], f1_d, (), ("fc1",))
        dma(FC[:, 256:1280], cs_d, (), ("fc2",))

        def gate_weights(l, src_d):
            ada_third(l, 2, SCv1, MR1, ADB1)
            for hf in range(2):
                mm(bank(4 + hf), sel0, MR1[:, hf * 512:(hf + 1) * 512], True, True, ("mr",), (pk(4 + hf),))

        def fold_gate_into(src_d, keyw):
            for k in range(8):
                sb_, sk_ = stage()
                dma(sb_, src_d[k * 128:(k + 1) * 128, :], (), sk_)
                for hf in range(2):
                    tt("dve", WO3[:, k, hf * 512:(hf + 1) * 512], sb_[:, hf * 512:(hf + 1) * 512],
                       bank(4 + hf), ALU.mult, sk_ + (pk(4 + hf),), (keyw,))

        for t in range(2):
            ada_third(1, t, SCv1, MR1, ADB1)
            cols_from_rows(MR1, t, False)
        mods_finish(8, False)
        ada_third(1, 2, SCv1, MR1, ADB1)
        for hf in range(2):
            cp("dve", GRH[hf], MR1[:, hf * 512:(hf + 1) * 512], ("mr",), ("gr",))
        gate_weights(0, wout_d)
        fold_gate_into(wout_d, "wo")
        load_cast_weights(fin_d, 0, 2048, W1v, 0, "w1")
        P.barrier()

        def ogb_load(qb):
            for c4 in range(4):
                dma(OGBS[qb % 2][:, 2 * c4:2 * c4 + 2, :],
                    og_d.rearrange("(c p) n -> p c n", p=128)[:, 2 * c4:2 * c4 + 2, qb * 512:(qb + 1) * 512],
                    ("ogd",), (f"ogb{qb % 2}_{c4}",))

        def x1_load(ti):
            dma(X1T[ti % 4], x_d[ti * 128:(ti + 1) * 128, :], (), (f"x1t{ti % 4}",))

        ogb_load(0)
        x1_load(0)
        x1_load(1)
        for qb in range(8):
            if qb + 1 < 8:
                ogb_load(qb + 1)
            OGB = OGBS[qb % 2]

            fb1 = {}

            def F1b(j, fb1=fb1):
                fe2(fb1[j], j, 0, 2 + (j % 2))

            def O1(j, qb=qb, fb1=fb1, OGB=OGB):
                ti = qb * 4 + j
                xb = ti % 4
                if ti + 2 < 32:
                    x1_load(ti + 2)
                for hf in range(2):
                    for c in range(8):
                        mm(bank(hf), OGB[:, c, j * 128:(j + 1) * 128], WO3[:, c, hf * 512:(hf + 1) * 512],
                           c == 0, c == 7, (f"ogb{qb % 2}_{c // 2}", "wo"), (pk(hf),))
                for hf in range(2):
                    tt("dve", X1T[xb][:, hf * 512:(hf + 1) * 512], bank(hf), X1T[xb][:, hf * 512:(hf + 1) * 512],
                       ALU.add, (pk(hf), f"x1t{xb}"), (f"x1t{xb}",))
                dma(x1_d[ti * 128:(ti + 1) * 128, :], X1T[xb], (f"x1t{xb}",), ("x1d",), q="pool")
                fb1[j] = fe1(None, sb=(X1T[xb], f"x1t{xb}"))

            def B1(j, qb=qb):
                ti = qb * 4 + j
                xb = ti % 2
                for hf in range(2):
                    for k in range(8):
                        mm(bank(4 + hf), HT3[:, k, j * 128:(j + 1) * 128], W1v[:, k, hf * 512:(hf + 1) * 512],
                           k == 0, k == 7, (f"hT{j}_{k}", "w1"), (pk(4 + hf),))
                    cp("act" if hf == 0 else "dve", UO[xb][:, hf * 512:(hf + 1) * 512], bank(4 + hf),
                       (pk(4 + hf),), (f"uo{xb}_{hf}",))
                dma(u_d[ti * 128:(ti + 1) * 128, :], UO[xb], (f"uo{xb}_0", f"uo{xb}_1"), ("ud",), q="pool")

            def G1(c0, c1):
                for c in range(c0, c1):
                    bnk = 6 + (c % 2)
                    for k in range(8):
                        mm(bank(bnk), W1v[:, k, 1024 + c * 128:1024 + (c + 1) * 128], HT3[:, k, :],
                           k == 0, k == 7, tuple(f"hT{j}_{k}" for j in range(4)) + ("w1",), (pk(bnk),))
                    act(GO[:, c, :], bank(bnk), AF.Silu, (pk(bnk),), ("go",))

            O1(0); O1(1); F1b(0); O1(2); F1b(1); B1(0); O1(3); F1b(2); B1(1); F1b(3); B1(2)
            G1(0, 4); B1(3); G1(4, 8)
            for c4 in range(4):
                dma(g_d.rearrange("(c p) n -> p c n", p=128)[:, 2 * c4:2 * c4 + 2, qb * 512:(qb + 1) * 512],
                    GO[:, 2 * c4:2 * c4 + 2, :], ("go",), ("gd",))

        P.barrier()
        dma(TTZ, tt_d, (), ("ttz",))
        UG = [W1[:, i * 8192:(i + 1) * 8192].rearrange("p (l c) -> p l c", l=32) for i in range(2)]
        Y5 = YY.rearrange("p (c k l r) -> p c k l r", c=2, k=128, l=32)
        u_v = u_d.rearrange("(nh nl) c -> nh nl c", nl=32)
        g_v = g_d.rearrange("(c p) n -> p c n", p=128)
        ev = [0]
        for gr in range(4):
            ub = gr % 2
            for n8 in range(8):
                dma(UG[ub][:, 4 * n8:4 * n8 + 4, :], u_v[:, 4 * n8:4 * n8 + 4, gr * 256:(gr + 1) * 256],
                    ("ud",), (f"ug{ub}_{n8}",))
            dma(GG, g_v[:, 2 * gr:2 * gr + 2, :], ("gd",), ("gg",))
            for nl in range(32):
                bnk = nl % 2
                for cc in range(2):
                    mm(bank(bnk)[:, cc * 256:(cc + 1) * 256], UG[ub][:, nl, cc * 128:(cc + 1) * 128], F1,
                       True, True, (f"ug{ub}_{nl // 4}", "fc1"), (pk(bnk),))
                cp("act" if nl % 4 != 3 else "dve", Y5[:, :, :, nl, :],
                   bank(bnk).rearrange("p (c k r) -> p c k r", c=2, r=2), (pk(bnk),), ("yy",))
            def chdft(kp, gr=gr):
                pbk = (2, 3, 6)[kp % 3]
                pb = PB[kp % 4]
                for cc in range(2):
                    mm(bank(pbk), Y5[:, cc, 2 * kp:2 * kp + 2, :, :].rearrange("p a l r -> p (a l r)"),
                       CS3[:, cc, :], cc == 0, cc == 1, ("yy", "fc2"), (pk(pbk),))
                cp("act" if kp % 3 != 2 else "dve", pb, bank(pbk), (pk(pbk),), (f"pb{kp % 4}",))

            def stage2(kp, gr=gr):
                pb = PB[kp % 4]
                fb = 4 + ((kp // 4) % 2)
                for par in range(2):
                    for mc in range(2):
                        col = (((kp % 4) * 2 + par) * 2 + mc) * 32
                        mm(bank(fb)[:, col:col + 32], pb[:, mc * 128:(mc + 1) * 128], TT5[:, par, kp, 0, :],
                           True, False, (f"pb{kp % 4}", "ttz"), (pk(fb),))
                        mm(bank(fb)[:, col:col + 32], pb[:, 256 + mc * 128:256 + (mc + 1) * 128],
                           TT5[:, par, kp, 1, :], False, True, (f"pb{kp % 4}", "ttz"), (pk(fb),))
                if kp % 4 == 3:
                    k0 = 2 * (kp - 3)
                    fbv = bank(fb).rearrange("p (a m h) -> p a m h", m=2, h=32)
                    for mc in range(2):
                        gv = GG[:, mc, :].rearrange("p (h l) -> p h l", l=128)[:, :, k0:k0 + 8]
                        ov = FG[:, 2 * gr + mc, :].rearrange("p (h l) -> p h l", l=128)[:, :, k0:k0 + 8]
                        tt("dve", ov, fbv[:, :, mc, :].rearrange("p a h -> p h a"), gv, ALU.mult,
                           (pk(fb), "gg"), ("fg",))

            chdft(0)
            chdft(1)
            for kp in range(64):
                if kp + 2 < 64:
                    chdft(kp + 2)
                stage2(kp)

        P.barrier()
        for hf in range(2):
            mm(bank(4 + hf), sel0, GRH[hf], True, True, ("gr",), (pk(4 + hf),))
        fold_gate_into(fout_d, "wo")
        P.barrier()
        ZT = [YY[:, i * 2048:(i + 1) * 2048].bitcast(F32) for i in range(4)]

        def c_L(ti):
            zb = ti % 4
            dma(ZT[zb], x1_d[ti * 128:(ti + 1) * 128, :], ("x1d",), (f"zt{zb}",))

        def c_X(ti):
            zb = ti % 4
            bo = 2 * (ti % 2)
            for hf in range(2):
                for c in range(8):
                    mm(bank(bo + hf), FG[:, c, ti * 128:(ti + 1) * 128], WO3[:, c, hf * 512:(hf + 1) * 512],
                       c == 0, c == 7, ("fg", "wo"), (pk(bo + hf),))
            for hf in range(2):
                tt("dve", ZT[zb][:, hf * 512:(hf + 1) * 512], bank(bo + hf), ZT[zb][:, hf * 512:(hf + 1) * 512],
                   ALU.add, (pk(bo + hf), f"zt{zb}"), (f"zt{zb}",))

        def c_Ya(ti):
            zb = ti % 4
            sl = ti % 4
            act(XN[ti % 2], ZT[zb], AF.Square, (f"zt{zb}",), (f"xn{ti % 2}", f"ssx{sl}"), accum_out=SSX[:, sl:sl + 1])
            ts("dve", MSX[:, sl:sl + 1], SSX[:, sl:sl + 1], 1.0 / D, EPS, ALU.mult, ALU.add,
               (f"ssx{sl}",), (f"msx{sl}",))
            tt("pool", RSX[:, sl:sl + 1], MSX[:, sl:sl + 1], neghalf, ALU.pow, (f"msx{sl}",), (f"rsx{sl}",))

        def c_Yb(ti):
            zb = ti % 4
            sl = ti % 4
            stt(ZT[zb], ZT[zb], RSX[:, sl:sl + 1], fg_bc, ALU.mult, ALU.mult, (f"zt{zb}", f"rsx{sl}", "fgbc"), (f"zt{zb}",))
            dma(out_d[ti * 128:(ti + 1) * 128, :], ZT[zb], (f"zt{zb}",), ("outd",), q="pool")

        c_L(0)
        c_L(1)
        c_X(0)
        for ti in range(32):
            if ti + 2 < 32:
                c_L(ti + 2)
            if ti + 1 < 32:
                c_X(ti + 1)
            c_Ya(ti)
            if ti >= 1:
                c_Yb(ti - 1)
        c_Yb(31)

        P.emit(nc, sems, dsems)
        return nc, P


def _in_maps(inp):
    C = _consts()
    f = lambda a: np.ascontiguousarray(np.asarray(a, dtype=np.float32))
    x = f(inp["x"]); c = f(inp["c"]); ctx = f(inp["ctx"]); c_ctx = f(inp["c_ctx"])
    norm_g = f(inp["norm_g"])
    ngT = np.concatenate([norm_g[0].reshape(8, 128).T, norm_g[1].reshape(8, 128).T], axis=1)
    gbc = np.concatenate([np.broadcast_to(f(inp["attn_qn_g"])[0][None, :], (128, 128)),
                          np.broadcast_to(f(inp["attn_kn_g"])[0][None, :], (128, 128)),
                          np.broadcast_to(f(inp["final_g"])[None, :], (128, 1024))], axis=1)
    lam = np.concatenate([f(inp["lam_q1"])[0], f(inp["lam_k1"])[0], f(inp["lam_q2"])[0],
                          f(inp["lam_k2"])[0]])[None, :]
    shared = dict(
        ada_w=f(inp["ada_w"]), ada_b=f(inp["ada_b"]), ngT=np.ascontiguousarray(ngT),
        win=f(inp["attn_in_w"])[0], wout=f(inp["attn_out_w"])[0], fin=f(inp["fourier_in_w"])[0],
        fout=f(inp["fourier_out_w"])[0], gbc=np.ascontiguousarray(gbc), lam=np.ascontiguousarray(lam),
        sgT=np.ascontiguousarray(f(inp["attn_subln_g"])[0][:, None]),
        cf32=C["cf32"], cbf=C["cbf"], rope=C["rope"], f1=C["f1"], cs=C["cs"], tt=C["tt"])
    maps = []
    for b in range(N_CORES):
        cT = np.concatenate([c[b].reshape(8, 128).T, c_ctx.reshape(8, 128).T], axis=1)
        m = dict(shared)
        m.update(x=x[b], ctx=ctx[b], cT=np.ascontiguousarray(cT))
        maps.append(m)
    return maps


_PROG = {}


def kernel(**inputs):
    if "full" not in _PROG:
        _PROG["full"] = build_program("full")[0]
    nc = _PROG["full"]
    res = run_bass_kernel_spmd(nc, _in_maps(inputs), core_ids=list(range(N_CORES)))
    out = np.stack([np.asarray(r["out"], dtype=np.float32) for r in res.results], axis=0)
    return out
```

```python
import math
import contextlib
import numpy as np
import ml_dtypes
import concourse.bass as bass
import concourse.mybir as mybir
from concourse.bass_utils import run_bass_kernel_spmd

F32 = mybir.dt.float32
BF16 = mybir.dt.bfloat16
AF = mybir.ActivationFunctionType
ALU = mybir.AluOpType
AX = mybir.AxisListType
NPBF = ml_dtypes.bfloat16

S = 4096
D = 1024
CTX = 256
NKT = 34
EPS = 1e-6
LAM_INIT0 = 0.8 - 0.6 * math.exp(-0.3 * 0)
N_CORES = 8
ARENA_BYTES = 212736
import os
DBGL = int(os.environ.get('KDBG', '9'))
DBG2 = int(os.environ.get('KDBG2', '2'))
KTBANK = int(os.environ.get('KTBANK', '0'))


class Prog:
    ENGS = ("pe", "act", "dve", "pool", "sp")

    def __init__(self):
        self.ops = []
        self.lw = {}
        self.rd = {}
        self.bar_deps = set()
        self.bar_done = set(self.ENGS)
        self.last_on = {}
        self.dma_since = []

    def add(self, eng, fn, r=(), w=(), dma=False):
        i = len(self.ops)
        deps = {}
        for k in r:
            j = self.lw.get(k)
            if j is not None:
                deps[j] = True
        for k in w:
            j = self.lw.get(k)
            if j is not None:
                deps.setdefault(j, False)
            for j in self.rd.get(k, ()):
                deps.setdefault(j, False)
        if eng not in self.bar_done:
            for j in self.bar_deps:
                deps.setdefault(j, True)
            self.bar_done.add(eng)
        self.ops.append([eng, fn, deps, dma])
        for k in r:
            lst = self.rd.setdefault(k, [])
            if not dma:
                lst[:] = [j for j in lst if self.ops[j][3] or self.ops[j][0] != eng]
            lst.append(i)
        for k in w:
            self.lw[k] = i
            self.rd[k] = []
        self.last_on[eng] = i
        if dma:
            self.dma_since.append(i)
        return i

    def barrier(self):
        deps = set(self.last_on.values()) | set(self.dma_since)
        if len(self.bar_done) < len(self.ENGS):
            deps |= self.bar_deps
        self.bar_deps = deps
        self.bar_done = set()
        self.dma_since = []

    def emit(self, nc, sems, dsems, final_wait_all=True):
        ops = self.ops
        ms = set()
        for i, op in enumerate(ops):
            eng, fn, deps, dma = op
            nd = []
            for j, raw in deps.items():
                ej, _, _, dj = ops[j]
                if (not dj) and ej == eng:
                    if eng == "pe":
                        continue
                nd.append(j)
            op[2] = nd
            for j in nd:
                ms.add(j)
        val = {}
        prev = {}
        cnt = {e: 0 for e in self.ENGS}
        dcnt = {e: 0 for e in self.ENGS}
        for i, (eng, fn, deps, dma) in enumerate(ops):
            if dma:
                n = dcnt[eng]
                K = len(dsems[eng])
                sem = dsems[eng][n % K]
                val[i] = (sem, 16 * (n // K + 1))
                if n >= K:
                    prev[i] = (sem, 16 * (n // K))
                dcnt[eng] = n + 1
            elif i in ms:
                cnt[eng] += 1
                val[i] = (sems[eng], cnt[eng])
        self.stats = dict(n_ops=len(ops), milestones=dict(cnt), dmas=dict(dcnt))

        def run(eng, e):
            waited = {}

            def wait(sem, v):
                if waited.get(id(sem), 0) < v:
                    e.wait_ge(sem, v)
                    waited[id(sem)] = v

            for i, (en, fn, deps, dma) in enumerate(ops):
                if en != eng:
                    continue
                for j in deps:
                    wait(*val[j])
                if i in prev:
                    wait(*prev[i])
                ins = fn(e)
                if i in val:
                    sem, v = val[i]
                    ins.then_inc(sem, 16 if dma else 1)
            if eng == "sp" and final_wait_all:
                for q in self.ENGS:
                    n = dcnt[q]
                    K = len(dsems[q])
                    for s_i in range(min(n, K)):
                        uses = (n - 1 - s_i) // K + 1
                        wait(dsems[q][s_i], 16 * uses)
                for q in ("pe", "act", "dve", "pool"):
                    if cnt[q] > 0:
                        wait(sems[q], cnt[q])

        with nc.Block() as block:
            @block.tensor
            def _(e):
                run("pe", e)

            @block.scalar
            def _(e):
                run("act", e)

            @block.vector
            def _(e):
                run("dve", e)

            @block.gpsimd
            def _(e):
                run("pool", e)

            @block.sync
            def _(e):
                run("sp", e)


class Arena:
    def __init__(self, ap, nbytes):
        self.ap = ap
        self.cap = nbytes
        self.off = 0
        self.peak = 0

    def alloc(self, nbytes, dtype=BF16):
        off = (self.off + 63) // 64 * 64
        assert off + nbytes <= self.cap, f"arena overflow: {off}+{nbytes} > {self.cap}"
        self.off = off + nbytes
        self.peak = max(self.peak, self.off)
        v = self.ap[:, off // 2:(off + nbytes) // 2]
        if dtype == F32:
            v = v.bitcast(F32)
        return v


def _rope_tables():
    tab = np.zeros((NKT, 128, 384), np.float32)
    tab[:2, :, 0:128] = 1.0
    tab[:2, :, 256:320] = 1.0
    n = np.arange(S)
    rows = (n // 64).astype(np.float32)
    cols = (n % 64).astype(np.float32)

    def cs(dim):
        q = dim // 4
        inv = (np.float32(10000.0) ** (-(np.arange(q, dtype=np.float32) / np.float32(q)))).astype(np.float32)
        ang = np.stack([rows[:, None] * inv, cols[:, None] * inv], axis=1).astype(np.float32)
        c = np.cos(ang).astype(np.float32)
        s = np.sin(ang).astype(np.float32)
        ce = np.broadcast_to(c[:, :, None, :], (S, 2, 2, q)).reshape(S, dim)
        se = np.broadcast_to(s[:, :, None, :], (S, 2, 2, q)).reshape(S, dim)
        return ce, se

    ca, sa = cs(128)
    cb, sb = cs(64)
    full = np.concatenate([ca, sa, cb, sb], axis=1).reshape(32, 128, 384)
    tab[2:] = full
    return tab


def _fourier_tables():
    nh = np.arange(128)[:, None].astype(np.float64)
    kl = np.arange(128)[None, :].astype(np.float64)
    ang = 2 * np.pi * nh * kl / 128.0
    norm = 1.0 / math.sqrt(4096.0 * 256.0)
    f1 = np.zeros((128, 128, 2))
    f1[:, :, 0] = np.cos(ang) * norm
    f1[:, :, 1] = -np.sin(ang) * norm
    f1 = f1.reshape(128, 256)
    j = (np.arange(2)[None, :, None] * 128 + np.arange(128)[:, None, None]).astype(np.float64)
    m = np.arange(256)[None, None, :].astype(np.float64)
    a2 = 2 * np.pi * j * m / 256.0
    cs = np.concatenate([np.cos(a2), np.sin(a2)], axis=2).reshape(128, 1024)
    par = np.arange(2)[:, None, None, None, None, None]
    nlo = np.arange(32)[None, :, None, None, None, None].astype(np.float64)
    ri = np.arange(2)[None, None, :, None, None, None]
    kp = np.arange(64)[None, None, None, :, None, None]
    wh = np.arange(2)[None, None, None, None, :, None]
    khi = np.arange(32)[None, None, None, None, None, :]
    k = (2 * kp + par) + 128 * khi
    ang3 = 2 * np.pi * nlo * k / 4096.0
    tr = np.cos(ang3)
    ti = -np.sin(ang3)
    shape = (2, 32, 2, 64, 2, 32)
    tr = np.broadcast_to(tr, shape)
    ti = np.broadcast_to(ti, shape)
    rib = np.broadcast_to(ri, shape)
    whb = np.broadcast_to(wh, shape)
    t = np.where(whb == 0, np.where(rib == 0, tr, -ti), np.where(rib == 0, ti, tr))
    tt = t.reshape(128, 64 * 2 * 32)
    ttz = np.zeros((128, 2, 64 * 2 * 32))
    ttz[0:64, 0, :] = tt[0:64]
    ttz[64:128, 1, :] = tt[64:128]
    return f1.astype(NPBF), cs.astype(NPBF), ttz.reshape(128, 8192).astype(NPBF)


_CONSTS = {}


def _consts():
    if _CONSTS:
        return _CONSTS
    cf32 = np.zeros((128, 388), np.float32)
    cf32[:, 260:388] = 1.0
    cf32[0, 0:128] = 1.0
    cf32[32, 128:256] = 1.0
    cf32[0, 256] = 1.0
    cf32[32, 258] = 1.0
    cbf = np.zeros((128, 256), np.float32)
    cbf[:, 0:128] = np.eye(128, dtype=np.float32)
    cbf[:, 128:256] = 1.0
    f1, cs, tt = _fourier_tables()
    _CONSTS.update(cf32=cf32, cbf=cbf.astype(NPBF), rope=_rope_tables(), f1=f1, cs=cs, tt=tt)
    return _CONSTS


def build_program(stage="full"):
    nc = bass.Bass("TRN2", target_bir_lowering=False)
    P = Prog()

    def din(name, shape, dt=F32):
        return nc.dram_tensor(name, list(shape), dt, kind="ExternalInput").ap()

    def dint(name, shape, dt, ext=False):
        return nc.dram_tensor(name, list(shape), dt,
                              kind=("ExternalOutput" if ext else "Internal")).ap()

    x_d = din("x", [S, D])
    ctx_d = din("ctx", [CTX, D])
    cT_d = din("cT", [128, 16])
    adaw_d = din("ada_w", [2, D, 3 * D])
    adab_d = din("ada_b", [2, 3 * D])
    ngT_d = din("ngT", [128, 16])
    win_d = din("win", [D, 3584])
    wout_d = din("wout", [D, D])
    fin_d = din("fin", [D, 2 * D])
    fout_d = din("fout", [D, D])
    gbc_d = din("gbc", [128, 1280])
    lam_d = din("lam", [1, 256])
    sgT_d = din("sgT", [128, 1])
    cf32_d = din("cf32", [128, 388])
    cbf_d = din("cbf", [128, 256], BF16)
    rope_d = din("rope", [NKT, 128, 384])
    f1_d = din("f1", [128, 256], BF16)
    cs_d = din("cs", [128, 1024], BF16)
    tt_d = din("tt", [128, 8192], BF16)
    out_d = nc.dram_tensor("out", [S, D], F32, kind="ExternalOutput").ap()
    og_d = dint("ogd", [D, S], BF16, ext=(stage in ("L0", "L0s")))
    x1_d = dint("x1d", [S, D], F32)
    u_d = dint("ud", [S, D], BF16)
    g_d = dint("gd", [D, S], BF16)
    dbg_d = dint("dbg", [128, 2048], F32, ext=True) if stage[0] in "PS" else None

    es = contextlib.ExitStack()
    with es:
        arena_t = es.enter_context(nc.sbuf_tensor("arena", [128, ARENA_BYTES // 2], BF16))
        ps = es.enter_context(nc.psum_tensor("ps", [128, 8, 512], F32))
        sems = {e: es.enter_context(nc.semaphore("s_" + e)) for e in ("pe", "act", "dve", "pool")}
        dsems = {e: [] for e in Prog.ENGS}
        dsems["sp"] = [es.enter_context(nc.semaphore(f"d_sp{i}")) for i in range(12)]
        dsems["pool"] = [es.enter_context(nc.semaphore(f"d_pl{i}")) for i in range(6)]
        AR = Arena(arena_t[:, :], ARENA_BYTES)

        def bank(i):
            return ps[:, i, :]

        def bankbf(i):
            return ps[:, i, :].bitcast(BF16)

        def pk(i):
            return f"ps{i}"

        CBF = AR.alloc(512)
        ident = CBF[:, 0:128]
        onesb = CBF[:, 128:256]
        CF = AR.alloc(388 * 4, F32)
        sel0 = CF[:, 0:128]
        sel32 = CF[:, 128:256]
        e0 = CF[:, 256:258]
        e32 = CF[:, 258:260]
        onesf = CF[:, 260:388]
        GB = AR.alloc(256 * 4, F32)
        qn_bc = GB[:, 0:128]
        kn_bc = GB[:, 128:256]
        SM = AR.alloc(128 * 4, F32)
        CT = SM[:, 0:16]
        NG = SM[:, 16:32]
        MODS = SM[:, 32:64]
        neghalf = SM[:, 64:65]
        sgcol = SM[:, 65:66]
        gcol = SM[:, 66:67]
        neglam = SM[:, 67:68]
        SSX = SM[:, 68:72]
        MSX = SM[:, 72:76]
        RSX = SM[:, 76:80]
        SSH = SM[:, 80:84]
        MSH = SM[:, 84:88]
        RSH = SM[:, 88:92]
        LR = SM[:, 92:94]
        LT = SM[0:1, 96:128]
        junk = AR.alloc(256)
        mark_persist = AR.off

        XT = [AR.alloc(4096, F32) for _ in range(2)]
        XN = [AR.alloc(2048) for _ in range(2)]
        ROPE = [AR.alloc(384 * 4, F32) for _ in range(2)]
        HT = AR.alloc(8192)
        HT3 = HT.rearrange("p (k t) -> p k t", k=8)
        tmp_off = (AR.off + 63) // 64 * 64
        TMP = [AR.alloc(2048, F32) for _ in range(4)]
        W1 = AR.alloc(32768)
        W1v = W1.rearrange("p (k c) -> p k c", k=8)
        mark_generic = AR.off

        def dma(out, in_, r, w, q="sp"):
            return P.add(q, lambda e: e.dma_start(out=out, in_=in_), r, w, dma=True)

        def act(out, in_, func, r, w, bias=0.0, scale=1.0, accum_out=None):
            if accum_out is None:
                return P.add("act", lambda e: e.activation(out=out, in_=in_, func=func,
                                                           bias=bias, scale=scale), r, w)
            return P.add("act", lambda e: e.activation(out=out, in_=in_, func=func, bias=bias,
                                                       scale=scale, accum_out=accum_out), r, w)

        def tt(eng, out, in0, in1, op, r, w):
            return P.add(eng, lambda e: e.tensor_tensor(out=out, in0=in0, in1=in1, op=op), r, w)

        def ts(eng, out, in0, s1, s2, op0, op1, r, w):
            if s2 is None:
                return P.add(eng, lambda e: e.tensor_scalar(out=out, in0=in0, scalar1=s1,
                                                            scalar2=None, op0=op0), r, w)
            return P.add(eng, lambda e: e.tensor_scalar(out=out, in0=in0, scalar1=s1, scalar2=s2,
                                                        op0=op0, op1=op1), r, w)

        def stt(out, in0, scalar, in1, op0, op1, r, w):
            return P.add("dve", lambda e: e.scalar_tensor_tensor(out=out, in0=in0, scalar=scalar,
                                                                 in1=in1, op0=op0, op1=op1), r, w)

        def cp(eng, out, in_, r, w):
            if eng == "act":
                return P.add("act", lambda e: e.copy(out=out, in_=in_), r, w)
            return P.add(eng, lambda e: e.tensor_copy(out=out, in_=in_), r, w)

        def mm(out, lhsT, rhs, start, stop, r, w):
            return P.add("pe", lambda e: e.matmul(out, lhsT, rhs, start=start, stop=stop), r, w)

        def tr(out, in_, r, w):
            return P.add("pe", lambda e: e.transpose(out, in_, ident), r, w)

        def memset(eng, ap, v, w):
            return P.add(eng, lambda e: e.memset(ap, v), (), w)

        fe_cnt = [0]

        def fe1(src_ap, sb=None):
            n = fe_cnt[0]
            fe_cnt[0] += 1
            b = n % 2
            sl = n % 4
            xt, xn = XT[b], XN[b]
            kx, kn = f"xt{b}", f"xn{b}"
            if sb is None:
                dma(xt, src_ap, (), (kx,))
            else:
                xt, kx = sb
            act(xn, xt, AF.Square, (kx,), (kn, f"ssx{sl}"), accum_out=SSX[:, sl:sl + 1])
            ts("dve", MSX[:, sl:sl + 1], SSX[:, sl:sl + 1], 1.0 / D, EPS, ALU.mult, ALU.add,
               (f"ssx{sl}",), (f"msx{sl}",))
            tt("pool", RSX[:, sl:sl + 1], MSX[:, sl:sl + 1], neghalf, ALU.pow,
               (f"msx{sl}", "consts"), (f"rsx{sl}",))
            act(xn, xt, AF.Identity, (kx, f"rsx{sl}"), (kn,), scale=RSX[:, sl:sl + 1])
            return b

        def fe2(b, hslot, moff, tbank):
            xn, kn = XN[b], f"xn{b}"
            tb = bankbf(tbank)
            for c in range(8):
                tr(tb[:, c * 128:(c + 1) * 128], xn[:, c * 128:(c + 1) * 128], (kn, "consts"),
                   (pk(tbank),))
            for c in range(8):
                ts("dve", HT3[:, c, hslot * 128:(hslot + 1) * 128], tb[:, c * 128:(c + 1) * 128],
                   MODS[:, moff + 8 + c:moff + 9 + c], MODS[:, moff + c:moff + c + 1],
                   ALU.mult, ALU.add, (pk(tbank), "mods"), (f"hT{hslot}_{c}",))

        def hkeys(slots):
            return tuple(f"hT{j}_{c}" for j in slots for c in range(8))

        STG = [(XT[0], ("xt0",)), (XT[1], ("xt1",)),
               (AR.ap[:, tmp_off // 2:(tmp_off + 4096) // 2].bitcast(F32), ("tmp0", "tmp1")),
               (AR.ap[:, (tmp_off + 4096) // 2:(tmp_off + 8192) // 2].bitcast(F32), ("tmp2", "tmp3"))]
        stg_n = [0]

        def stage():
            i = stg_n[0]
            stg_n[0] += 1
            return STG[i % len(STG)]

        def ada_third(l, t, SCv, MR, ADB):
            dma(ADB[0:1, :], adab_d[l:l + 1, t * 1024:(t + 1) * 1024], (), ("adb0",))
            dma(ADB[32:33, :], adab_d[l:l + 1, t * 1024:(t + 1) * 1024], (), ("adb32",))
            for k in range(8):
                sb_, sk_ = stage()
                dma(sb_, adaw_d[l, k * 128:(k + 1) * 128, t * 1024:(t + 1) * 1024], (), sk_)
                for hf in range(2):
                    mm(bank(hf), SCv[:, k, :], sb_[:, hf * 512:(hf + 1) * 512], k == 0, k == 7,
                       sk_ + ("sc",), (pk(hf),))
            for hf in range(2):
                tt("dve", MR[:, hf * 512:(hf + 1) * 512], bank(hf), ADB[:, hf * 512:(hf + 1) * 512],
                   ALU.add, (pk(hf), "adb0", "adb32"), ("mr",))

        def cols_from_rows(MR, which, with_ctx):
            pc = bank(2)
            for c in range(8):
                i0 = ((0 * 2 + which) * 8 + c) * 2
                mm(pc[:, i0:i0 + 2], MR[:, c * 128:(c + 1) * 128], e0, True, True,
                   ("mr", "c_cf"), (pk(2),))
                if with_ctx:
                    i1 = ((1 * 2 + which) * 8 + c) * 2
                    mm(pc[:, i1:i1 + 2], MR[:, c * 128:(c + 1) * 128], e32, True, True,
                       ("mr", "c_cf"), (pk(2),))

        def mods_finish(ngoff, with_ctx):
            pc = bank(2).rearrange("p (n two) -> p n two", two=2)
            for src in range(2 if with_ctx else 1):
                o = src * 16
                cp("dve", MODS[:, o:o + 8], pc[:, src * 16:src * 16 + 8, 0], (pk(2),), ("mods",))
                ts("dve", MODS[:, o + 8:o + 16], pc[:, src * 16 + 8:src * 16 + 16, 0], 1.0, None,
                   ALU.add, None, (pk(2),), ("mods",))
                tt("dve", MODS[:, o + 8:o + 16], MODS[:, o + 8:o + 16], NG[:, ngoff:ngoff + 8],
                   ALU.mult, ("mods", "c_ng"), ("mods",))

        def load_cast_weights(src_d, c0, ncols, dst3, dcol0, keyw):
            i = 0
            for k in range(8):
                for cc in range(0, ncols, 1024):
                    w = min(1024, ncols - cc)
                    sb_, sk_ = stage()
                    dma(sb_[:, 0:w], src_d[k * 128:(k + 1) * 128, c0 + cc:c0 + cc + w], (), sk_)
                    eng = ("act", "dve")[i % 2]
                    cp(eng, dst3[:, k, dcol0 + cc:dcol0 + cc + w], sb_[:, 0:w], sk_, (keyw,))
                    i += 1

        def finish_dbg():
            P.barrier()
            dma(dbg_d[:, 0:32], MODS, ("mods",), ())
            cp("pool", TMP[0][:, 0:512], KTA[:, 0, 0:512], (), ("tmp0",))
            dma(dbg_d[:, 512:1024], TMP[0][:, 0:512], ("tmp0",), ())
            cp("pool", TMP[1][:, 0:512], KTB[:, 1, 256:768], (), ("tmp1",))
            dma(dbg_d[:, 1024:1536], TMP[1][:, 0:512], ("tmp1",), ())
            cp("pool", TMP[2][:, 0:256], VA[:, 3, :], (), ("tmp2",))
            cp("pool", TMP[2][:, 256:512], VB[:, 3, 0:256], (), ("tmp2",))
            dma(dbg_d[:, 1536:2048], TMP[2][:, 0:512], ("tmp2",), ())
            memset("pool", TMP[3][:, 0:32], 0.0, ("tmp3",))
            cp("pool", TMP[3][:, 0:1], neglam, (), ("tmp3",))
            cp("pool", TMP[3][:, 1:2], gcol, (), ("tmp3",))
            dma(dbg_d[:, 32:64], TMP[3][:, 0:32], ("tmp3",), ())
            P.emit(nc, sems, dsems)
            return nc, P

        dma(CBF, cbf_d, (), ("c_cbf",))
        dma(CF, cf32_d, (), ("c_cf",))
        dma(GB, gbc_d[:, 0:256], (), ("c_gb",))
        dma(CT, cT_d, (), ("c_ct",))
        dma(NG, ngT_d, (), ("c_ng",))
        dma(sgcol, sgT_d, (), ("c_sg",))
        LAMT = ROPE[0][:, 0:256]
        dma(LAMT[0:1, :], lam_d, (), ("rope0",))
        memset("pool", neghalf, -0.5, ("c_nh",))
        memset("pool", LR, 0.0, ("lr",))
        if stage == "S0":
            cp("pool", MODS, GB[:, 0:32], ("c_gb",), ("mods",))
            P.emit(nc, sems, dsems)
            return nc, P

        KTA = AR.alloc(2 * 4352 * 2).rearrange("p (h n) -> p h n", h=2)
        KTB = AR.alloc(4 * 4352 * 2).rearrange("p (h n) -> p h n", h=4)
        VA = AR.alloc(NKT * 256 * 2).rearrange("p (t c) -> p t c", t=NKT)
        VB = AR.alloc(NKT * 512 * 2).rearrange("p (t c) -> p t c", t=NKT)
        ROT = AR.alloc(4096)
        QTA = AR.alloc(4096).rearrange("p (h t) -> p h t", h=4)
        QTB = AR.alloc(8192).rearrange("p (h i t) -> p h i t", h=4, i=2)
        SG = AR.alloc(8192).rearrange("p (c t) -> p c t", c=8)
        NPT = 4
        PT = [AR.alloc(2048) for _ in range(NPT)]
        GF = AR.alloc(1024, F32)
        SQ = AR.alloc(1024)
        PA = AR.alloc(1024)
        QS = [AR.alloc(1024) for _ in range(3)]
        l0_peak = AR.off
        SCf = QTB.rearrange("p h i t -> p (h i t)")[:, 0:2048].bitcast(F32)
        SCv = SCf.rearrange("p (k m) -> p k m", k=8)
        MR = SG.rearrange("p c t -> p (c t)")[:, 0:2048].bitcast(F32)
        ADB = SG.rearrange("p c t -> p (c t)")[:, 2048:4096].bitcast(F32)

        memset("pool", SCf, 0.0, ("sc",))
        memset("pool", ADB, 0.0, ("adb0", "adb32"))
        act(SCv[:, :, 0], CT[:, 0:8], AF.Silu, ("c_ct", "sc"), ("sc",))
        act(SCv[:, :, 32], CT[:, 8:16], AF.Silu, ("c_ct", "sc"), ("sc",))
        for t in range(2):
            ada_third(0, t, SCv, MR, ADB)
            cols_from_rows(MR, t, True)
        mods_finish(0, True)
        if stage == "S1":
            return finish_dbg()

        LV = LAMT[0:1, :].rearrange("p (a n) -> p a n", a=4)
        LP = LT[:, 0:2]
        tt("dve", LAMT[0:1, 0:64], LV[:, 0, :], LV[:, 1, :], ALU.mult, ("rope0",), ("rope0",))
        tt("dve", LAMT[0:1, 128:192], LV[:, 2, :], LV[:, 3, :], ALU.mult, ("rope0",), ("rope0",))
        P.add("dve", lambda e: e.reduce_sum(out=LP[:, 0:1], in_=LAMT[0:1, 0:64], axis=AX.X),
              ("rope0",), ("lp",))
        P.add("dve", lambda e: e.reduce_sum(out=LP[:, 1:2], in_=LAMT[0:1, 128:192], axis=AX.X),
              ("rope0",), ("lp",))
        act(LP, LP, AF.Exp, ("lp",), ("lp",))
        tt("dve", LR[0:1, 0:1], LP[:, 1:2], LP[:, 0:1], ALU.subtract, ("lp", "lr"), ("lr",))
        ts("dve", LR[0:1, 0:1], LR[0:1, 0:1], -LAM_INIT0, None, ALU.add, None, ("lr",), ("lr",))
        mm(bank(3)[:, 0:2], sel0, LR, True, True, ("lr", "c_cf"), (pk(3),))
        cp("dve", neglam, bank(3)[:, 0:1], (pk(3),), ("consts2",))
        ts("dve", gcol, sgcol, 1.0 - LAM_INIT0, None, ALU.mult, None, ("c_sg",), ("consts2",))

        if stage == "S2":
            return finish_dbg()
        load_cast_weights(win_d, 0, 1536, W1v, 0, "w1")
        if stage == "S3":
            return finish_dbg()

        P.barrier()
        memset("pool", QTB.rearrange("p h i t -> p (h i t)"), 0.0, ("qtb0", "qtb1"))

        def kv_post(kt, s, rb):
            b1, b2, b3 = 4 * s + 1, 4 * s + 2, 4 * s + 3
            rope = ROPE[rb]
            rk = f"rope{rb}"
            rot = ROT[:, s * 768:(s + 1) * 768]
            rkey = f"rotk{s}"
            cp("act", VA[:, kt, :], bank(b1)[:, 256:512], (pk(b1),), (f"va{kt}",))
            cp("act", VB[:, kt, :], bank(b3), (pk(b3),), (f"vb{kt}",))
            for h in range(2):
                act(junk, bank(b1)[:, h * 128:(h + 1) * 128], AF.Square, (pk(b1),), (f"ssh{h}", "junk"),
                    accum_out=SSH[:, h:h + 1])
            ts("dve", MSH[:, 0:2], SSH[:, 0:2], 1.0 / 128, EPS, ALU.mult, ALU.add,
               ("ssh0", "ssh1"), ("msh",))
            tt("pool", RSH[:, 0:2], MSH[:, 0:2], neghalf.broadcast_to([128, 2]), ALU.pow,
               ("msh", "consts"), ("rsh",))
            tt("pool", GF[:, 0:128], rope[:, 0:128], kn_bc, ALU.mult, (rk, "consts"), ("gf",))
            tt("pool", GF[:, 128:256], rope[:, 128:256], kn_bc, ALU.mult, (rk, "consts"), ("gf",))
            for h in range(2):
                stt(TMP[0][:, h * 128:(h + 1) * 128], bank(b1)[:, h * 128:(h + 1) * 128],
                    RSH[:, h:h + 1], GF[:, 0:128], ALU.mult, ALU.mult, (pk(b1), "rsh", "gf"), ("tmp0",))
                stt(TMP[1][:, h * 128:(h + 1) * 128], bank(b1)[:, h * 128:(h + 1) * 128],
                    RSH[:, h:h + 1], GF[:, 128:256], ALU.mult, ALU.mult, (pk(b1), "rsh", "gf"), ("tmp1",))
            t1 = TMP[0][:, 0:256].rearrange("p (g two f) -> p g two f", two=2, f=32)
            t2 = TMP[1][:, 0:256].rearrange("p (g two f) -> p g two f", two=2, f=32)
            ro = rot[:, 0:256].rearrange("p (g two f) -> p g two f", two=2, f=32)
            tt("pool", ro[:, :, 0, :], t1[:, :, 0, :], t2[:, :, 1, :], ALU.subtract,
               ("tmp0", "tmp1"), (rkey,))
            tt("pool", ro[:, :, 1, :], t2[:, :, 0, :], t1[:, :, 1, :], ALU.add,
               ("tmp0", "tmp1"), (rkey,))
            xb = bank(b2).rearrange("p (g d) -> p g d", g=8)
            cbb = rope[:, 256:320].unsqueeze(1).broadcast_to([128, 8, 64])
            sbb = rope[:, 320:384].unsqueeze(1).broadcast_to([128, 8, 64])
            tt("dve", TMP[2].rearrange("p (g d) -> p g d", g=8), xb, cbb, ALU.mult, (pk(b2), rk), ("tmp2",))
            tt("dve", TMP[3].rearrange("p (g d) -> p g d", g=8), xb, sbb, ALU.mult, (pk(b2), rk), ("tmp3",))
            t1 = TMP[2].rearrange("p (g two f) -> p g two f", two=2, f=16)
            t2 = TMP[3].rearrange("p (g two f) -> p g two f", two=2, f=16)
            ro = rot[:, 256:768].rearrange("p (g two f) -> p g two f", two=2, f=16)
            tt("pool", ro[:, :, 0, :], t1[:, :, 0, :], t2[:, :, 1, :], ALU.subtract,
               ("tmp2", "tmp3"), (rkey,))
            tt("pool", ro[:, :, 1, :], t2[:, :, 0, :], t1[:, :, 1, :], ALU.add,
               ("tmp2", "tmp3"), (rkey,))

        def kv_trans(kt, s):
            b0 = 4 * s + KTBANK
            rot = ROT[:, s * 768:(s + 1) * 768]
            tb = bankbf(b0)
            for j in range(6):
                tr(tb[:, j * 128:(j + 1) * 128], rot[:, j * 128:(j + 1) * 128], (f"rotk{s}", "consts"),
                   (pk(b0),))
            if DBG2 >= 1:
                cp("dve", KTA[:, :, kt * 128:(kt + 1) * 128],
                   tb[:, 0:256].rearrange("p (h t) -> p h t", h=2), (pk(b0),), (f"kta{kt}",))
            if DBG2 >= 2:
                cp("dve", KTB[:, :, kt * 128:(kt + 1) * 128],
                   tb[:, 256:768].rearrange("p (h t) -> p h t", h=4), (pk(b0),), (f"ktb{kt}_0", f"ktb{kt}_1"))

        NK1 = NKT if stage != "S4" else 3

        p1_buf = {}

        def p1_A1(kt):
            src = ctx_d[kt * 128:(kt + 1) * 128, :] if kt < 2 else x_d[(kt - 2) * 128:(kt - 1) * 128, :]
            p1_buf[kt] = fe1(src)

        def p1_rope(kt):
            dma(ROPE[kt % 2], rope_d[kt], (), (f"rope{kt % 2}",))

        def p1_A2(kt):
            s_ = kt % 2
            fe2(p1_buf[kt], s_, 16 if kt < 2 else 0, 4 * s_)

        def p1_B(kt):
            s_ = kt % 2
            for j, bnk in enumerate((4 * s_ + 1, 4 * s_ + 2, 4 * s_ + 3)):
                for k in range(8):
                    mm(bank(bnk), HT3[:, k, s_ * 128:(s_ + 1) * 128], W1v[:, k, j * 512:(j + 1) * 512],
                       k == 0, k == 7, (f"hT{s_}_{k}", "w1"), (pk(bnk),))
            kv_post(kt, s_, kt % 2)

        p1_A1(0)
        p1_A1(1)
        p1_rope(0)
        p1_A2(0)
        for kt in range(NK1):
            if kt + 2 < NK1:
                p1_A1(kt + 2)
            if kt + 1 < NK1:
                p1_rope(kt + 1)
            if kt + 1 < NK1:
                p1_A2(kt + 1)
            p1_B(kt)
            if kt >= 1:
                kv_trans(kt - 1, (kt - 1) % 2)
        kv_trans(NK1 - 1, (NK1 - 1) % 2)

        if stage in ("P1", "S4"):
            return finish_dbg()

        P.barrier()
        load_cast_weights(win_d, 1536, 2048, W1v, 0, "w1")
        P.barrier()
        OG3 = HT3
        QTBf = QTB

        def q_post(j, rb, ba, bb):
            rope = ROPE[rb]
            rk = f"rope{rb}"
            rot = ROT[:, (j % 2) * 1024:(j % 2 + 1) * 1024]
            rqk = f"rotq{j % 2}"
            for h in range(4):
                act(junk, bank(ba)[:, h * 128:(h + 1) * 128], AF.Square, (pk(ba),), (f"ssh{h}", "junk"),
                    accum_out=SSH[:, h:h + 1])
            ts("dve", MSH[:, 0:4], SSH[:, 0:4], 1.0 / 128, EPS, ALU.mult, ALU.add,
               ("ssh0", "ssh1", "ssh2", "ssh3"), ("msh",))
            tt("pool", RSH[:, 0:4], MSH[:, 0:4], neghalf.broadcast_to([128, 4]), ALU.pow,
               ("msh", "consts"), ("rsh",))
            tt("pool", GF[:, 0:128], rope[:, 0:128], qn_bc, ALU.mult, (rk, "consts"), ("gf",))
            tt("pool", GF[:, 128:256], rope[:, 128:256], qn_bc, ALU.mult, (rk, "consts"), ("gf",))
            for h in range(4):
                stt(TMP[0][:, h * 128:(h + 1) * 128], bank(ba)[:, h * 128:(h + 1) * 128],
                    RSH[:, h:h + 1], GF[:, 0:128], ALU.mult, ALU.mult, (pk(ba), "rsh", "gf"), ("tmp0",))
                stt(TMP[1][:, h * 128:(h + 1) * 128], bank(ba)[:, h * 128:(h + 1) * 128],
                    RSH[:, h:h + 1], GF[:, 128:256], ALU.mult, ALU.mult, (pk(ba), "rsh", "gf"), ("tmp1",))
            t1 = TMP[0].rearrange("p (g two f) -> p g two f", two=2, f=32)
            t2 = TMP[1].rearrange("p (g two f) -> p g two f", two=2, f=32)
            ro = rot[:, 0:512].rearrange("p (g two f) -> p g two f", two=2, f=32)
            tt("pool", ro[:, :, 0, :], t1[:, :, 0, :], t2[:, :, 1, :], ALU.subtract,
               ("tmp0", "tmp1"), (rqk,))
            tt("pool", ro[:, :, 1, :], t2[:, :, 0, :], t1[:, :, 1, :], ALU.add,
               ("tmp0", "tmp1"), (rqk,))
            xb = bank(bb).rearrange("p (g d) -> p g d", g=8)
            cbb = rope[:, 256:320].unsqueeze(1).broadcast_to([128, 8, 64])
            sbb = rope[:, 320:384].unsqueeze(1).broadcast_to([128, 8, 64])
            tt("dve", TMP[2].rearrange("p (g d) -> p g d", g=8), xb, cbb, ALU.mult, (pk(bb), rk), ("tmp2",))
            tt("dve", TMP[3].rearrange("p (g d) -> p g d", g=8), xb, sbb, ALU.mult, (pk(bb), rk), ("tmp3",))
            t1 = TMP[2].rearrange("p (g two f) -> p g two f", two=2, f=16)
            t2 = TMP[3].rearrange("p (g two f) -> p g two f", two=2, f=16)
            ro = rot[:, 512:1024].rearrange("p (g two f) -> p g two f", two=2, f=16)
            tt("pool", ro[:, :, 0, :], t1[:, :, 0, :], t2[:, :, 1, :], ALU.subtract,
               ("tmp2", "tmp3"), (rqk,))
            tt("pool", ro[:, :, 1, :], t2[:, :, 0, :], t1[:, :, 1, :], ALU.add,
               ("tmp2", "tmp3"), (rqk,))

        def q_trans(j, bt):
            rot = ROT[:, (j % 2) * 1024:(j % 2 + 1) * 1024]
            rqk = f"rotq{j % 2}"
            tb = bankbf(bt)
            for c in range(8):
                tr(tb[:, c * 128:(c + 1) * 128], rot[:, c * 128:(c + 1) * 128], (rqk, "consts"),
                   (pk(bt),))
            cp("dve", QTA[:, :, j * 128:(j + 1) * 128],
               tb[:, 0:512].rearrange("p (h t) -> p h t", h=4), (pk(bt),), ("qta",))
            tbb = tb[:, 512:1024].rearrange("p (h t) -> p h t", h=4)
            cp("dve", QTB[0:64, :, 0, j * 128:(j + 1) * 128], tbb[0:64], (pk(bt),), ("qtb0",))
            cp("dve", QTB[64:128, :, 1, j * 128:(j + 1) * 128], tbb[64:128], (pk(bt),), ("qtb1",))

        maps = [("B", h, i) for h in range(4) for i in range(2)] + [("A", h, 0) for h in range(4)]
        SCALE_A = 128.0 ** -0.5
        SCALE_B = 64.0 ** -0.5

        def attention_block(qb):
            seq = [(mi, pr) for mi in range(len(maps)) for pr in range(NKT // 2)]

            def qk(idx):
                mi, pr = seq[idx]
                kind, h, i = maps[mi]
                sb = 2 * (idx % 2)
                for t in range(2):
                    kt = 2 * pr + t
                    if kind == "A":
                        lhsT = KTA[:, h // 2, kt * 128:(kt + 1) * 128]
                        rhs = QTA[:, h, :]
                        r = (f"kta{kt}", "qta")
                    else:
                        lhsT = KTB[:, h, kt * 128:(kt + 1) * 128]
                        rhs = QTB[:, h, i, :]
                        r = (f"ktb{kt}_{h % 2}", f"qtb{i}")
                    mm(bank(sb + t), lhsT, rhs, True, True, r, (pk(sb + t),))

            def ex(idx):
                mi, pr = seq[idx]
                kind = maps[mi][0]
                sb = 2 * (idx % 2)
                pt = PT[idx % NPT]
                act(pt.rearrange("p (t n) -> p t n", t=2), ps[:, sb:sb + 2, :], AF.Exp,
                    (pk(sb), pk(sb + 1)), (f"pt{idx % NPT}",),
                    scale=(SCALE_A if kind == "A" else SCALE_B))

            def pv(idx):
                mi, pr = seq[idx]
                kind, h, i = maps[mi]
                ob = 4 + 2 * (mi % 2)
                pt = PT[idx % NPT]
                for t in range(2):
                    kt = 2 * pr + t
                    if kind == "A":
                        lhsT = VA[:, kt, (h // 2) * 128:(h // 2 + 1) * 128]
                        r = (f"va{kt}", f"pt{idx % NPT}")
                    else:
                        lhsT = VB[:, kt, h * 128:(h + 1) * 128]
                        r = (f"vb{kt}", f"pt{idx % NPT}")
                    mm(bank(ob), lhsT, pt[:, t * 512:(t + 1) * 512], kt == 0, kt == NKT - 1, r, (pk(ob),))

            def finish(mi):
                kind, h, i = maps[mi]
                ob = 4 + 2 * (mi % 2)
                T = TMP[mi % 2]
                tk = f"tmp{mi % 2}"
                P.add("dve", lambda e: e.reciprocal(out=T, in_=bank(ob + 1)), (pk(ob + 1),), (tk,))
                tt("dve", T, bank(ob), T, ALU.mult, (pk(ob), tk), (tk,))
                if kind == "A":
                    tt("pool", SG[:, h, :], T, SG[:, h, :], ALU.mult, (tk, f"sg{h}"), (f"sg{h}",))
                elif i == 1:
                    T0, T1 = TMP[0], TMP[1]
                    bssq = ob + 1

                    def f1():
                        stt(T0, T1, neglam, T0, ALU.mult, ALU.add, ("tmp0", "tmp1", "consts2"), ("tmp0",))
                        tt("pool", SQ, T0, T0, ALU.mult, ("tmp0",), ("sq",))

                    def f2():
                        mm(bank(bssq), onesb, SQ, True, True, ("sq", "consts"), (pk(bssq),))
                        ts("dve", T1, bank(bssq), 1.0 / 128, EPS, ALU.mult, ALU.add, (pk(bssq),), ("tmp1",))

                    def f3():
                        act(T1, T1, AF.Ln, ("tmp1",), ("tmp1",))
                        act(T1, T1, AF.Exp, ("tmp1",), ("tmp1",), scale=-0.5)

                    def f4():
                        stt(T0, T0, gcol, T1, ALU.mult, ALU.mult, ("tmp0", "tmp1", "consts2"), ("tmp0",))
                        tt("pool", SG[:, 4 + h, :], T0, SG[:, 4 + h, :], ALU.mult, ("tmp0", f"sg{4 + h}"),
                           (f"sg{4 + h}",))

                    return [f1, f2, f3, f4]
                return []

            def psum2(idx):
                mi, pr = seq[idx]
                pt = PT[idx % NPT]
                q = pr // 2
                if pr == NKT // 2 - 1:
                    tt("dve", QS[q % 3], pt[:, 0:512], pt[:, 512:1024], ALU.add, (f"pt{idx % NPT}",),
                       (f"qs{q % 3}",))
                elif pr % 2 == 0:
                    tt("dve", PA, pt[:, 0:512], pt[:, 512:1024], ALU.add, (f"pt{idx % NPT}",), ("pa",))
                else:
                    tt("dve", QS[q % 3], pt[:, 0:512], PA, ALU.add, (f"pt{idx % NPT}", "pa"), (f"qs{q % 3}",))
                    tt("dve", QS[q % 3], QS[q % 3], pt[:, 512:1024], ALU.add, (f"pt{idx % NPT}", f"qs{q % 3}"),
                       (f"qs{q % 3}",))

            def den(idx):
                mi, pr = seq[idx]
                ob = 4 + 2 * (mi % 2)
                q = pr // 2
                mm(bank(ob + 1), onesb, QS[q % 3], pr == 1, pr == NKT // 2 - 1,
                   (f"qs{q % 3}", "consts"), (pk(ob + 1),))

            n = len(seq)
            deferred = {}
            DD = 3
            qk(0)
            qk(1)
            for idx in range(n):
                for f in deferred.pop(idx, ()):
                    f()
                ex(idx)
                psum2(idx)
                if idx + 2 < n:
                    qk(idx + 2)
                mi, pr = seq[idx]
                pv(idx)
                if pr % 2 == 1 or pr == NKT // 2 - 1:
                    deferred.setdefault(idx + DD, []).append(lambda idx=idx: den(idx))
                if pr == NKT // 2 - 1:
                    def fin(mi=mi, base=idx + DD):
                        for k, f in enumerate(finish(mi)):
                            deferred.setdefault(base + 2 + 2 * k, []).append(f)
                    deferred.setdefault(idx + DD, []).append(fin)
            while deferred:
                k = min(deferred)
                for f in deferred.pop(k):
                    f()

        NQB = 8 if stage != "L0s" else 1
        fb2 = {}

        def A2a(j, qb):
            ti = qb * 4 + j
            fb2[(qb, j)] = fe1(x_d[ti * 128:(ti + 1) * 128, :])

        def R2(j, qb):
            ti = qb * 4 + j
            dma(ROPE[ti % 2], rope_d[2 + ti], (), (f"rope{ti % 2}",))

        for qb in range(NQB):
            def A2b(j, qb=qb):
                fe2(fb2[(qb, j)], j, 0, 4 + (j % 2))

            def B2(j, qb=qb):
                ti = qb * 4 + j
                ba, bb = 2 * (j % 2), 2 * (j % 2) + 1
                for jj, bnk in enumerate((ba, bb)):
                    for k in range(8):
                        mm(bank(bnk), HT3[:, k, j * 128:(j + 1) * 128], W1v[:, k, jj * 512:(jj + 1) * 512],
                           k == 0, k == 7, (f"hT{j}_{k}", "w1"), (pk(bnk),))
                q_post(j, ti % 2, ba, bb)

            def C2(j):
                q_trans(j, 6 + (j % 2))

            def G2(c0, c1):
                for c in range(c0, c1):
                    bnk = 4 + (c % 2)
                    for k in range(8):
                        mm(bank(bnk), W1v[:, k, 1024 + c * 128:1024 + (c + 1) * 128], HT3[:, k, :],
                           k == 0, k == 7, tuple(f"hT{j}_{k}" for j in range(4)) + ("w1",), (pk(bnk),))
                    act(SG[:, c, :], bank(bnk), AF.Silu, (pk(bnk),), (f"sg{c}",))

            if qb == 0:
                A2a(0, qb); R2(0, qb); A2a(1, qb); R2(1, qb)
            A2b(0); A2a(2, qb); A2b(1); B2(0); R2(2, qb); A2a(3, qb); A2b(2); B2(1); R2(3, qb)
            C2(0); A2b(3); B2(2); C2(1)
            G2(0, 4); B2(3); C2(2); G2(4, 8); C2(3)
            if qb + 1 < NQB:
                A2a(0, qb + 1); R2(0, qb + 1); A2a(1, qb + 1); R2(1, qb + 1)
            attention_block(qb)
            for c4 in range(4):
                dma(og_d.rearrange("(c p) n -> p c n", p=128)[:, 2 * c4:2 * c4 + 2, qb * 512:(qb + 1) * 512],
                    SG[:, 2 * c4:2 * c4 + 2, :], (f"sg{2 * c4}", f"sg{2 * c4 + 1}"), ("ogd",))

        if stage in ("L0", "L0s"):
            P.emit(nc, sems, dsems)
            return nc, P


        P.barrier()
        AR.off = mark_generic
        WO = AR.alloc(16384)
        WO3 = WO.rearrange("p (k c) -> p k c", k=8)
        FC = AR.alloc(2560)
        F1 = FC[:, 0:256]
        CS3 = FC[:, 256:1280].rearrange("p (c n) -> p c n", c=2)
        GRH = [TMP[0], TMP[1]]
        PB = [AR.alloc(1024) for _ in range(4)]
        GG = AR.alloc(16384).rearrange("p (c n) -> p c n", c=2)
        YY = AR.alloc(32768)
        FG = AR.alloc(65536).rearrange("p (c n) -> p c n", c=8)
        L1X = AR.alloc(4096, F32)
        STG[:] = [(XT[0], ("xt0",)), (XT[1], ("xt1",)), (L1X, ("l1x",))]
        YYf = YY
        SCf1 = YYf[:, 0:2048].bitcast(F32)
        SCv1 = SCf1.rearrange("p (k m) -> p k m", k=8)
        MR1 = YYf[:, 2048:4096].bitcast(F32)
        ADB1 = YYf[:, 4096:6144].bitcast(F32)
        OGB = YYf[:, 6144:10240].rearrange("p (c t) -> p c t", c=8)
        X1T = [YYf[:, 10240 + i * 2048:10240 + (i + 1) * 2048].bitcast(F32) for i in range(2)]
        UO = [YYf[:, 14336 + i * 1024:14336 + (i + 1) * 1024] for i in range(2)]
        FGf = FG.rearrange("p c n -> p (c n)")
        GO = FGf[:, 0:4096].rearrange("p (c t) -> p c t", c=8)
        X1T = [FGf[:, 4096 + i * 2048:4096 + (i + 1) * 2048].bitcast(F32) for i in range(4)]
        OGBS = [OGB, FGf[:, 12288:16384].rearrange("p (c t) -> p c t", c=8)]
        gen_off = mark_persist
        TTZ = AR.ap[:, (mark_persist + 63) // 64 * 64 // 2:(mark_persist + 63) // 64 * 64 // 2 + 8192]
        TT5 = TTZ.rearrange("p (a k w h) -> p a k w h", a=2, k=64, w=2)

        memset("pool", SCf1, 0.0, ("sc",))
        memset("pool", ADB1, 0.0, ("adb0", "adb32"))
        act(SCv1[:, :, 0], CT[:, 0:8], AF.Silu, ("sc",), ("sc",))
        fg_bc = AR.ap[:, (tmp_off + 4096) // 2:(tmp_off + 8192) // 2].bitcast(F32)
        dma(fg_bc, gbc_d[:, 256:1280], (), ("fgbc",))
        dma(FC[:, 0:256], f1_d, (), ("fc1",))
        dma(FC[:, 256:1280], cs_d, (), ("fc2",))

        def gate_weights(l, src_d):
            ada_third(l, 2, SCv1, MR1, ADB1)
            for hf in range(2):
                mm(bank(4 + hf), sel0, MR1[:, hf * 512:(hf + 1) * 512], True, True, ("mr",), (pk(4 + hf),))

        def fold_gate_into(src_d, keyw):
            for k in range(8):
                sb_, sk_ = stage()
                dma(sb_, src_d[k * 128:(k + 1) * 128, :], (), sk_)
                for hf in range(2):
                    tt("dve", WO3[:, k, hf * 512:(hf + 1) * 512], sb_[:, hf * 512:(hf + 1) * 512],
                       bank(4 + hf), ALU.mult, sk_ + (pk(4 + hf),), (keyw,))

        for t in range(2):
            ada_third(1, t, SCv1, MR1, ADB1)
            cols_from_rows(MR1, t, False)
        mods_finish(8, False)
        ada_third(1, 2, SCv1, MR1, ADB1)
        for hf in range(2):
            cp("dve", GRH[hf], MR1[:, hf * 512:(hf + 1) * 512], ("mr",), ("gr",))
        gate_weights(0, wout_d)
        fold_gate_into(wout_d, "wo")
        load_cast_weights(fin_d, 0, 2048, W1v, 0, "w1")
        P.barrier()

        def ogb_load(qb):
            for c4 in range(4):
                dma(OGBS[qb % 2][:, 2 * c4:2 * c4 + 2, :],
                    og_d.rearrange("(c p) n -> p c n", p=128)[:, 2 * c4:2 * c4 + 2, qb * 512:(qb + 1) * 512],
                    ("ogd",), (f"ogb{qb % 2}_{c4}",))

        def x1_load(ti):
            dma(X1T[ti % 4], x_d[ti * 128:(ti + 1) * 128, :], (), (f"x1t{ti % 4}",))

        ogb_load(0)
        x1_load(0)
        x1_load(1)
        for qb in range(8):
            if qb + 1 < 8:
                ogb_load(qb + 1)
            OGB = OGBS[qb % 2]

            fb1 = {}

            def F1b(j, fb1=fb1):
                fe2(fb1[j], j, 0, 2 + (j % 2))

            def O1(j, qb=qb, fb1=fb1, OGB=OGB):
                ti = qb * 4 + j
                xb = ti % 4
                if ti + 2 < 32:
                    x1_load(ti + 2)
                for hf in range(2):
                    for c in range(8):
                        mm(bank(hf), OGB[:, c, j * 128:(j + 1) * 128], WO3[:, c, hf * 512:(hf + 1) * 512],
                           c == 0, c == 7, (f"ogb{qb % 2}_{c // 2}", "wo"), (pk(hf),))
                for hf in range(2):
                    tt("dve", X1T[xb][:, hf * 512:(hf + 1) * 512], bank(hf), X1T[xb][:, hf * 512:(hf + 1) * 512],
                       ALU.add, (pk(hf), f"x1t{xb}"), (f"x1t{xb}",))
                dma(x1_d[ti * 128:(ti + 1) * 128, :], X1T[xb], (f"x1t{xb}",), ("x1d",), q="pool")
                fb1[j] = fe1(None, sb=(X1T[xb], f"x1t{xb}"))

            def B1(j, qb=qb):
                ti = qb * 4 + j
                xb = ti % 2
                for hf in range(2):
                    for k in range(8):
                        mm(bank(4 + hf), HT3[:, k, j * 128:(j + 1) * 128], W1v[:, k, hf * 512:(hf + 1) * 512],
                           k == 0, k == 7, (f"hT{j}_{k}", "w1"), (pk(4 + hf),))
                    cp("act" if hf == 0 else "dve", UO[xb][:, hf * 512:(hf + 1) * 512], bank(4 + hf),
                       (pk(4 + hf),), (f"uo{xb}_{hf}",))
                dma(u_d[ti * 128:(ti + 1) * 128, :], UO[xb], (f"uo{xb}_0", f"uo{xb}_1"), ("ud",), q="pool")

            def G1(c0, c1):
                for c in range(c0, c1):
                    bnk = 6 + (c % 2)
                    for k in range(8):
                        mm(bank(bnk), W1v[:, k, 1024 + c * 128:1024 + (c + 1) * 128], HT3[:, k, :],
                           k == 0, k == 7, tuple(f"hT{j}_{k}" for j in range(4)) + ("w1",), (pk(bnk),))
                    act(GO[:, c, :], bank(bnk), AF.Silu, (pk(bnk),), ("go",))

            O1(0); O1(1); F1b(0); O1(2); F1b(1); B1(0); O1(3); F1b(2); B1(1); F1b(3); B1(2)
            G1(0, 4); B1(3); G1(4, 8)
            for c4 in range(4):
                dma(g_d.rearrange("(c p) n -> p c n", p=128)[:, 2 * c4:2 * c4 + 2, qb * 512:(qb + 1) * 512],
                    GO[:, 2 * c4:2 * c4 + 2, :], ("go",), ("gd",))

        P.barrier()
        dma(TTZ, tt_d, (), ("ttz",))
        UG = [W1[:, i * 8192:(i + 1) * 8192].rearrange("p (l c) -> p l c", l=32) for i in range(2)]
        Y5 = YY.rearrange("p (c k l r) -> p c k l r", c=2, k=128, l=32)
        u_v = u_d.rearrange("(nh nl) c -> nh nl c", nl=32)
        g_v = g_d.rearrange("(c p) n -> p c n", p=128)
        ev = [0]
        for gr in range(4):
            ub = gr % 2
            for n8 in range(8):
                dma(UG[ub][:, 4 * n8:4 * n8 + 4, :], u_v[:, 4 * n8:4 * n8 + 4, gr * 256:(gr + 1) * 256],
                    ("ud",), (f"ug{ub}_{n8}",))
            dma(GG, g_v[:, 2 * gr:2 * gr + 2, :], ("gd",), ("gg",))
            for nl in range(32):
                bnk = nl % 2
                for cc in range(2):
                    mm(bank(bnk)[:, cc * 256:(cc + 1) * 256], UG[ub][:, nl, cc * 128:(cc + 1) * 128], F1,
                       True, True, (f"ug{ub}_{nl // 4}", "fc1"), (pk(bnk),))
                cp("act" if nl % 4 != 3 else "dve", Y5[:, :, :, nl, :],
                   bank(bnk).rearrange("p (c k r) -> p c k r", c=2, r=2), (pk(bnk),), ("yy",))
            def chdft(kp, gr=gr):
                pbk = (2, 3, 6)[kp % 3]
                pb = PB[kp % 4]
                for cc in range(2):
                    mm(bank(pbk), Y5[:, cc, 2 * kp:2 * kp + 2, :, :].rearrange("p a l r -> p (a l r)"),
                       CS3[:, cc, :], cc == 0, cc == 1, ("yy", "fc2"), (pk(pbk),))
                cp("act" if kp % 3 != 2 else "dve", pb, bank(pbk), (pk(pbk),), (f"pb{kp % 4}",))

            def stage2(kp, gr=gr):
                pb = PB[kp % 4]
                fb = 4 + ((kp // 4) % 2)
                for par in range(2):
                    for mc in range(2):
                        col = (((kp % 4) * 2 + par) * 2 + mc) * 32
                        mm(bank(fb)[:, col:col + 32], pb[:, mc * 128:(mc + 1) * 128], TT5[:, par, kp, 0, :],
                           True, False, (f"pb{kp % 4}", "ttz"), (pk(fb),))
                        mm(bank(fb)[:, col:col + 32], pb[:, 256 + mc * 128:256 + (mc + 1) * 128],
                           TT5[:, par, kp, 1, :], False, True, (f"pb{kp % 4}", "ttz"), (pk(fb),))
                if kp % 4 == 3:
                    k0 = 2 * (kp - 3)
                    fbv = bank(fb).rearrange("p (a m h) -> p a m h", m=2, h=32)
                    for mc in range(2):
                        gv = GG[:, mc, :].rearrange("p (h l) -> p h l", l=128)[:, :, k0:k0 + 8]
                        ov = FG[:, 2 * gr + mc, :].rearrange("p (h l) -> p h l", l=128)[:, :, k0:k0 + 8]
                        tt("dve", ov, fbv[:, :, mc, :].rearrange("p a h -> p h a"), gv, ALU.mult,
                           (pk(fb), "gg"), ("fg",))

            chdft(0)
            chdft(1)
            for kp in range(64):
                if kp + 2 < 64:
                    chdft(kp + 2)
                stage2(kp)

        P.barrier()
        for hf in range(2):
            mm(bank(4 + hf), sel0, GRH[hf], True, True, ("gr",), (pk(4 + hf),))
        fold_gate_into(fout_d, "wo")
        P.barrier()
        ZT = [YY[:, i * 2048:(i + 1) * 2048].bitcast(F32) for i in range(4)]

        def c_L(ti):
            zb = ti % 4
            dma(ZT[zb], x1_d[ti * 128:(ti + 1) * 128, :], ("x1d",), (f"zt{zb}",))

        def c_X(ti):
            zb = ti % 4
            bo = 2 * (ti % 2)
            for hf in range(2):
                for c in range(8):
                    mm(bank(bo + hf), FG[:, c, ti * 128:(ti + 1) * 128], WO3[:, c, hf * 512:(hf + 1) * 512],
                       c == 0, c == 7, ("fg", "wo"), (pk(bo + hf),))
            for hf in range(2):
                tt("dve", ZT[zb][:, hf * 512:(hf + 1) * 512], bank(bo + hf), ZT[zb][:, hf * 512:(hf + 1) * 512],
                   ALU.add, (pk(bo + hf), f"zt{zb}"), (f"zt{zb}",))

        def c_Ya(ti):
            zb = ti % 4
            sl = ti % 4
            act(XN[ti % 2], ZT[zb], AF.Square, (f"zt{zb}",), (f"xn{ti % 2}", f"ssx{sl}"), accum_out=SSX[:, sl:sl + 1])
            ts("dve", MSX[:, sl:sl + 1], SSX[:, sl:sl + 1], 1.0 / D, EPS, ALU.mult, ALU.add,
               (f"ssx{sl}",), (f"msx{sl}",))
            tt("pool", RSX[:, sl:sl + 1], MSX[:, sl:sl + 1], neghalf, ALU.pow, (f"msx{sl}",), (f"rsx{sl}",))

        def c_Yb(ti):
            zb = ti % 4
            sl = ti % 4
            stt(ZT[zb], ZT[zb], RSX[:, sl:sl + 1], fg_bc, ALU.mult, ALU.mult, (f"zt{zb}", f"rsx{sl}", "fgbc"), (f"zt{zb}",))
            dma(out_d[ti * 128:(ti + 1) * 128, :], ZT[zb], (f"zt{zb}",), ("outd",), q="pool")

        c_L(0)
        c_L(1)
        c_X(0)
        for ti in range(32):
            if ti + 2 < 32:
                c_L(ti + 2)
            if ti + 1 < 32:
                c_X(ti + 1)
            c_Ya(ti)
            if ti >= 1:
                c_Yb(ti - 1)
        c_Yb(31)

        P.emit(nc, sems, dsems)
        return nc, P


def _in_maps(inp):
    C = _consts()
    f = lambda a: np.ascontiguousarray(np.asarray(a, dtype=np.float32))
    x = f(inp["x"]); c = f(inp["c"]); ctx = f(inp["ctx"]); c_ctx = f(inp["c_ctx"])
    norm_g = f(inp["norm_g"])
    ngT = np.concatenate([norm_g[0].reshape(8, 128).T, norm_g[1].reshape(8, 128).T], axis=1)
    gbc = np.concatenate([np.broadcast_to(f(inp["attn_qn_g"])[0][None, :], (128, 128)),
                          np.broadcast_to(f(inp["attn_kn_g"])[0][None, :], (128, 128)),
                          np.broadcast_to(f(inp["final_g"])[None, :], (128, 1024))], axis=1)
    lam = np.concatenate([f(inp["lam_q1"])[0], f(inp["lam_k1"])[0], f(inp["lam_q2"])[0],
                          f(inp["lam_k2"])[0]])[None, :]
    shared = dict(
        ada_w=f(inp["ada_w"]), ada_b=f(inp["ada_b"]), ngT=np.ascontiguousarray(ngT),
        win=f(inp["attn_in_w"])[0], wout=f(inp["attn_out_w"])[0], fin=f(inp["fourier_in_w"])[0],
        fout=f(inp["fourier_out_w"])[0], gbc=np.ascontiguousarray(gbc), lam=np.ascontiguousarray(lam),
        sgT=np.ascontiguousarray(f(inp["attn_subln_g"])[0][:, None]),
        cf32=C["cf32"], cbf=C["cbf"], rope=C["rope"], f1=C["f1"], cs=C["cs"], tt=C["tt"])
    maps = []
    for b in range(N_CORES):
        cT = np.concatenate([c[b].reshape(8, 128).T, c_ctx.reshape(8, 128).T], axis=1)
        m = dict(shared)
        m.update(x=x[b], ctx=ctx[b], cT=np.ascontiguousarray(cT))
        maps.append(m)
    return maps


_PROG = {}


def kernel(**inputs):
    if "full" not in _PROG:
        _PROG["full"] = build_program("full")[0]
    nc = _PROG["full"]
    res = run_bass_kernel_spmd(nc, _in_maps(inputs), core_ids=list(range(N_CORES)))
    out = np.stack([np.asarray(r["out"], dtype=np.float32) for r in res.results], axis=0)
    return out
```

```python
import math
import contextlib
import numpy as np
import ml_dtypes
import concourse.bass as bass
import concourse.mybir as mybir
from concourse.bass_utils import run_bass_kernel_spmd

F32 = mybir.dt.float32
F32R = mybir.dt.float32r
BF16 = mybir.dt.bfloat16
AF = mybir.ActivationFunctionType
ALU = mybir.AluOpType
AX = mybir.AxisListType
NPBF = ml_dtypes.bfloat16

S = 4096
D = 1024
CTX = 256
NKT = 34
EPS = 1e-6
LAM_INIT0 = 0.8 - 0.6 * math.exp(-0.3 * 0)
N_CORES = 8
ARENA_BYTES = 212736
import os
DBGL = int(os.environ.get('KDBG', '9'))
DBG2 = int(os.environ.get('KDBG2', '2'))
KTBANK = int(os.environ.get('KTBANK', '0'))


class Prog:
    ENGS = ("pe", "act", "dve", "pool", "sp")

    def __init__(self):
        self.ops = []
        self.lw = {}
        self.rd = {}
        self.bar_deps = set()
        self.bar_done = set(self.ENGS)
        self.last_on = {}
        self.dma_since = []

    def add(self, eng, fn, r=(), w=(), dma=False):
        i = len(self.ops)
        deps = {}
        for k in r:
            j = self.lw.get(k)
            if j is not None:
                deps[j] = True
        for k in w:
            j = self.lw.get(k)
            if j is not None:
                deps.setdefault(j, False)
            for j in self.rd.get(k, ()):
                deps.setdefault(j, False)
        if eng not in self.bar_done:
            for j in self.bar_deps:
                deps.setdefault(j, True)
            self.bar_done.add(eng)
        self.ops.append([eng, fn, deps, dma])
        for k in r:
            lst = self.rd.setdefault(k, [])
            if not dma:
                lst[:] = [j for j in lst if self.ops[j][3] or self.ops[j][0] != eng]
            lst.append(i)
        for k in w:
            self.lw[k] = i
            self.rd[k] = []
        self.last_on[eng] = i
        if dma:
            self.dma_since.append(i)
        return i

    def barrier(self):
        deps = set(self.last_on.values()) | set(self.dma_since)
        if len(self.bar_done) < len(self.ENGS):
            deps |= self.bar_deps
        self.bar_deps = deps
        self.bar_done = set()
        self.dma_since = []

    def emit(self, nc, sems, dsems, final_wait_all=True):
        ops = self.ops
        ms = set()
        for i, op in enumerate(ops):
            eng, fn, deps, dma = op
            nd = []
            for j, raw in deps.items():
                ej, _, _, dj = ops[j]
                if (not dj) and ej == eng:
                    if eng == "pe":
                        continue
                nd.append(j)
            op[2] = nd
            for j in nd:
                ms.add(j)
        val = {}
        prev = {}
        cnt = {e: 0 for e in self.ENGS}
        dcnt = {e: 0 for e in self.ENGS}
        for i, (eng, fn, deps, dma) in enumerate(ops):
            if dma:
                n = dcnt[eng]
                K = len(dsems[eng])
                sem = dsems[eng][n % K]
                val[i] = (sem, 16 * (n // K + 1))
                if n >= K:
                    prev[i] = (sem, 16 * (n // K))
                dcnt[eng] = n + 1
            elif i in ms:
                cnt[eng] += 1
                val[i] = (sems[eng], cnt[eng])
        self.stats = dict(n_ops=len(ops), milestones=dict(cnt), dmas=dict(dcnt))

        def run(eng, e):
            waited = {}

            def wait(sem, v):
                if waited.get(id(sem), 0) < v:
                    e.wait_ge(sem, v)
                    waited[id(sem)] = v

            for i, (en, fn, deps, dma) in enumerate(ops):
                if en != eng:
                    continue
                for j in deps:
                    wait(*val[j])
                if i in prev:
                    wait(*prev[i])
                ins = fn(e)
                if i in val:
                    sem, v = val[i]
                    ins.then_inc(sem, 16 if dma else 1)
            if eng == "sp" and final_wait_all:
                for q in self.ENGS:
                    n = dcnt[q]
                    K = len(dsems[q])
                    for s_i in range(min(n, K)):
                        uses = (n - 1 - s_i) // K + 1
                        wait(dsems[q][s_i], 16 * uses)
                for q in ("pe", "act", "dve", "pool"):
                    if cnt[q] > 0:
                        wait(sems[q], cnt[q])

        with nc.Block() as block:
            @block.tensor
            def _(e):
                run("pe", e)

            @block.scalar
            def _(e):
                run("act", e)

            @block.vector
            def _(e):
                run("dve", e)

            @block.gpsimd
            def _(e):
                run("pool", e)

            @block.sync
            def _(e):
                run("sp", e)


class Arena:
    def __init__(self, ap, nbytes):
        self.ap = ap
        self.cap = nbytes
        self.off = 0
        self.peak = 0

    def alloc(self, nbytes, dtype=BF16):
        off = (self.off + 63) // 64 * 64
        assert off + nbytes <= self.cap, f"arena overflow: {off}+{nbytes} > {self.cap}"
        self.off = off + nbytes
        self.peak = max(self.peak, self.off)
        v = self.ap[:, off // 2:(off + nbytes) // 2]
        if dtype == F32:
            v = v.bitcast(F32)
        return v


def _rope_tables():
    tab = np.zeros((NKT, 128, 384), np.float32)
    tab[:2, :, 0:128] = 1.0
    tab[:2, :, 256:320] = 1.0
    n = np.arange(S)
    rows = (n // 64).astype(np.float32)
    cols = (n % 64).astype(np.float32)

    def cs(dim):
        q = dim // 4
        inv = (np.float32(10000.0) ** (-(np.arange(q, dtype=np.float32) / np.float32(q)))).astype(np.float32)
        ang = np.stack([rows[:, None] * inv, cols[:, None] * inv], axis=1).astype(np.float32)
        c = np.cos(ang).astype(np.float32)
        s = np.sin(ang).astype(np.float32)
        ce = np.broadcast_to(c[:, :, None, :], (S, 2, 2, q)).reshape(S, dim)
        se = np.broadcast_to(s[:, :, None, :], (S, 2, 2, q)).reshape(S, dim)
        return ce, se

    ca, sa = cs(128)
    cb, sb = cs(64)
    full = np.concatenate([ca, sa, cb, sb], axis=1).reshape(32, 128, 384)
    tab[2:] = full
    return tab


def _fourier_tables():
    nh = np.arange(128)[:, None].astype(np.float64)
    kl = np.arange(128)[None, :].astype(np.float64)
    ang = 2 * np.pi * nh * kl / 128.0
    norm = 1.0 / math.sqrt(4096.0 * 256.0)
    f1 = np.zeros((128, 128, 2))
    f1[:, :, 0] = np.cos(ang) * norm
    f1[:, :, 1] = -np.sin(ang) * norm
    f1 = f1.reshape(128, 256)
    j = (np.arange(2)[None, :, None] * 128 + np.arange(128)[:, None, None]).astype(np.float64)
    m = np.arange(256)[None, None, :].astype(np.float64)
    a2 = 2 * np.pi * j * m / 256.0
    cs = np.concatenate([np.cos(a2), np.sin(a2)], axis=2).reshape(128, 1024)
    par = np.arange(2)[:, None, None, None, None, None]
    nlo = np.arange(32)[None, :, None, None, None, None].astype(np.float64)
    ri = np.arange(2)[None, None, :, None, None, None]
    kp = np.arange(64)[None, None, None, :, None, None]
    wh = np.arange(2)[None, None, None, None, :, None]
    khi = np.arange(32)[None, None, None, None, None, :]
    k = (2 * kp + par) + 128 * khi
    ang3 = 2 * np.pi * nlo * k / 4096.0
    tr = np.cos(ang3)
    ti = -np.sin(ang3)
    shape = (2, 32, 2, 64, 2, 32)
    tr = np.broadcast_to(tr, shape)
    ti = np.broadcast_to(ti, shape)
    rib = np.broadcast_to(ri, shape)
    whb = np.broadcast_to(wh, shape)
    t = np.where(whb == 0, np.where(rib == 0, tr, -ti), np.where(rib == 0, ti, tr))
    tt = t.reshape(128, 64 * 2 * 32)
    ttz = np.zeros((128, 2, 64 * 2 * 32))
    ttz[0:64, 0, :] = tt[0:64]
    ttz[64:128, 1, :] = tt[64:128]
    return f1.astype(NPBF), cs.astype(NPBF), ttz.reshape(128, 8192).astype(NPBF)


_CONSTS = {}


def _consts():
    if _CONSTS:
        return _CONSTS
    cf32 = np.zeros((128, 388), np.float32)
    cf32[:, 260:388] = 1.0
    cf32[0, 0:128] = 1.0
    cf32[32, 128:256] = 1.0
    cf32[0, 256] = 1.0
    cf32[32, 258] = 1.0
    cbf = np.zeros((128, 256), np.float32)
    cbf[:, 0:128] = np.eye(128, dtype=np.float32)
    cbf[:, 128:256] = 1.0
    f1, cs, tt = _fourier_tables()
    _CONSTS.update(cf32=cf32, cbf=cbf.astype(NPBF), rope=_rope_tables(), f1=f1, cs=cs, tt=tt)
    return _CONSTS


def build_program(stage="full"):
    nc = bass.Bass("TRN2", target_bir_lowering=False)
    P = Prog()

    def din(name, shape, dt=F32):
        return nc.dram_tensor(name, list(shape), dt, kind="ExternalInput").ap()

    def dint(name, shape, dt, ext=False):
        return nc.dram_tensor(name, list(shape), dt,
                              kind=("ExternalOutput" if ext else "Internal")).ap()

    x_d = din("x", [S, D])
    ctx_d = din("ctx", [CTX, D])
    cT_d = din("cT", [128, 16])
    adaw_d = din("ada_w", [2, D, 3 * D])
    adab_d = din("ada_b", [2, 3 * D])
    ngT_d = din("ngT", [128, 16])
    win_d = din("win", [D, 3584])
    wout_d = din("wout", [D, D])
    fin_d = din("fin", [D, 2 * D])
    fout_d = din("fout", [D, D])
    gbc_d = din("gbc", [128, 1280])
    lam_d = din("lam", [1, 256])
    sgT_d = din("sgT", [128, 1])
    cf32_d = din("cf32", [128, 388])
    cbf_d = din("cbf", [128, 256], BF16)
    rope_d = din("rope", [NKT, 128, 384])
    f1_d = din("f1", [128, 256], BF16)
    cs_d = din("cs", [128, 1024], BF16)
    tt_d = din("tt", [128, 8192], BF16)
    out_d = nc.dram_tensor("out", [S, D], F32, kind="ExternalOutput").ap()
    og_d = dint("ogd", [D, S], BF16, ext=(stage in ("L0", "L0s")))
    x1_d = dint("x1d", [S, D], F32)
    u_d = dint("ud", [S, D], BF16)
    g_d = dint("gd", [D, S], BF16)
    dbg_d = dint("dbg", [128, 2048], F32, ext=True) if stage[0] in "PS" else None

    es = contextlib.ExitStack()
    with es:
        arena_t = es.enter_context(nc.sbuf_tensor("arena", [128, ARENA_BYTES // 2], BF16))
        ps = es.enter_context(nc.psum_tensor("ps", [128, 8, 512], F32))
        sems = {e: es.enter_context(nc.semaphore("s_" + e)) for e in ("pe", "act", "dve", "pool")}
        dsems = {e: [] for e in Prog.ENGS}
        dsems["sp"] = [es.enter_context(nc.semaphore(f"d_sp{i}")) for i in range(12)]
        dsems["pool"] = [es.enter_context(nc.semaphore(f"d_pl{i}")) for i in range(6)]
        AR = Arena(arena_t[:, :], ARENA_BYTES)

        def bank(i):
            return ps[:, i, :]

        def bankbf(i):
            return ps[:, i, :].bitcast(BF16)

        def pk(i):
            return f"ps{i}"

        CBF = AR.alloc(512)
        ident = CBF[:, 0:128]
        onesb = CBF[:, 128:256]
        CF = AR.alloc(388 * 4, F32)
        sel0 = CF[:, 0:128]
        sel32 = CF[:, 128:256]
        e0 = CF[:, 256:258]
        e32 = CF[:, 258:260]
        onesf = CF[:, 260:388]
        GB = AR.alloc(256 * 4, F32)
        qn_bc = GB[:, 0:128]
        kn_bc = GB[:, 128:256]
        SM = AR.alloc(128 * 4, F32)
        CT = SM[:, 0:16]
        NG = SM[:, 16:32]
        MODS = SM[:, 32:64]
        neghalf = SM[:, 64:65]
        sgcol = SM[:, 65:66]
        gcol = SM[:, 66:67]
        neglam = SM[:, 67:68]
        SSX = SM[:, 68:72]
        MSX = SM[:, 72:76]
        RSX = SM[:, 76:80]
        SSH = SM[:, 80:84]
        MSH = SM[:, 84:88]
        RSH = SM[:, 88:92]
        LR = SM[:, 92:94]
        LT = SM[0:1, 96:128]
        junk = AR.alloc(256)
        mark_persist = AR.off

        XT = [AR.alloc(4096, F32) for _ in range(2)]
        XN = [AR.alloc(2048) for _ in range(2)]
        ROPE = [AR.alloc(384 * 4, F32) for _ in range(2)]
        HT = AR.alloc(8192)
        HT3 = HT.rearrange("p (k t) -> p k t", k=8)
        tmp_off = (AR.off + 63) // 64 * 64
        TMP = [AR.alloc(2048, F32) for _ in range(4)]
        W1 = AR.alloc(32768)
        W1v = W1.rearrange("p (k c) -> p k c", k=8)
        mark_generic = AR.off

        def dma(out, in_, r, w, q="sp"):
            return P.add(q, lambda e: e.dma_start(out=out, in_=in_), r, w, dma=True)

        def act(out, in_, func, r, w, bias=0.0, scale=1.0, accum_out=None):
            if accum_out is None:
                return P.add("act", lambda e: e.activation(out=out, in_=in_, func=func,
                                                           bias=bias, scale=scale), r, w)
            return P.add("act", lambda e: e.activation(out=out, in_=in_, func=func, bias=bias,
                                                       scale=scale, accum_out=accum_out), r, w)

        def tt(eng, out, in0, in1, op, r, w):
            return P.add(eng, lambda e: e.tensor_tensor(out=out, in0=in0, in1=in1, op=op), r, w)

        def ts(eng, out, in0, s1, s2, op0, op1, r, w):
            if s2 is None:
                return P.add(eng, lambda e: e.tensor_scalar(out=out, in0=in0, scalar1=s1,
                                                            scalar2=None, op0=op0), r, w)
            return P.add(eng, lambda e: e.tensor_scalar(out=out, in0=in0, scalar1=s1, scalar2=s2,
                                                        op0=op0, op1=op1), r, w)

        def stt(out, in0, scalar, in1, op0, op1, r, w):
            return P.add("dve", lambda e: e.scalar_tensor_tensor(out=out, in0=in0, scalar=scalar,
                                                                 in1=in1, op0=op0, op1=op1), r, w)

        def cp(eng, out, in_, r, w):
            if eng == "act":
                return P.add("act", lambda e: e.copy(out=out, in_=in_), r, w)
            return P.add(eng, lambda e: e.tensor_copy(out=out, in_=in_), r, w)

        def mm(out, lhsT, rhs, start, stop, r, w):
            return P.add("pe", lambda e: e.matmul(out, lhsT, rhs, start=start, stop=stop), r, w)

        def tr(out, in_, r, w):
            return P.add("pe", lambda e: e.transpose(out, in_, ident), r, w)

        def memset(eng, ap, v, w):
            return P.add(eng, lambda e: e.memset(ap, v), (), w)

        fe_cnt = [0]

        def fe1(src_ap, sb=None):
            n = fe_cnt[0]
            fe_cnt[0] += 1
            b = n % 2
            sl = n % 4
            xt, xn = XT[b], XN[b]
            kx, kn = f"xt{b}", f"xn{b}"
            if sb is None:
                dma(xt, src_ap, (), (kx,))
            else:
                xt, kx = sb
            act(xn, xt, AF.Square, (kx,), (kn, f"ssx{sl}"), accum_out=SSX[:, sl:sl + 1])
            ts("dve", MSX[:, sl:sl + 1], SSX[:, sl:sl + 1], 1.0 / D, EPS, ALU.mult, ALU.add,
               (f"ssx{sl}",), (f"msx{sl}",))
            tt("pool", RSX[:, sl:sl + 1], MSX[:, sl:sl + 1], neghalf, ALU.pow,
               (f"msx{sl}", "consts"), (f"rsx{sl}",))
            act(xn, xt, AF.Identity, (kx, f"rsx{sl}"), (kn,), scale=RSX[:, sl:sl + 1])
            return b

        def fe2(b, hslot, moff, tbank):
            xn, kn = XN[b], f"xn{b}"
            tb = bankbf(tbank)
            for c in range(8):
                tr(tb[:, c * 128:(c + 1) * 128], xn[:, c * 128:(c + 1) * 128], (kn, "consts"),
                   (pk(tbank),))
            for c in range(8):
                ts("dve", HT3[:, c, hslot * 128:(hslot + 1) * 128], tb[:, c * 128:(c + 1) * 128],
                   MODS[:, moff + 8 + c:moff + 9 + c], MODS[:, moff + c:moff + c + 1],
                   ALU.mult, ALU.add, (pk(tbank), "mods"), (f"hT{hslot}_{c}",))

        def hkeys(slots):
            return tuple(f"hT{j}_{c}" for j in slots for c in range(8))

        STG = [(XT[0], ("xt0",)), (XT[1], ("xt1",)),
               (AR.ap[:, tmp_off // 2:(tmp_off + 4096) // 2].bitcast(F32), ("tmp0", "tmp1")),
               (AR.ap[:, (tmp_off + 4096) // 2:(tmp_off + 8192) // 2].bitcast(F32), ("tmp2", "tmp3"))]
        stg_n = [0]

        def stg_next():
            i = stg_n[0]
            stg_n[0] += 1
            return STG[i % len(STG)]

        def ada_third(l, t, SCv, MR, ADB):
            dma(ADB[0:1, :], adab_d[l:l + 1, t * 1024:(t + 1) * 1024], (), ("adb0",))
            dma(ADB[32:33, :], adab_d[l:l + 1, t * 1024:(t + 1) * 1024], (), ("adb32",))
            for k in range(8):
                sb_, sk_ = stg_next()
                dma(sb_, adaw_d[l, k * 128:(k + 1) * 128, t * 1024:(t + 1) * 1024], (), sk_)
                for hf in range(2):
                    mm(bank(hf), SCv[:, k, :], sb_[:, hf * 512:(hf + 1) * 512],
                       k == 0, k == 7, sk_ + ("sc",), (pk(hf),))
            for hf in range(2):
                tt("dve", MR[:, hf * 512:(hf + 1) * 512], bank(hf), ADB[:, hf * 512:(hf + 1) * 512],
                   ALU.add, (pk(hf), "adb0", "adb32"), ("mr",))

        def cols_from_rows(MR, which, with_ctx):
            pc = bank(2)
            for c in range(8):
                i0 = ((0 * 2 + which) * 8 + c) * 2
                mm(pc[:, i0:i0 + 2], MR[:, c * 128:(c + 1) * 128], e0, True, True,
                   ("mr", "c_cf"), (pk(2),))
                if with_ctx:
                    i1 = ((1 * 2 + which) * 8 + c) * 2
                    mm(pc[:, i1:i1 + 2], MR[:, c * 128:(c + 1) * 128], e32, True, True,
                       ("mr", "c_cf"), (pk(2),))

        def mods_finish(ngoff, with_ctx):
            pc = bank(2).rearrange("p (n two) -> p n two", two=2)
            for src in range(2 if with_ctx else 1):
                o = src * 16
                cp("dve", MODS[:, o:o + 8], pc[:, src * 16:src * 16 + 8, 0], (pk(2),), ("mods",))
                ts("dve", MODS[:, o + 8:o + 16], pc[:, src * 16 + 8:src * 16 + 16, 0], 1.0, None,
                   ALU.add, None, (pk(2),), ("mods",))
                tt("dve", MODS[:, o + 8:o + 16], MODS[:, o + 8:o + 16], NG[:, ngoff:ngoff + 8],
                   ALU.mult, ("mods", "c_ng"), ("mods",))

        def load_cast_weights(src_d, c0, ncols, dst3, dcol0, keyw, engs=("act", "dve"), xt_only=False):
            i = 0
            for k in range(8):
                for cc in range(0, ncols, 1024):
                    w = min(1024, ncols - cc)
                    sb_, sk_ = STG[i % 2] if xt_only else stg_next()
                    dma(sb_[:, 0:w], src_d[k * 128:(k + 1) * 128, c0 + cc:c0 + cc + w], (), sk_)
                    eng = engs[i % len(engs)]
                    cp(eng, dst3[:, k, dcol0 + cc:dcol0 + cc + w], sb_[:, 0:w], sk_, (keyw,))
                    i += 1

        def finish_dbg():
            P.barrier()
            dma(dbg_d[:, 0:32], MODS, ("mods",), ())
            cp("pool", TMP[0][:, 0:512], KTA[:, 0, 0:512], (), ("tmp0",))
            dma(dbg_d[:, 512:1024], TMP[0][:, 0:512], ("tmp0",), ())
            cp("pool", TMP[1][:, 0:512], KTB[:, 1, 256:768], (), ("tmp1",))
            dma(dbg_d[:, 1024:1536], TMP[1][:, 0:512], ("tmp1",), ())
            cp("pool", TMP[2][:, 0:256], VA[:, 3, :], (), ("tmp2",))
            cp("pool", TMP[2][:, 256:512], VB[:, 3, 0:256], (), ("tmp2",))
            dma(dbg_d[:, 1536:2048], TMP[2][:, 0:512], ("tmp2",), ())
            memset("pool", TMP[3][:, 0:32], 0.0, ("tmp3",))
            cp("pool", TMP[3][:, 0:1], neglam, (), ("tmp3",))
            cp("pool", TMP[3][:, 1:2], gcol, (), ("tmp3",))
            dma(dbg_d[:, 32:64], TMP[3][:, 0:32], ("tmp3",), ())
            P.emit(nc, sems, dsems)
            return nc, P

        dma(CBF, cbf_d, (), ("c_cbf",))
        dma(CF, cf32_d, (), ("c_cf",))
        dma(GB, gbc_d[:, 0:256], (), ("c_gb",))
        dma(CT, cT_d, (), ("c_ct",))
        dma(NG, ngT_d, (), ("c_ng",))
        dma(sgcol, sgT_d, (), ("c_sg",))
        LAMT = ROPE[0][:, 0:256]
        dma(LAMT[0:1, :], lam_d, (), ("rope0",))
        memset("pool", neghalf, -0.5, ("c_nh",))
        memset("pool", LR, 0.0, ("lr",))
        if stage == "S0":
            cp("pool", MODS, GB[:, 0:32], ("c_gb",), ("mods",))
            P.emit(nc, sems, dsems)
            return nc, P

        KTA = AR.alloc(2 * 4352 * 2).rearrange("p (h n) -> p h n", h=2)
        KTB = AR.alloc(4 * 4352 * 2).rearrange("p (h n) -> p h n", h=4)
        VA = AR.alloc(NKT * 256 * 2).rearrange("p (t c) -> p t c", t=NKT)
        VB = AR.alloc(NKT * 512 * 2).rearrange("p (t c) -> p t c", t=NKT)
        ROT = AR.alloc(4096)
        QTA = AR.alloc(4096).rearrange("p (h t) -> p h t", h=4)
        QTB = AR.alloc(8192).rearrange("p (h i t) -> p h i t", h=4, i=2)
        SG = AR.alloc(8192).rearrange("p (c t) -> p c t", c=8)
        NPT = 4
        PT = [AR.alloc(2048) for _ in range(NPT)]
        GF = AR.alloc(1024, F32)
        SQ = AR.alloc(1024)
        PA = AR.alloc(1024)
        QS = [AR.alloc(1024) for _ in range(3)]
        l0_peak = AR.off
        SCf = QTB.rearrange("p h i t -> p (h i t)")[:, 0:2048].bitcast(F32)
        SCv = SCf.rearrange("p (k m) -> p k m", k=8)
        MR = SG.rearrange("p c t -> p (c t)")[:, 0:2048].bitcast(F32)
        ADB = SG.rearrange("p c t -> p (c t)")[:, 2048:4096].bitcast(F32)

        memset("pool", SCf, 0.0, ("sc",))
        memset("pool", ADB, 0.0, ("adb0", "adb32"))
        act(SCv[:, :, 0], CT[:, 0:8], AF.Silu, ("c_ct", "sc"), ("sc",))
        act(SCv[:, :, 32], CT[:, 8:16], AF.Silu, ("c_ct", "sc"), ("sc",))
        for t in range(2):
            ada_third(0, t, SCv, MR, ADB)
            cols_from_rows(MR, t, True)
        mods_finish(0, True)
        if stage == "S1":
            return finish_dbg()

        LV = LAMT[0:1, :].rearrange("p (a n) -> p a n", a=4)
        LP = LT[:, 0:2]
        tt("dve", LAMT[0:1, 0:64], LV[:, 0, :], LV[:, 1, :], ALU.mult, ("rope0",), ("rope0",))
        tt("dve", LAMT[0:1, 128:192], LV[:, 2, :], LV[:, 3, :], ALU.mult, ("rope0",), ("rope0",))
        P.add("dve", lambda e: e.reduce_sum(out=LP[:, 0:1], in_=LAMT[0:1, 0:64], axis=AX.X),
              ("rope0",), ("lp",))
        P.add("dve", lambda e: e.reduce_sum(out=LP[:, 1:2], in_=LAMT[0:1, 128:192], axis=AX.X),
              ("rope0",), ("lp",))
        act(LP, LP, AF.Exp, ("lp",), ("lp",))
        tt("dve", LR[0:1, 0:1], LP[:, 1:2], LP[:, 0:1], ALU.subtract, ("lp", "lr"), ("lr",))
        ts("dve", LR[0:1, 0:1], LR[0:1, 0:1], -LAM_INIT0, None, ALU.add, None, ("lr",), ("lr",))
        mm(bank(3)[:, 0:2], sel0, LR, True, True, ("lr", "c_cf"), (pk(3),))
        cp("dve", neglam, bank(3)[:, 0:1], (pk(3),), ("consts2",))
        ts("dve", gcol, sgcol, 1.0 - LAM_INIT0, None, ALU.mult, None, ("c_sg",), ("consts2",))

        if stage == "S2":
            return finish_dbg()
        load_cast_weights(win_d, 0, 1536, W1v, 0, "w1")
        if stage == "S3":
            return finish_dbg()

        P.barrier()
        memset("pool", QTB.rearrange("p h i t -> p (h i t)"), 0.0, ("qtb0", "qtb1"))

        def kv_post(kt, s, rb):
            b1, b2, b3 = 4 * s + 1, 4 * s + 2, 4 * s + 3
            rope = ROPE[rb]
            rk = f"rope{rb}"
            rot = ROT[:, s * 768:(s + 1) * 768]
            rkey = f"rotk{s}"
            cp("act", VA[:, kt, :], bank(b1)[:, 256:512], (pk(b1),), (f"va{kt}",))
            cp("act", VB[:, kt, :], bank(b3), (pk(b3),), (f"vb{kt}",))
            for h in range(2):
                act(junk, bank(b1)[:, h * 128:(h + 1) * 128], AF.Square, (pk(b1),), (f"ssh{h}", "junk"),
                    accum_out=SSH[:, h:h + 1])
            ts("dve", MSH[:, 0:2], SSH[:, 0:2], 1.0 / 128, EPS, ALU.mult, ALU.add,
               ("ssh0", "ssh1"), ("msh",))
            tt("pool", RSH[:, 0:2], MSH[:, 0:2], neghalf.broadcast_to([128, 2]), ALU.pow,
               ("msh", "consts"), ("rsh",))
            tt("pool", GF[:, 0:128], rope[:, 0:128], kn_bc, ALU.mult, (rk, "consts"), ("gf",))
            tt("pool", GF[:, 128:256], rope[:, 128:256], kn_bc, ALU.mult, (rk, "consts"), ("gf",))
            for h in range(2):
                stt(TMP[0][:, h * 128:(h + 1) * 128], bank(b1)[:, h * 128:(h + 1) * 128],
                    RSH[:, h:h + 1], GF[:, 0:128], ALU.mult, ALU.mult, (pk(b1), "rsh", "gf"), ("tmp0",))
                stt(TMP[1][:, h * 128:(h + 1) * 128], bank(b1)[:, h * 128:(h + 1) * 128],
                    RSH[:, h:h + 1], GF[:, 128:256], ALU.mult, ALU.mult, (pk(b1), "rsh", "gf"), ("tmp1",))
            t1 = TMP[0][:, 0:256].rearrange("p (g two f) -> p g two f", two=2, f=32)
            t2 = TMP[1][:, 0:256].rearrange("p (g two f) -> p g two f", two=2, f=32)
            ro = rot[:, 0:256].rearrange("p (g two f) -> p g two f", two=2, f=32)
            tt("pool", ro[:, :, 0, :], t1[:, :, 0, :], t2[:, :, 1, :], ALU.subtract,
               ("tmp0", "tmp1"), (rkey,))
            tt("pool", ro[:, :, 1, :], t2[:, :, 0, :], t1[:, :, 1, :], ALU.add,
               ("tmp0", "tmp1"), (rkey,))
            xb = bank(b2).rearrange("p (g d) -> p g d", g=8)
            cbb = rope[:, 256:320].unsqueeze(1).broadcast_to([128, 8, 64])
            sbb = rope[:, 320:384].unsqueeze(1).broadcast_to([128, 8, 64])
            tt("dve", TMP[2].rearrange("p (g d) -> p g d", g=8), xb, cbb, ALU.mult, (pk(b2), rk), ("tmp2",))
            tt("dve", TMP[3].rearrange("p (g d) -> p g d", g=8), xb, sbb, ALU.mult, (pk(b2), rk), ("tmp3",))
            t1 = TMP[2].rearrange("p (g two f) -> p g two f", two=2, f=16)
            t2 = TMP[3].rearrange("p (g two f) -> p g two f", two=2, f=16)
            ro = rot[:, 256:768].rearrange("p (g two f) -> p g two f", two=2, f=16)
            tt("pool", ro[:, :, 0, :], t1[:, :, 0, :], t2[:, :, 1, :], ALU.subtract,
               ("tmp2", "tmp3"), (rkey,))
            tt("pool", ro[:, :, 1, :], t2[:, :, 0, :], t1[:, :, 1, :], ALU.add,
               ("tmp2", "tmp3"), (rkey,))

        def kv_trans(kt, s):
            b0 = 4 * s + KTBANK
            rot = ROT[:, s * 768:(s + 1) * 768]
            tb = bankbf(b0)
            for j in range(6):
                tr(tb[:, j * 128:(j + 1) * 128], rot[:, j * 128:(j + 1) * 128], (f"rotk{s}", "consts"),
                   (pk(b0),))
            if DBG2 >= 1:
                cp("dve", KTA[:, :, kt * 128:(kt + 1) * 128],
                   tb[:, 0:256].rearrange("p (h t) -> p h t", h=2), (pk(b0),), (f"kta{kt}",))
            if DBG2 >= 2:
                cp("dve", KTB[:, :, kt * 128:(kt + 1) * 128],
                   tb[:, 256:768].rearrange("p (h t) -> p h t", h=4), (pk(b0),), (f"ktb{kt}_0", f"ktb{kt}_1"))

        NK1 = NKT if stage != "S4" else 3

        p1_buf = {}

        def p1_A1(kt):
            src = ctx_d[kt * 128:(kt + 1) * 128, :] if kt < 2 else x_d[(kt - 2) * 128:(kt - 1) * 128, :]
            p1_buf[kt] = fe1(src)

        def p1_rope(kt):
            dma(ROPE[kt % 2], rope_d[kt], (), (f"rope{kt % 2}",))

        def p1_A2(kt):
            s_ = kt % 2
            fe2(p1_buf[kt], s_, 16 if kt < 2 else 0, 4 * s_)

        def p1_B(kt):
            s_ = kt % 2
            for j, bnk in enumerate((4 * s_ + 1, 4 * s_ + 2, 4 * s_ + 3)):
                for k in range(8):
                    mm(bank(bnk), HT3[:, k, s_ * 128:(s_ + 1) * 128], W1v[:, k, j * 512:(j + 1) * 512],
                       k == 0, k == 7, (f"hT{s_}_{k}", "w1"), (pk(bnk),))
            kv_post(kt, s_, kt % 2)

        p1_A1(0)
        p1_A1(1)
        p1_rope(0)
        p1_A2(0)
        for kt in range(NK1):
            if kt + 2 < NK1:
                p1_A1(kt + 2)
            if kt + 1 < NK1:
                p1_rope(kt + 1)
            if kt + 1 < NK1:
                p1_A2(kt + 1)
            p1_B(kt)
            if kt >= 1:
                kv_trans(kt - 1, (kt - 1) % 2)
        kv_trans(NK1 - 1, (NK1 - 1) % 2)

        if stage in ("P1", "S4"):
            return finish_dbg()

        P.barrier()
        load_cast_weights(win_d, 1536, 2048, W1v, 0, "w1")
        P.barrier()
        OG3 = HT3
        QTBf = QTB

        def q_post(j, rb, ba, bb):
            rope = ROPE[rb]
            rk = f"rope{rb}"
            rot = ROT[:, (j % 2) * 1024:(j % 2 + 1) * 1024]
            rqk = f"rotq{j % 2}"
            for h in range(4):
                act(junk, bank(ba)[:, h * 128:(h + 1) * 128], AF.Square, (pk(ba),), (f"ssh{h}", "junk"),
                    accum_out=SSH[:, h:h + 1])
            ts("dve", MSH[:, 0:4], SSH[:, 0:4], 1.0 / 128, EPS, ALU.mult, ALU.add,
               ("ssh0", "ssh1", "ssh2", "ssh3"), ("msh",))
            tt("pool", RSH[:, 0:4], MSH[:, 0:4], neghalf.broadcast_to([128, 4]), ALU.pow,
               ("msh", "consts"), ("rsh",))
            tt("pool", GF[:, 0:128], rope[:, 0:128], qn_bc, ALU.mult, (rk, "consts"), ("gf",))
            tt("pool", GF[:, 128:256], rope[:, 128:256], qn_bc, ALU.mult, (rk, "consts"), ("gf",))
            for h in range(4):
                stt(TMP[0][:, h * 128:(h + 1) * 128], bank(ba)[:, h * 128:(h + 1) * 128],
                    RSH[:, h:h + 1], GF[:, 0:128], ALU.mult, ALU.mult, (pk(ba), "rsh", "gf"), ("tmp0",))
                stt(TMP[1][:, h * 128:(h + 1) * 128], bank(ba)[:, h * 128:(h + 1) * 128],
                    RSH[:, h:h + 1], GF[:, 128:256], ALU.mult, ALU.mult, (pk(ba), "rsh", "gf"), ("tmp1",))
            t1 = TMP[0].rearrange("p (g two f) -> p g two f", two=2, f=32)
            t2 = TMP[1].rearrange("p (g two f) -> p g two f", two=2, f=32)
            ro = rot[:, 0:512].rearrange("p (g two f) -> p g two f", two=2, f=32)
            tt("pool", ro[:, :, 0, :], t1[:, :, 0, :], t2[:, :, 1, :], ALU.subtract,
               ("tmp0", "tmp1"), (rqk,))
            tt("pool", ro[:, :, 1, :], t2[:, :, 0, :], t1[:, :, 1, :], ALU.add,
               ("tmp0", "tmp1"), (rqk,))
            xb = bank(bb).rearrange("p (g d) -> p g d", g=8)
            cbb = rope[:, 256:320].unsqueeze(1).broadcast_to([128, 8, 64])
            sbb = rope[:, 320:384].unsqueeze(1).broadcast_to([128, 8, 64])
            tt("dve", TMP[2].rearrange("p (g d) -> p g d", g=8), xb, cbb, ALU.mult, (pk(bb), rk), ("tmp2",))
            tt("dve", TMP[3].rearrange("p (g d) -> p g d", g=8), xb, sbb, ALU.mult, (pk(bb), rk), ("tmp3",))
            t1 = TMP[2].rearrange("p (g two f) -> p g two f", two=2, f=16)
            t2 = TMP[3].rearrange("p (g two f) -> p g two f", two=2, f=16)
            ro = rot[:, 512:1024].rearrange("p (g two f) -> p g two f", two=2, f=16)
            tt("pool", ro[:, :, 0, :], t1[:, :, 0, :], t2[:, :, 1, :], ALU.subtract,
               ("tmp2", "tmp3"), (rqk,))
            tt("pool", ro[:, :, 1, :], t2[:, :, 0, :], t1[:, :, 1, :], ALU.add,
               ("tmp2", "tmp3"), (rqk,))

        def q_trans(j, bt):
            rot = ROT[:, (j % 2) * 1024:(j % 2 + 1) * 1024]
            rqk = f"rotq{j % 2}"
            tb = bankbf(bt)
            for c in range(8):
                tr(tb[:, c * 128:(c + 1) * 128], rot[:, c * 128:(c + 1) * 128], (rqk, "consts"),
                   (pk(bt),))
            cp("dve", QTA[:, :, j * 128:(j + 1) * 128],
               tb[:, 0:512].rearrange("p (h t) -> p h t", h=4), (pk(bt),), ("qta",))
            tbb = tb[:, 512:1024].rearrange("p (h t) -> p h t", h=4)
            cp("dve", QTB[0:64, :, 0, j * 128:(j + 1) * 128], tbb[0:64], (pk(bt),), ("qtb0",))
            cp("dve", QTB[64:128, :, 1, j * 128:(j + 1) * 128], tbb[64:128], (pk(bt),), ("qtb1",))

        maps = [("B", h, i) for h in range(4) for i in range(2)] + [("A", h, 0) for h in range(4)]
        SCALE_A = 128.0 ** -0.5
        SCALE_B = 64.0 ** -0.5

        def attention_block(qb):
            seq = [(mi, pr) for mi in range(len(maps)) for pr in range(NKT // 2)]

            def qk(idx):
                mi, pr = seq[idx]
                kind, h, i = maps[mi]
                sb = 2 * (idx % 2)
                for t in range(2):
                    kt = 2 * pr + t
                    if kind == "A":
                        lhsT = KTA[:, h // 2, kt * 128:(kt + 1) * 128]
                        rhs = QTA[:, h, :]
                        r = (f"kta{kt}", "qta")
                    else:
                        lhsT = KTB[:, h, kt * 128:(kt + 1) * 128]
                        rhs = QTB[:, h, i, :]
                        r = (f"ktb{kt}_{h % 2}", f"qtb{i}")
                    mm(bank(sb + t), lhsT, rhs, True, True, r, (pk(sb + t),))

            def ex(idx):
                mi, pr = seq[idx]
                kind = maps[mi][0]
                sb = 2 * (idx % 2)
                pt = PT[idx % NPT]
                act(pt.rearrange("p (t n) -> p t n", t=2), ps[:, sb:sb + 2, :], AF.Exp,
                    (pk(sb), pk(sb + 1)), (f"pt{idx % NPT}",),
                    scale=(SCALE_A if kind == "A" else SCALE_B))

            def pv(idx):
                mi, pr = seq[idx]
                kind, h, i = maps[mi]
                ob = 4 + 2 * (mi % 2)
                pt = PT[idx % NPT]
                for t in range(2):
                    kt = 2 * pr + t
                    if kind == "A":
                        lhsT = VA[:, kt, (h // 2) * 128:(h // 2 + 1) * 128]
                        r = (f"va{kt}", f"pt{idx % NPT}")
                    else:
                        lhsT = VB[:, kt, h * 128:(h + 1) * 128]
                        r = (f"vb{kt}", f"pt{idx % NPT}")
                    mm(bank(ob), lhsT, pt[:, t * 512:(t + 1) * 512], kt == 0, kt == NKT - 1, r, (pk(ob),))

            def finish(mi):
                kind, h, i = maps[mi]
                ob = 4 + 2 * (mi % 2)
                T = TMP[mi % 2]
                tk = f"tmp{mi % 2}"
                P.add("dve", lambda e: e.reciprocal(out=T, in_=bank(ob + 1)), (pk(ob + 1),), (tk,))
                tt("dve", T, bank(ob), T, ALU.mult, (pk(ob), tk), (tk,))
                if kind == "A":
                    tt("pool", SG[:, h, :], T, SG[:, h, :], ALU.mult, (tk, f"sg{h}"), (f"sg{h}",))
                elif i == 1:
                    T0, T1 = TMP[0], TMP[1]
                    bssq = ob + 1

                    def f1():
                        stt(T0, T1, neglam, T0, ALU.mult, ALU.add, ("tmp0", "tmp1", "consts2"), ("tmp0",))
                        tt("pool", SQ, T0, T0, ALU.mult, ("tmp0",), ("sq",))

                    def f2():
                        mm(bank(bssq), onesb, SQ, True, True, ("sq", "consts"), (pk(bssq),))
                        ts("dve", T1, bank(bssq), 1.0 / 128, EPS, ALU.mult, ALU.add, (pk(bssq),), ("tmp1",))

                    def f3():
                        act(T1, T1, AF.Ln, ("tmp1",), ("tmp1",))
                        act(T1, T1, AF.Exp, ("tmp1",), ("tmp1",), scale=-0.5)

                    def f4():
                        stt(T0, T0, gcol, T1, ALU.mult, ALU.mult, ("tmp0", "tmp1", "consts2"), ("tmp0",))
                        tt("pool", SG[:, 4 + h, :], T0, SG[:, 4 + h, :], ALU.mult, ("tmp0", f"sg{4 + h}"),
                           (f"sg{4 + h}",))

                    return [f1, f2, f3, f4]
                return []

            def psum2(idx):
                mi, pr = seq[idx]
                pt = PT[idx % NPT]
                q = pr // 2
                if pr == NKT // 2 - 1:
                    tt("dve", QS[q % 3], pt[:, 0:512], pt[:, 512:1024], ALU.add, (f"pt{idx % NPT}",),
                       (f"qs{q % 3}",))
                elif pr % 2 == 0:
                    tt("dve", PA, pt[:, 0:512], pt[:, 512:1024], ALU.add, (f"pt{idx % NPT}",), ("pa",))
                else:
                    tt("dve", QS[q % 3], pt[:, 0:512], PA, ALU.add, (f"pt{idx % NPT}", "pa"), (f"qs{q % 3}",))
                    tt("dve", QS[q % 3], QS[q % 3], pt[:, 512:1024], ALU.add, (f"pt{idx % NPT}", f"qs{q % 3}"),
                       (f"qs{q % 3}",))

            def den(idx):
                mi, pr = seq[idx]
                ob = 4 + 2 * (mi % 2)
                q = pr // 2
                mm(bank(ob + 1), onesb, QS[q % 3], pr == 1, pr == NKT // 2 - 1,
                   (f"qs{q % 3}", "consts"), (pk(ob + 1),))

            n = len(seq)
            deferred = {}
            DD = 3
            qk(0)
            qk(1)
            for idx in range(n):
                for f in deferred.pop(idx, ()):
                    f()
                ex(idx)
                psum2(idx)
                if idx + 2 < n:
                    qk(idx + 2)
                mi, pr = seq[idx]
                pv(idx)
                if pr % 2 == 1 or pr == NKT // 2 - 1:
                    deferred.setdefault(idx + DD, []).append(lambda idx=idx: den(idx))
                if pr == NKT // 2 - 1:
                    def fin(mi=mi, base=idx + DD):
                        for k, f in enumerate(finish(mi)):
                            deferred.setdefault(base + 2 + 2 * k, []).append(f)
                    deferred.setdefault(idx + DD, []).append(fin)
            while deferred:
                k = min(deferred)
                for f in deferred.pop(k):
                    f()

        NQB = 8 if stage != "L0s" else 1
        fb2 = {}

        def A2a(j, qb):
            ti = qb * 4 + j
            fb2[(qb, j)] = fe1(x_d[ti * 128:(ti + 1) * 128, :])

        def R2(j, qb):
            ti = qb * 4 + j
            dma(ROPE[ti % 2], rope_d[2 + ti], (), (f"rope{ti % 2}",))

        for qb in range(NQB):
            def A2b(j, qb=qb):
                fe2(fb2[(qb, j)], j, 0, 4 + (j % 2))

            def B2(j, qb=qb):
                ti = qb * 4 + j
                ba, bb = 2 * (j % 2), 2 * (j % 2) + 1
                for jj, bnk in enumerate((ba, bb)):
                    for k in range(8):
                        mm(bank(bnk), HT3[:, k, j * 128:(j + 1) * 128], W1v[:, k, jj * 512:(jj + 1) * 512],
                           k == 0, k == 7, (f"hT{j}_{k}", "w1"), (pk(bnk),))
                q_post(j, ti % 2, ba, bb)

            def C2(j):
                q_trans(j, 6 + (j % 2))

            def G2(c0, c1):
                for c in range(c0, c1):
                    bnk = 4 + (c % 2)
                    for k in range(8):
                        mm(bank(bnk), W1v[:, k, 1024 + c * 128:1024 + (c + 1) * 128], HT3[:, k, :],
                           k == 0, k == 7, tuple(f"hT{j}_{k}" for j in range(4)) + ("w1",), (pk(bnk),))
                    act(SG[:, c, :], bank(bnk), AF.Silu, (pk(bnk),), (f"sg{c}",))

            if qb == 0:
                A2a(0, qb); R2(0, qb); A2a(1, qb); R2(1, qb)
            A2b(0); A2a(2, qb); A2b(1); B2(0); R2(2, qb); A2a(3, qb); A2b(2); B2(1); R2(3, qb)
            C2(0); A2b(3); B2(2); C2(1)
            G2(0, 4); B2(3); C2(2); G2(4, 8); C2(3)
            if qb + 1 < NQB:
                A2a(0, qb + 1); R2(0, qb + 1); A2a(1, qb + 1); R2(1, qb + 1)
            elif stage == "full":
                load_cast_weights(fin_d, 0, 2048, W1v, 0, "w1", engs=("pool",), xt_only=True)
            attention_block(qb)
            for c4 in range(4):
                dma(og_d.rearrange("(c p) n -> p c n", p=128)[:, 2 * c4:2 * c4 + 2, qb * 512:(qb + 1) * 512],
                    SG[:, 2 * c4:2 * c4 + 2, :], (f"sg{2 * c4}", f"sg{2 * c4 + 1}"), ("ogd",))

        if stage in ("L0", "L0s"):
            P.emit(nc, sems, dsems)
            return nc, P


        P.barrier()
        AR.off = mark_generic
        WO = AR.alloc(16384)
        WO3 = WO.rearrange("p (k c) -> p k c", k=8)
        FC = AR.alloc(2560)
        F1 = FC[:, 0:256]
        CS3 = FC[:, 256:1280].rearrange("p (c n) -> p c n", c=2)
        GRH = [TMP[0], TMP[1]]
        PB = [AR.alloc(1024) for _ in range(4)]
        GG = AR.alloc(16384).rearrange("p (c n) -> p c n", c=2)
        YY = AR.alloc(32768)
        FG = AR.alloc(65536).rearrange("p (c n) -> p c n", c=8)
        L1X = AR.alloc(4096, F32)
        STG[:] = [(XT[0], ("xt0",)), (XT[1], ("xt1",)), (L1X, ("l1x",))]
        YYf = YY
        SCf1 = YYf[:, 0:2048].bitcast(F32)
        SCv1 = SCf1.rearrange("p (k m) -> p k m", k=8)
        MR1 = YYf[:, 2048:4096].bitcast(F32)
        ADB1 = YYf[:, 4096:6144].bitcast(F32)
        OGB = YYf[:, 6144:10240].rearrange("p (c t) -> p c t", c=8)
        X1T = [YYf[:, 10240 + i * 2048:10240 + (i + 1) * 2048].bitcast(F32) for i in range(2)]
        UO = [YYf[:, 14336 + i * 1024:14336 + (i + 1) * 1024] for i in range(2)]
        FGf = FG.rearrange("p c n -> p (c n)")
        GO = FGf[:, 0:4096].rearrange("p (c t) -> p c t", c=8)
        X1T = [FGf[:, 4096 + i * 2048:4096 + (i + 1) * 2048].bitcast(F32) for i in range(4)]
        OGBS = [OGB, FGf[:, 12288:16384].rearrange("p (c t) -> p c t", c=8)]
        gen_off = mark_persist
        TTZ = AR.ap[:, (mark_persist + 63) // 64 * 64 // 2:(mark_persist + 63) // 64 * 64 // 2 + 8192]
        TT5 = TTZ.rearrange("p (a k w h) -> p a k w h", a=2, k=64, w=2)

        memset("pool", SCf1, 0.0, ("sc",))
        memset("pool", ADB1, 0.0, ("adb0", "adb32"))
        act(SCv1[:, :, 0], CT[:, 0:8], AF.Silu, ("sc",), ("sc",))
        fg_bc = AR.ap[:, (tmp_off + 4096) // 2:(tmp_off + 8192) // 2].bitcast(F32)
        dma(fg_bc, gbc_d[:, 256:1280], (), ("fgbc",))
        dma(FC[:, 0:256], f1_d, (), ("fc1",))
        dma(FC[:, 256:1280], cs_d, (), ("fc2",))

        def gate_weights(l, src_d):
            ada_third(l, 2, SCv1, MR1, ADB1)
            for hf in range(2):
                mm(bank(4 + hf), sel0, MR1[:, hf * 512:(hf + 1) * 512], True, True, ("mr",), (pk(4 + hf),))

        def fold_gate_into(src_d, keyw):
            for k in range(8):
                sb_, sk_ = stg_next()
                dma(sb_, src_d[k * 128:(k + 1) * 128, :], (), sk_)
                for hf in range(2):
                    tt("dve", WO3[:, k, hf * 512:(hf + 1) * 512], sb_[:, hf * 512:(hf + 1) * 512],
                       bank(4 + hf), ALU.mult, sk_ + (pk(4 + hf),), (keyw,))

        for t in range(2):
            ada_third(1, t, SCv1, MR1, ADB1)
            cols_from_rows(MR1, t, False)
        mods_finish(8, False)
        ada_third(1, 2, SCv1, MR1, ADB1)
        for hf in range(2):
            cp("dve", GRH[hf], MR1[:, hf * 512:(hf + 1) * 512], ("mr",), ("gr",))
        gate_weights(0, wout_d)
        fold_gate_into(wout_d, "wo")
        P.barrier()

        def ogb_load(qb):
            for c4 in range(4):
                dma(OGBS[qb % 2][:, 2 * c4:2 * c4 + 2, :],
                    og_d.rearrange("(c p) n -> p c n", p=128)[:, 2 * c4:2 * c4 + 2, qb * 512:(qb + 1) * 512],
                    ("ogd",), (f"ogb{qb % 2}_{c4}",))

        def x1_load(ti):
            dma(X1T[ti % 4], x_d[ti * 128:(ti + 1) * 128, :], (), (f"x1t{ti % 4}",))

        ogb_load(0)
        x1_load(0)
        x1_load(1)
        for qb in range(8):
            if qb + 1 < 8:
                ogb_load(qb + 1)
            OGB = OGBS[qb % 2]

            fb1 = {}

            def F1b(j, fb1=fb1):
                fe2(fb1[j], j, 0, 2 + (j % 2))

            def O1(j, qb=qb, fb1=fb1, OGB=OGB):
                ti = qb * 4 + j
                xb = ti % 4
                if ti + 2 < 32:
                    x1_load(ti + 2)
                for hf in range(2):
                    for c in range(8):
                        mm(bank(hf), OGB[:, c, j * 128:(j + 1) * 128], WO3[:, c, hf * 512:(hf + 1) * 512],
                           c == 0, c == 7, (f"ogb{qb % 2}_{c // 2}", "wo"), (pk(hf),))
                for hf in range(2):
                    tt("dve", X1T[xb][:, hf * 512:(hf + 1) * 512], bank(hf), X1T[xb][:, hf * 512:(hf + 1) * 512],
                       ALU.add, (pk(hf), f"x1t{xb}"), (f"x1t{xb}",))
                dma(x1_d[ti * 128:(ti + 1) * 128, :], X1T[xb], (f"x1t{xb}",), ("x1d",), q="pool")
                fb1[j] = fe1(None, sb=(X1T[xb], f"x1t{xb}"))

            def B1(j, qb=qb):
                ti = qb * 4 + j
                xb = ti % 2
                for hf in range(2):
                    for k in range(8):
                        mm(bank(4 + hf), HT3[:, k, j * 128:(j + 1) * 128], W1v[:, k, hf * 512:(hf + 1) * 512],
                           k == 0, k == 7, (f"hT{j}_{k}", "w1"), (pk(4 + hf),))
                    cp("act" if hf == 0 else "dve", UO[xb][:, hf * 512:(hf + 1) * 512], bank(4 + hf),
                       (pk(4 + hf),), (f"uo{xb}_{hf}",))
                dma(u_d[ti * 128:(ti + 1) * 128, :], UO[xb], (f"uo{xb}_0", f"uo{xb}_1"), ("ud",), q="pool")

            def G1(c0, c1):
                for c in range(c0, c1):
                    bnk = 6 + (c % 2)
                    for k in range(8):
                        mm(bank(bnk), W1v[:, k, 1024 + c * 128:1024 + (c + 1) * 128], HT3[:, k, :],
                           k == 0, k == 7, tuple(f"hT{j}_{k}" for j in range(4)) + ("w1",), (pk(bnk),))
                    act(GO[:, c, :], bank(bnk), AF.Silu, (pk(bnk),), ("go",))

            O1(0); O1(1); F1b(0); O1(2); F1b(1); B1(0); O1(3); F1b(2); B1(1); F1b(3); B1(2)
            G1(0, 4); B1(3); G1(4, 8)
            for c4 in range(4):
                dma(g_d.rearrange("(c p) n -> p c n", p=128)[:, 2 * c4:2 * c4 + 2, qb * 512:(qb + 1) * 512],
                    GO[:, 2 * c4:2 * c4 + 2, :], ("go",), ("gd",))

        P.barrier()
        dma(TTZ, tt_d, (), ("ttz",))
        UG = [W1[:, i * 8192:(i + 1) * 8192].rearrange("p (l c) -> p l c", l=32) for i in range(2)]
        Y5 = YY.rearrange("p (c k l r) -> p c k l r", c=2, k=128, l=32)
        u_v = u_d.rearrange("(nh nl) c -> nh nl c", nl=32)
        g_v = g_d.rearrange("(c p) n -> p c n", p=128)
        ev = [0]
        for gr in range(4):
            ub = gr % 2
            for n8 in range(8):
                dma(UG[ub][:, 4 * n8:4 * n8 + 4, :], u_v[:, 4 * n8:4 * n8 + 4, gr * 256:(gr + 1) * 256],
                    ("ud",), (f"ug{ub}_{n8}",))
            dma(GG, g_v[:, 2 * gr:2 * gr + 2, :], ("gd",), ("gg",))
            for nl in range(32):
                bnk = nl % 2
                for cc in range(2):
                    mm(bank(bnk)[:, cc * 256:(cc + 1) * 256], UG[ub][:, nl, cc * 128:(cc + 1) * 128], F1,
                       True, True, (f"ug{ub}_{nl // 4}", "fc1"), (pk(bnk),))
                cp("act" if nl % 4 != 3 else "dve", Y5[:, :, :, nl, :],
                   bank(bnk).rearrange("p (c k r) -> p c k r", c=2, r=2), (pk(bnk),), ("yy",))
            def chdft(kp, gr=gr):
                pbk = (2, 3, 6)[kp % 3]
                pb = PB[kp % 4]
                for cc in range(2):
                    mm(bank(pbk), Y5[:, cc, 2 * kp:2 * kp + 2, :, :].rearrange("p a l r -> p (a l r)"),
                       CS3[:, cc, :], cc == 0, cc == 1, ("yy", "fc2"), (pk(pbk),))
                cp("act" if kp % 3 != 2 else "dve", pb, bank(pbk), (pk(pbk),), (f"pb{kp % 4}",))

            def stage2(kp, gr=gr):
                pb = PB[kp % 4]
                fb = 4 + ((kp // 4) % 2)
                for par in range(2):
                    for mc in range(2):
                        col = (((kp % 4) * 2 + par) * 2 + mc) * 32
                        mm(bank(fb)[:, col:col + 32], pb[:, mc * 128:(mc + 1) * 128], TT5[:, par, kp, 0, :],
                           True, False, (f"pb{kp % 4}", "ttz"), (pk(fb),))
                        mm(bank(fb)[:, col:col + 32], pb[:, 256 + mc * 128:256 + (mc + 1) * 128],
                           TT5[:, par, kp, 1, :], False, True, (f"pb{kp % 4}", "ttz"), (pk(fb),))
                if kp % 4 == 3:
                    k0 = 2 * (kp - 3)
                    fbv = bank(fb).rearrange("p (a m h) -> p a m h", m=2, h=32)
                    for mc in range(2):
                        gv = GG[:, mc, :].rearrange("p (h l) -> p h l", l=128)[:, :, k0:k0 + 8]
                        ov = FG[:, 2 * gr + mc, :].rearrange("p (h l) -> p h l", l=128)[:, :, k0:k0 + 8]
                        tt("dve", ov, fbv[:, :, mc, :].rearrange("p a h -> p h a"), gv, ALU.mult,
                           (pk(fb), "gg"), ("fg",))

            chdft(0)
            chdft(1)
            for kp in range(64):
                if kp + 2 < 64:
                    chdft(kp + 2)
                stage2(kp)

        P.barrier()
        for hf in range(2):
            mm(bank(4 + hf), sel0, GRH[hf], True, True, ("gr",), (pk(4 + hf),))
        fold_gate_into(fout_d, "wo")
        P.barrier()
        ZT = [YY[:, i * 2048:(i + 1) * 2048].bitcast(F32) for i in range(4)]

        def c_L(ti):
            zb = ti % 4
            dma(ZT[zb], x1_d[ti * 128:(ti + 1) * 128, :], ("x1d",), (f"zt{zb}",))

        def c_X(ti):
            zb = ti % 4
            bo = 2 * (ti % 2)
            for hf in range(2):
                for c in range(8):
                    mm(bank(bo + hf), FG[:, c, ti * 128:(ti + 1) * 128], WO3[:, c, hf * 512:(hf + 1) * 512],
                       c == 0, c == 7, ("fg", "wo"), (pk(bo + hf),))
            for hf in range(2):
                tt("dve", ZT[zb][:, hf * 512:(hf + 1) * 512], bank(bo + hf), ZT[zb][:, hf * 512:(hf + 1) * 512],
                   ALU.add, (pk(bo + hf), f"zt{zb}"), (f"zt{zb}",))

        def c_Ya(ti):
            zb = ti % 4
            sl = ti % 4
            act(XN[ti % 2], ZT[zb], AF.Square, (f"zt{zb}",), (f"xn{ti % 2}", f"ssx{sl}"), accum_out=SSX[:, sl:sl + 1])
            ts("dve", MSX[:, sl:sl + 1], SSX[:, sl:sl + 1], 1.0 / D, EPS, ALU.mult, ALU.add,
               (f"ssx{sl}",), (f"msx{sl}",))
            tt("pool", RSX[:, sl:sl + 1], MSX[:, sl:sl + 1], neghalf, ALU.pow, (f"msx{sl}",), (f"rsx{sl}",))

        def c_Yb(ti):
            zb = ti % 4
            sl = ti % 4
            stt(ZT[zb], ZT[zb], RSX[:, sl:sl + 1], fg_bc, ALU.mult, ALU.mult, (f"zt{zb}", f"rsx{sl}", "fgbc"), (f"zt{zb}",))
            dma(out_d[ti * 128:(ti + 1) * 128, :], ZT[zb], (f"zt{zb}",), ("outd",), q="pool")

        c_L(0)
        c_L(1)
        c_X(0)
        for ti in range(32):
            if ti + 2 < 32:
                c_L(ti + 2)
            if ti + 1 < 32:
                c_X(ti + 1)
            c_Ya(ti)
            if ti >= 1:
                c_Yb(ti - 1)
        c_Yb(31)

        P.emit(nc, sems, dsems)
        return nc, P


def _in_maps(inp):
    C = _consts()
    f = lambda a: np.ascontiguousarray(np.asarray(a, dtype=np.float32))
    x = f(inp["x"]); c = f(inp["c"]); ctx = f(inp["ctx"]); c_ctx = f(inp["c_ctx"])
    norm_g = f(inp["norm_g"])
    ngT = np.concatenate([norm_g[0].reshape(8, 128).T, norm_g[1].reshape(8, 128).T], axis=1)
    gbc = np.concatenate([np.broadcast_to(f(inp["attn_qn_g"])[0][None, :], (128, 128)),
                          np.broadcast_to(f(inp["attn_kn_g"])[0][None, :], (128, 128)),
                          np.broadcast_to(f(inp["final_g"])[None, :], (128, 1024))], axis=1)
    lam = np.concatenate([f(inp["lam_q1"])[0], f(inp["lam_k1"])[0], f(inp["lam_q2"])[0],
                          f(inp["lam_k2"])[0]])[None, :]
    shared = dict(
        ada_w=f(inp["ada_w"]), ada_b=f(inp["ada_b"]), ngT=np.ascontiguousarray(ngT),
        win=f(inp["attn_in_w"])[0], wout=f(inp["attn_out_w"])[0], fin=f(inp["fourier_in_w"])[0],
        fout=f(inp["fourier_out_w"])[0], gbc=np.ascontiguousarray(gbc), lam=np.ascontiguousarray(lam),
        sgT=np.ascontiguousarray(f(inp["attn_subln_g"])[0][:, None]),
        cf32=C["cf32"], cbf=C["cbf"], rope=C["rope"], f1=C["f1"], cs=C["cs"], tt=C["tt"])
    maps = []
    for b in range(N_CORES):
        cT = np.concatenate([c[b].reshape(8, 128).T, c_ctx.reshape(8, 128).T], axis=1)
        m = dict(shared)
        m.update(x=x[b], ctx=ctx[b], cT=np.ascontiguousarray(cT))
        maps.append(m)
    return maps


_PROG = {}


def kernel(**inputs):
    if "full" not in _PROG:
        _PROG["full"] = build_program("full")[0]
    nc = _PROG["full"]
    res = run_bass_kernel_spmd(nc, _in_maps(inputs), core_ids=list(range(N_CORES)))
    out = np.stack([np.asarray(r["out"], dtype=np.float32) for r in res.results], axis=0)
    return out
```
